# Optimizing a Trainium2 kernel written in Bass

```python
import math
import jax, jax.numpy as jnp
from jax import lax
import numpy as np

D_MODEL = 1024
BATCH = 8
SEQ = 4096
DEPTH = 1
DEC_BATCH = 128
DEC_SEQ = 1
PAST_LEN = 8192
PAGE_SIZE = 128

MIX_WIDTH = 2 * D_MODEL
SSD_WIDTH = MIX_WIDTH // 2
SSD_HEAD_DIM = 64
SSD_HEADS = SSD_WIDTH // SSD_HEAD_DIM
SSD_GROUPS = 4
SSD_HEADS_PER_GROUP = SSD_HEADS // SSD_GROUPS
SSD_STATE = 128
SSD_CHUNK = 128
CONV_W = 4
CONV_DIM = SSD_WIDTH + 2 * SSD_GROUPS * SSD_STATE
ATT_WIDTH = MIX_WIDTH // 4
ATT_HEAD_DIM = 64
ATT_HEADS = ATT_WIDTH // ATT_HEAD_DIM
DILATED_PAIRS = ((128, 1), (512, 4), (2048, 16))
MAX_WINDOW = 2048
ATT_BLOCK = 128
ROPE_THETA = 500000.0
ROPE_DIM = ATT_HEAD_DIM // 4
MEM_LEN = 256
CROSS_WIDTH = MIX_WIDTH // 4
CROSS_HEADS = 4
CROSS_HEAD_DIM = CROSS_WIDTH // CROSS_HEADS
NORM_EPS = 1e-6
N_IN = 2 * SSD_WIDTH + 2 * SSD_GROUPS * SSD_STATE + SSD_HEADS + 4 * ATT_WIDTH + 2 * CROSS_WIDTH

kernel_name = 'hymba_ssd_dilated_swa_memxattn_step'

F32 = jnp.float32


def rms_norm(x, g):
    x32 = x.astype(F32)
    y = x32 * lax.rsqrt(jnp.mean(x32 * x32, axis=-1, keepdims=True) + NORM_EPS)
    return (y * g.astype(F32)).astype(x.dtype)


def split_projection(u):
    sizes = (SSD_WIDTH, CONV_DIM, SSD_HEADS, ATT_WIDTH, ATT_WIDTH, ATT_WIDTH, ATT_WIDTH, CROSS_WIDTH, CROSS_WIDTH)
    idx = np.cumsum(sizes)[:-1].tolist()
    return jnp.split(u, idx, axis=-1)


def heads(a, n):
    return a.reshape(a.shape[:-1] + (n, a.shape[-1] // n))


def rotary(x, pos):
    half = ROPE_DIM // 2
    inv = ROPE_THETA ** (-jnp.arange(half, dtype=F32) / half)
    ang = pos.astype(F32)[..., None] * inv
    cos = jnp.cos(ang)[..., None, :]
    sin = jnp.sin(ang)[..., None, :]
    xr = x[..., :ROPE_DIM].astype(F32)
    x1, x2 = xr[..., :half], xr[..., half:]
    rot = jnp.concatenate([x1 * cos - x2 * sin, x2 * cos + x1 * sin], axis=-1)
    return jnp.concatenate([rot.astype(x.dtype), x[..., ROPE_DIM:]], axis=-1)


def causal_conv(xbc, buf, w, b):
    t = xbc.shape[1]
    xp = jnp.concatenate([buf.astype(xbc.dtype), xbc], axis=1)
    xp32 = xp.astype(F32)
    w32 = w.astype(F32)
    y = b.astype(F32)
    for tap in range(CONV_W):
        y = y + xp32[:, tap:tap + t] * w32[tap]
    return jax.nn.silu(y), xp[:, xp.shape[1] - (CONV_W - 1):]


def ssd_split(xbc_c):
    bsz, t = xbc_c.shape[:2]
    gn = SSD_GROUPS * SSD_STATE
    xs = xbc_c[..., :SSD_WIDTH].reshape(bsz, t, SSD_HEADS, SSD_HEAD_DIM)
    bm = xbc_c[..., SSD_WIDTH:SSD_WIDTH + gn].reshape(bsz, t, SSD_GROUPS, SSD_STATE)
    cm = xbc_c[..., SSD_WIDTH + gn:].reshape(bsz, t, SSD_GROUPS, SSD_STATE)
    return xs, bm, cm


def segsum(a):
    n = a.shape[-1]
    cs = jnp.cumsum(a, axis=-1)
    diff = cs[..., :, None] - cs[..., None, :]
    mask = jnp.tril(jnp.ones((n, n), dtype=bool))
    return jnp.where(mask, diff, -jnp.inf)


def ssd_chunked(x, dt, a, bm, cm, d_skip):
    bsz, s = x.shape[:2]
    nc, L = s // SSD_CHUNK, SSD_CHUNK
    G, R, P, N = SSD_GROUPS, SSD_HEADS_PER_GROUP, SSD_HEAD_DIM, SSD_STATE
    xc = (x * dt[..., None]).reshape(bsz, nc, L, G, R, P)
    adt = (dt * a).reshape(bsz, nc, L, G, R).transpose(0, 3, 4, 1, 2)
    bc = bm.reshape(bsz, nc, L, G, N)
    cc = cm.reshape(bsz, nc, L, G, N)
    a_cs = jnp.cumsum(adt, axis=-1)
    decay = jnp.exp(segsum(adt))
    cb = jnp.einsum('bclgn,bcsgn->bcgls', cc, bc)
    y_diag = jnp.einsum('bcgls,bgrcls,bcsgrp->bclgrp', cb, decay, xc)
    decay_states = jnp.exp(a_cs[..., -1:] - a_cs)
    states = jnp.einsum('bclgn,bgrcl,bclgrp->bcgrpn', bc, decay_states, xc)
    init = jnp.zeros((bsz, 1, G, R, P, N), F32)
    states = jnp.concatenate([init, states], axis=1)
    chunk_decay = jnp.exp(segsum(jnp.pad(a_cs[..., -1], ((0, 0), (0, 0), (0, 0), (1, 0)))))
    new_states = jnp.einsum('bgrzc,bcgrpn->bzgrpn', chunk_decay, states)
    states, final = new_states[:, :-1], new_states[:, -1]
    y_off = jnp.einsum('bclgn,bcgrpn,bgrcl->bclgrp', cc, states, jnp.exp(a_cs))
    y = (y_diag + y_off).reshape(bsz, s, SSD_HEADS, P) + x * d_skip[:, None]
    return y, final.reshape(bsz, SSD_HEADS, P, N)


def ssd_recurrent(x, dt, a, bm, cm, d_skip, h0):
    bsz, t = x.shape[:2]
    G, R, P, N = SSD_GROUPS, SSD_HEADS_PER_GROUP, SSD_HEAD_DIM, SSD_STATE
    xg = x.reshape(bsz, t, G, R, P)
    dtg = dt.reshape(bsz, t, G, R)
    ag = a.reshape(G, R)

    def step(h, inp):
        x_t, dt_t, b_t, c_t = inp
        h = h * jnp.exp(dt_t * ag)[..., None, None] + jnp.einsum('bgrp,bgn->bgrpn', x_t * dt_t[..., None], b_t)
        return h, jnp.einsum('bgrpn,bgn->bgrp', h, c_t)

    h, ys = lax.scan(step, h0.astype(F32).reshape(bsz, G, R, P, N),
                     (xg.swapaxes(0, 1), dtg.swapaxes(0, 1), bm.swapaxes(0, 1), cm.swapaxes(0, 1)))
    y = ys.swapaxes(0, 1).reshape(bsz, t, SSD_HEADS, P) + x * d_skip[:, None]
    return y, h.reshape(bsz, SSD_HEADS, P, N)


def combine_dilations(parts):
    m_all = parts[0][0]
    for m, _, _ in parts[1:]:
        m_all = jnp.maximum(m_all, m)
    num = 0.0
    den = 0.0
    for m, l, acc in parts:
        wgt = jnp.exp(m - m_all)
        num = num + wgt[..., None] * acc
        den = den + wgt * l
    return num / den[..., None]


def dilated_prompt(q, k, v):
    bsz, s, nh, e = q.shape
    scale = e ** -0.5
    parts = []
    for window, dil in DILATED_PAIRS:
        nw = window // dil
        L = s // dil
        nb = -(-L // ATT_BLOCK)
        lp = nb * ATT_BLOCK

        def sub(arr):
            arr = arr.reshape(bsz, L, dil, nh, e)
            return jnp.pad(arr, ((0, 0), (0, lp - L), (0, 0), (0, 0), (0, 0)))

        def kblocks(arr):
            arr = jnp.pad(sub(arr), ((0, 0), (ATT_BLOCK, 0), (0, 0), (0, 0), (0, 0)))
            arr = arr.reshape(bsz, nb + 1, ATT_BLOCK, dil, nh, e)
            return jnp.concatenate([arr[:, :-1], arr[:, 1:]], axis=2)

        qb = sub(q).reshape(bsz, nb, ATT_BLOCK, dil, nh, e)
        kb, vb = kblocks(k), kblocks(v)
        sc = jnp.einsum('bnqrhe,bnkrhe->bnrhqk', qb, kb) * scale
        qi = jnp.arange(ATT_BLOCK)[:, None]
        ki = jnp.arange(2 * ATT_BLOCK)[None, :]
        off = qi + ATT_BLOCK - ki
        key_sub = jnp.arange(nb)[:, None, None] * ATT_BLOCK - ATT_BLOCK + ki[None]
        valid = ((off >= 0) & (off <= nw))[None] & (key_sub >= 0)
        sc = jnp.where(valid[None, :, None, None], sc, -jnp.inf)
        m = jnp.max(sc, axis=-1)
        p = jnp.exp(sc - m[..., None])
        l = jnp.sum(p, axis=-1)
        acc = jnp.einsum('bnrhqk,bnkrhe->bnqrhe', p, vb)
        acc = acc.reshape(bsz, lp, dil, nh, e)[:, :L].reshape(bsz, s, nh, e)

        def back(arr):
            return arr.transpose(0, 1, 4, 2, 3).reshape(bsz, lp, dil, nh)[:, :L].reshape(bsz, s, nh)

        parts.append((back(m), back(l), acc))
    return combine_dilations(parts)


def dilated_sample(q, k_all, v_all, n_past):
    t = q.shape[1]
    scale = q.shape[-1] ** -0.5
    parts = []
    for window, dil in DILATED_PAIRS:
        nw = window // dil
        idx = n_past + jnp.arange(t)[:, None] - dil * jnp.arange(nw + 1)[None, :]
        valid = idx >= 0
        idx = jnp.maximum(idx, 0)
        kg = k_all[:, idx]
        vg = v_all[:, idx]
        sc = jnp.einsum('bthe,btjhe->bthj', q, kg) * scale
        sc = jnp.where(valid[None, :, None, :], sc, -jnp.inf)
        m = jnp.max(sc, axis=-1)
        p = jnp.exp(sc - m[..., None])
        l = jnp.sum(p, axis=-1)
        acc = jnp.einsum('bthj,btjhe->bthe', p, vg)
        parts.append((m, l, acc))
    return combine_dilations(parts)


def memory_kv(mem, mem_norm_g, w_mem_kv):
    m = rms_norm(mem, mem_norm_g)
    kv = jnp.einsum('bmd,dn->bmn', m, w_mem_kv)
    k, v = jnp.split(kv, 2, axis=-1)
    return heads(k, CROSS_HEADS), heads(v, CROSS_HEADS)


def cross_attend(q, k, v):
    scale = q.shape[-1] ** -0.5
    sc = jnp.einsum('bthe,bmhe->bhtm', q.astype(F32), k.astype(F32)) * scale
    p = jax.nn.softmax(sc, axis=-1)
    return jnp.einsum('bhtm,bmhe->bthe', p, v.astype(F32))


def mix_output(y_ssd, z, o_att, g_att, o_cross, g_cross, ssd_norm_g, w_out, dtype):
    bsz, t = z.shape[:2]
    y = y_ssd.reshape(bsz, t, SSD_WIDTH) * jax.nn.silu(z.astype(F32))
    yg = y.reshape(bsz, t, SSD_GROUPS, SSD_WIDTH // SSD_GROUPS)
    yg = yg * lax.rsqrt(jnp.mean(yg * yg, axis=-1, keepdims=True) + NORM_EPS)
    y = yg.reshape(bsz, t, SSD_WIDTH) * ssd_norm_g.astype(F32)
    att = o_att.reshape(bsz, t, ATT_WIDTH) * jax.nn.silu(g_att.astype(F32))
    crs = o_cross.reshape(bsz, t, CROSS_WIDTH) * jax.nn.silu(g_cross.astype(F32))
    cat = jnp.concatenate([y, att, crs], axis=-1).astype(dtype)
    return jnp.einsum('btn,nd->btd', cat, w_out)


def ssd_params(dt_raw, dt_bias, a_log):
    dt = jax.nn.softplus(dt_raw.astype(F32) + dt_bias.astype(F32))
    a = -jnp.exp(a_log.astype(F32))
    return dt, a


def layer_prompt(x, mem, ln_g, w_in, conv_w, conv_b, dt_bias, a_log, d_skip, ssd_norm_g, mem_norm_g, w_mem_kv, w_out):
    bsz, s, _ = x.shape
    h = rms_norm(x, ln_g)
    z, xbc, dt_raw, q_a, k_a, v_a, g_a, q_c, g_c = split_projection(jnp.einsum('btd,dn->btn', h, w_in))
    prefix = jnp.zeros((bsz, CONV_W - 1, CONV_DIM), xbc.dtype)
    xbc_c, conv_state = causal_conv(xbc, prefix, conv_w, conv_b)
    xs, bm, cm = ssd_split(xbc_c)
    dt, a = ssd_params(dt_raw, dt_bias, a_log)
    y_ssd, ssm_state = ssd_chunked(xs, dt, a, bm, cm, d_skip.astype(F32))
    pos = jnp.arange(s, dtype=jnp.int32)
    q = rotary(heads(q_a, ATT_HEADS), pos)
    k = rotary(heads(k_a, ATT_HEADS), pos)
    v = heads(v_a, ATT_HEADS)
    o_att = dilated_prompt(q.astype(F32), k.astype(F32), v.astype(F32))
    win = min(MAX_WINDOW, s)
    mk, mv = memory_kv(mem, mem_norm_g, w_mem_kv)
    o_cross = cross_attend(heads(q_c, CROSS_HEADS), mk, mv)
    out = mix_output(y_ssd, z, o_att, g_a, o_cross, g_c, ssd_norm_g, w_out, x.dtype)
    return x + out, (k[:, s - win:], v[:, s - win:], mk, mv, conv_state, ssm_state)


def layer_sample(x, pos, win_k, win_v, mem_k, mem_v, conv_buf, ssm_h,
                 ln_g, w_in, conv_w, conv_b, dt_bias, a_log, d_skip, ssd_norm_g, w_out):
    h = rms_norm(x, ln_g)
    z, xbc, dt_raw, q_a, k_a, v_a, g_a, q_c, g_c = split_projection(jnp.einsum('btd,dn->btn', h, w_in))
    xbc_c, conv_new = causal_conv(xbc, conv_buf, conv_w, conv_b)
    xs, bm, cm = ssd_split(xbc_c)
    dt, a = ssd_params(dt_raw, dt_bias, a_log)
    y_ssd, ssm_new = ssd_recurrent(xs, dt, a, bm, cm, d_skip.astype(F32), ssm_h)
    q = rotary(heads(q_a, ATT_HEADS), pos)
    k = rotary(heads(k_a, ATT_HEADS), pos)
    v = heads(v_a, ATT_HEADS)
    k_all = jnp.concatenate([win_k.astype(F32), k.astype(F32)], axis=1)
    v_all = jnp.concatenate([win_v.astype(F32), v.astype(F32)], axis=1)
    o_att = dilated_sample(q.astype(F32), k_all, v_all, win_k.shape[1])
    o_cross = cross_attend(heads(q_c, CROSS_HEADS), mem_k, mem_v)
    out = mix_output(y_ssd, z, o_att, g_a, o_cross, g_c, ssd_norm_g, w_out, x.dtype)
    return x + out, (k, v, conv_new, ssm_new)


def setup_inputs(seed: int = 0) -> dict:
    key = jax.random.key(seed)
    ks = jax.random.split(key, 24)
    nrm = jax.random.normal
    w_buf = min(MAX_WINDOW, PAST_LEN)
    dt0 = jnp.exp(jax.random.uniform(ks[13], (DEPTH, SSD_HEADS)) * (math.log(0.1) - math.log(0.001)) + math.log(0.001))
    return {
        'x_prompt': nrm(ks[0], (BATCH, SEQ, D_MODEL), F32),
        'x_sample': nrm(ks[1], (DEC_BATCH, DEC_SEQ, D_MODEL), F32),
        'mem_prompt': nrm(ks[2], (BATCH, MEM_LEN, D_MODEL), F32),
        'cache_win_k': nrm(ks[3], (DEPTH, DEC_BATCH, w_buf, ATT_HEADS, ATT_HEAD_DIM), F32),
        'cache_win_v': nrm(ks[4], (DEPTH, DEC_BATCH, w_buf, ATT_HEADS, ATT_HEAD_DIM), F32),
        'cache_mem_k': nrm(ks[5], (DEPTH, DEC_BATCH, MEM_LEN, CROSS_HEADS, CROSS_HEAD_DIM), F32),
        'cache_mem_v': nrm(ks[6], (DEPTH, DEC_BATCH, MEM_LEN, CROSS_HEADS, CROSS_HEAD_DIM), F32),
        'state_conv': nrm(ks[7], (DEPTH, DEC_BATCH, CONV_W - 1, CONV_DIM), F32),
        'state_ssm': 0.5 * nrm(ks[8], (DEPTH, DEC_BATCH, SSD_HEADS, SSD_HEAD_DIM, SSD_STATE), F32),
        'pos_sample': jnp.broadcast_to(PAST_LEN + jnp.arange(DEC_SEQ, dtype=jnp.int32), (DEC_BATCH, DEC_SEQ)).astype(jnp.int32),
        'ln_g': 1.0 + 0.02 * nrm(ks[9], (DEPTH, D_MODEL), F32),
        'w_in': nrm(ks[10], (DEPTH, D_MODEL, N_IN), F32) * D_MODEL ** -0.5,
        'conv_w': nrm(ks[11], (DEPTH, CONV_W, CONV_DIM), F32) * CONV_W ** -0.5,
        'conv_b': 0.02 * nrm(ks[12], (DEPTH, CONV_DIM), F32),
        'dt_bias': dt0 + jnp.log(-jnp.expm1(-dt0)),
        'a_log': jnp.log(jax.random.uniform(ks[14], (DEPTH, SSD_HEADS), F32, 1.0, 16.0)),
        'd_skip': 1.0 + 0.1 * nrm(ks[15], (DEPTH, SSD_HEADS), F32),
        'ssd_norm_g': 1.0 + 0.02 * nrm(ks[16], (DEPTH, SSD_WIDTH), F32),
        'mem_norm_g': 1.0 + 0.02 * nrm(ks[17], (DEPTH, D_MODEL), F32),
        'w_mem_kv': nrm(ks[18], (DEPTH, D_MODEL, 2 * CROSS_WIDTH), F32) * D_MODEL ** -0.5,
        'w_out': nrm(ks[19], (DEPTH, MIX_WIDTH, D_MODEL), F32) * MIX_WIDTH ** -0.5,
        'final_norm_g': 1.0 + 0.02 * nrm(ks[20], (D_MODEL,), F32),
    }


def reference(x_prompt, x_sample, mem_prompt, cache_win_k, cache_win_v, cache_mem_k, cache_mem_v,
              state_conv, state_ssm, pos_sample, ln_g, w_in, conv_w, conv_b, dt_bias, a_log, d_skip,
              ssd_norm_g, mem_norm_g, w_mem_kv, w_out, final_norm_g):
    hp, hs = x_prompt, x_sample
    new_p, new_s = [], []
    for i in range(DEPTH):
        hp, sp = layer_prompt(hp, mem_prompt, ln_g[i], w_in[i], conv_w[i], conv_b[i], dt_bias[i], a_log[i],
                              d_skip[i], ssd_norm_g[i], mem_norm_g[i], w_mem_kv[i], w_out[i])
        hs, ss = layer_sample(hs, pos_sample, cache_win_k[i], cache_win_v[i], cache_mem_k[i], cache_mem_v[i],
                              state_conv[i], state_ssm[i], ln_g[i], w_in[i], conv_w[i], conv_b[i], dt_bias[i],
                              a_log[i], d_skip[i], ssd_norm_g[i], w_out[i])
        new_p.append(sp)
        new_s.append(ss)
    y_prompt = rms_norm(hp, final_norm_g)
    y_sample = rms_norm(hs, final_norm_g)
    win_k_prompt, win_v_prompt, mem_k_prompt, mem_v_prompt, conv_prompt, ssm_prompt = [jnp.stack(a) for a in zip(*new_p)]
    win_k_sample, win_v_sample, conv_sample, ssm_sample = [jnp.stack(a) for a in zip(*new_s)]
    return (y_prompt, y_sample, win_k_prompt, win_v_prompt, mem_k_prompt, mem_v_prompt, conv_prompt, ssm_prompt,
            win_k_sample, win_v_sample, conv_sample, ssm_sample)
```

```python
import math
from contextlib import ExitStack
import numpy as np
import concourse.bass as bass
import concourse.mybir as mybir
from concourse.bass_utils import run_bass_kernel_spmd

F32 = mybir.dt.float32
BF16 = mybir.dt.bfloat16
I32 = mybir.dt.int32
ALU = mybir.AluOpType
AF = mybir.ActivationFunctionType
AX = mybir.AxisListType

COMPUTE = ("pe", "act", "dve", "pool")
ENGINES = ("pe", "act", "dve", "pool", "sp")

D = 1024
SEQ = 4096
NT = SEQ // 128
NIN = 6160
NB = 16
EPS = 1e-6
DILS = (1, 4, 16)


class Sched:
    def __init__(self, nc):
        self.nc = nc
        self.ops = {e: [] for e in ENGINES}
        self.cnt = {e: 0 for e in COMPUTE}
        self.last_write = {}
        self.readers = {}
        self.dma_cnt = {}
        self.waited = {e: {} for e in ENGINES}

    def _tok_val(self, tok):
        kind, name, val = tok
        if kind == "dma":
            return ("dma:" + name, 16 * self.dma_cnt[name])
        return (name, val)

    def op(self, eng, fn, reads=(), writes=(), dma=None):
        waits = {}

        def need(tok, raw):
            if tok is None:
                return
            kind, name, val = tok
            if kind == "eng" and name == eng and (eng == "pe" or not raw):
                return
            s, v = self._tok_val(tok)
            if waits.get(s, 0) < v:
                waits[s] = v

        for r in reads:
            need(self.last_write.get(r), True)
        for w in writes:
            need(self.last_write.get(w), True)
            for t in self.readers.get(w, ()):
                need(t, False)
        wl = []
        for s, v in waits.items():
            if self.waited[eng].get(s, 0) < v:
                self.waited[eng][s] = v
                wl.append((s, v))
        if dma is not None:
            self.dma_cnt[dma] = self.dma_cnt.get(dma, 0) + 1
            tok = ("dma", dma, None)
            inc = ("dma:" + dma, 16)
        else:
            self.cnt[eng] += 1
            tok = ("eng", eng, self.cnt[eng])
            inc = (eng, 1)
        self.ops[eng].append((wl, fn, inc))
        for r in reads:
            self.readers.setdefault(r, []).append(tok)
        for w in writes:
            self.last_write[w] = tok
            self.readers[w] = []
        return tok

    def barrier_all(self, eng="sp"):
        wl = []
        for e in COMPUTE:
            if e != eng and self.cnt[e] > 0 and self.waited[eng].get(e, 0) < self.cnt[e]:
                self.waited[eng][e] = self.cnt[e]
                wl.append((e, self.cnt[e]))
        for k, c in self.dma_cnt.items():
            s = "dma:" + k
            if self.waited[eng].get(s, 0) < 16 * c:
                self.waited[eng][s] = 16 * c
                wl.append((s, 16 * c))
        self.ops[eng].append((wl, None, None))

    def full_barrier(self):
        for e in ENGINES:
            self.barrier_all(e)

    def emit(self):
        nc = self.nc
        names = list(COMPUTE) + ["dma:" + k for k in self.dma_cnt]
        with ExitStack() as es:
            sems = {n: es.enter_context(nc.semaphore(("s_" + n).replace(":", "_")))
                    for n in names}
            block = es.enter_context(nc.Block())

            def run(engname):
                def body(eng):
                    for wl, fn, inc in self.ops[engname]:
                        for s, v in wl:
                            eng.wait_ge(sems[s], v)
                        if fn is not None:
                            ins = fn(eng)
                            ins.then_inc(sems[inc[0]], inc[1])
                return body

            block.tensor(run("pe"))
            block.scalar(run("act"))
            block.vector(run("dve"))
            block.gpsimd(run("pool"))
            block.sync(run("sp"))


import os
UPTO = os.environ.get('K_UPTO', 'S')
BSTOP = int(os.environ.get('K_BSTOP', '9'))


def build_program():
    nc = bass.Bass("TRN2", target_bir_lowering=False)
    S = Sched(nc)

    def din(name, shape, dt=F32):
        return nc.dram_tensor(name, list(shape), dt, kind="ExternalInput").ap()

    def dout(name, shape, dt=F32):
        return nc.dram_tensor(name, list(shape), dt, kind="ExternalOutput").ap()

    def dscr(name, shape, dt):
        return nc.dram_tensor(name, list(shape), dt).ap()

    x = din("x", [SEQ, D])
    xs = din("xs", [NB, D])
    mem = din("mem", [256, D])
    cwk = din("cwk", [NB, 2048, 512])
    cwv = din("cwv", [NB, 2048, 512])
    cmk = din("cmk", [NB, 256, 512])
    cmv = din("cmv", [NB, 256, 512])
    sconv = din("sconv", [NB, 3, 2048])
    sssm = din("sssm", [NB * 16 * 64, 128])
    pos = din("pos", [NB, 1], I32)
    ln_g = din("ln_g", [1, D])
    w_in = din("w_in", [D, NIN])
    conv_w = din("conv_w", [4, 2048])
    conv_b = din("conv_b", [1, 2048])
    dt_bias = din("dt_bias", [1, 16])
    a_log = din("a_log", [1, 16])
    d_skip = din("d_skip", [1, 16])
    ssd_norm_g = din("ssd_norm_g", [1, D])
    mem_norm_g = din("mem_norm_g", [1, D])
    w_mem_kv = din("w_mem_kv", [D, D])
    w_out = din("w_out", [2048, D])
    final_norm_g = din("final_norm_g", [1, D])
    c_ident = din("c_ident", [128, 128])
    c_tri = din("c_tri", [128, 128])
    c_triT = din("c_triT", [128, 128])
    c_posf = din("c_posf", [128, NT])
    c_inv = din("c_inv", [1, 8])
    c_esel = din("c_esel", [1, 256])

    yp = dout("yp", [SEQ, D])
    ys = dout("ys", [NB, D])
    wk = dout("wk", [2048, 512])
    wv = dout("wv", [2048, 512])
    mko = dout("mko", [256, 512])
    mvo = dout("mvo", [256, 512])
    convp = dout("convp", [3, 2048])
    ssmp = dout("ssmp", [1024, 128])
    wks = dout("wks", [NB, 512])
    wvs = dout("wvs", [NB, 512])
    convs = dout("convs", [NB, 3, 2048])
    ssms = dout("ssms", [NB * 16 * 64, 128])

    cat_scr = dscr("cat_scr", [NT, 128, 12, 128], BF16)
    qkT_scr = dscr("qkT_scr", [128, 8, SEQ], BF16)
    v_scr = dscr("v_scr", [SEQ, 528], BF16)
    sg_scr = dscr("sg_scr", [SEQ, 512], BF16)
    o_scr = dscr("o_scr", [3, SEQ, 520], F32)
    s_scr = dscr("s_scr", [NB, 4096], F32)
    us_scr = dscr("us_scr", [NB, NIN], F32)
    cat_s_scr = dscr("cat_s_scr", [NB, 2048], F32)
    sx_scr = dscr("sx_scr", [NB, 1024], F32)
    sB_scr = dscr("sB_scr", [NB, 1024], F32)
    sC_scr = dscr("sC_scr", [NB, 1024], F32)
    sdA_scr = dscr("sdA_scr", [NB, 16], F32)
    sq_scr = dscr("sq_scr", [NB, 512], F32)
    sk_scr = dscr("sk_scr", [NB, 512], F32)
    sv_scr = dscr("sv_scr", [NB, 512], F32)

    with ExitStack() as es:
        def sb(name, shape, dt=F32):
            return es.enter_context(nc.sbuf_tensor(name, list(shape), dt))

        def ps(name, shape, dt=F32):
            return es.enter_context(nc.psum_tensor(name, list(shape), dt))

        ps_tr = ps("ps_tr", [128, 1024], BF16)
        ps_cb = ps("ps_cb", [128, 512], F32)
        ps_in = ps("ps_in", [128, 1024], F32)
        ps_4 = ps("ps_4", [128, 2048], F32)
        ps_big = ps_4[:, 0:1024]
        ps_y = ps_4[:, 1024:2048]

        arena1 = sb("arena1", [128, 50688], BF16)
        w_in_bf = arena1[:, 0:8 * NIN].rearrange("p (k n) -> p k n", n=NIN)
        V3 = arena1[:, 0:3 * 32 * 528].rearrange("p (d t c) -> p d t c", d=3, t=32)
        w_out_bf = arena1[:, 0:16 * 1024].rearrange("p (k n) -> p k n", n=1024)

        identf = sb("identf", [128, 128]); identb = sb("identb", [128, 128], BF16)
        tri = sb("tri", [128, 128]); ustr = sb("ustr", [128, 128]); onesf = sb("onesf", [128, 128])
        onesb = sb("onesb", [128, 128], BF16)
        mask2 = sb("mask2", [128, 2, 128], BF16)
        tmpc = sb("tmpc", [128, 128])
        ln_bc = sb("ln_bc", [128, D]); sng_bc = sb("sng_bc", [128, D])
        dtb_bc = sb("dtb_bc", [128, 16]); a_bc = sb("a_bc", [128, 16]); dsk_bc = sb("dsk_bc", [128, 16])
        cwb = sb("cwb", [128, 80])
        posf = sb("posf", [128, 33]); posi = sb("posi", [128, 1], I32)
        inv_bc = sb("inv_bc", [128, 8])
        cosT = sb("cosT", [128, 33, 8]); sinT = sb("sinT", [128, 33, 8])
        ang = sb("ang", [128, 33, 8]); rr = sb("rr", [128, 33, 8]); yy = sb("yy", [128, 33, 8])
        MKT = sb("MKT", [128, 4, 256], BF16); MV = sb("MV", [128, 2, 512], BF16)
        stat = sb("stat", [128, 16])

        A2 = 22316
        arena2 = sb("arena2", [128, A2], F32)
        off = [0]

        def carve(n_f32):
            a = off[0]
            off[0] += n_f32
            assert off[0] <= A2, off[0]
            return arena2[:, a:a + n_f32]

        def carve_bf(n_bf16):
            n = (n_bf16 + 1) // 2
            return carve(n).bitcast(BF16)[:, 0:n_bf16]

        A = AF

        def dma(out, in_, reads, writes, key, eng="sp", **kw):
            q = "act" if eng == "act_store" else "sp"
            return S.op(q, lambda e: e.dma_start(out=out, in_=in_, **kw), reads=reads, writes=writes, dma=key)

        dma(identf[:], c_ident, [], ["identf"], "c0")
        dma(tri[:], c_tri, [], ["tri"], "c0")
        dma(tmpc[:], c_triT, [], ["tmpc"], "c0")
        dma(ln_bc[:], ln_g.partition_broadcast(128), [], ["ln_bc"], "c0")
        dma(sng_bc[:], ssd_norm_g.partition_broadcast(128), [], ["sng_bc"], "c0")
        dma(dtb_bc[:], dt_bias.partition_broadcast(128), [], ["dtb_bc"], "c0")
        dma(a_bc[:], a_log.partition_broadcast(128), [], ["a_bc"], "c0")
        dma(dsk_bc[:], d_skip.partition_broadcast(128), [], ["dsk_bc"], "c0")
        dma(inv_bc[:], c_inv.partition_broadcast(128), [], ["inv_bc"], "c0")
        dma(posf[:, 0:NT], c_posf, [], ["posf"], "c0")
        S.op("pool", lambda e: e.memset(posi[:], 0), writes=["posi"])
        dma(posi[0:NB, :], pos, ["posi"], ["posi"], "c0")
        S.op("dve", lambda e: e.tensor_copy(out=posf[:, NT:NT + 1], in_=posi[:]), reads=["posi", "posf"], writes=["posf"])
        S.op("dve", lambda e: e.tensor_copy(out=identb[:], in_=identf[:]), reads=["identf"], writes=["identb"])
        S.op("dve", lambda e: e.tensor_scalar(out=ustr[:], in0=tri[:], scalar1=-1.0, scalar2=1.0, op0=ALU.mult, op1=ALU.add), reads=["tri"], writes=["ustr"])
        S.op("pool", lambda e: e.memset(onesf[:], 1.0), writes=["onesf"])
        S.op("pool", lambda e: e.memset(onesb[:], 1.0), writes=["onesb"])
        S.op("dve", lambda e: e.tensor_copy(out=mask2[:, 0, :], in_=tmpc[:]), reads=["tmpc"], writes=["mask2a"])
        S.op("dve", lambda e: e.tensor_copy(out=mask2[:, 1, :], in_=tri[:]), reads=["tri"], writes=["mask2b"])
        S.op("act", lambda e: e.activation(out=a_bc[:], in_=a_bc[:], func=A.Exp), reads=["a_bc"], writes=["a_bc"])
        S.op("dve", lambda e: e.tensor_scalar(out=a_bc[:], in0=a_bc[:], scalar1=-1.0, scalar2=None, op0=ALU.mult), reads=["a_bc"], writes=["a_bc"])

        MAGIC = 12582912.0
        C1 = 6.28125
        C2 = 2 * math.pi - C1
        S.op("dve", lambda e: e.tensor_tensor(out=ang[:], in0=posf[:].unsqueeze(2).broadcast_to([128, 33, 8]),
                                              in1=inv_bc[:].unsqueeze(1).broadcast_to([128, 33, 8]), op=ALU.mult),
             reads=["posf", "inv_bc"], writes=["ang"])
        for (dst, shift, name) in ((sinT, 0.0, "sinT"), (cosT, 0.25, "cosT")):
            S.op("dve", lambda e, shift=shift: e.tensor_scalar(out=rr[:], in0=ang[:], scalar1=1.0 / (2 * math.pi), scalar2=shift, op0=ALU.mult, op1=ALU.add), reads=["ang"], writes=["rr"])
            S.op("dve", lambda e: e.tensor_scalar(out=rr[:], in0=rr[:], scalar1=MAGIC, scalar2=None, op0=ALU.add), reads=["rr"], writes=["rr"])
            S.op("dve", lambda e: e.tensor_scalar(out=rr[:], in0=rr[:], scalar1=-MAGIC, scalar2=None, op0=ALU.add), reads=["rr"], writes=["rr"])
            S.op("dve", lambda e: e.scalar_tensor_tensor(out=yy[:], in0=rr[:], scalar=-C1, in1=ang[:], op0=ALU.mult, op1=ALU.add), reads=["rr", "ang"], writes=["yy"])
            S.op("dve", lambda e: e.scalar_tensor_tensor(out=yy[:], in0=rr[:], scalar=-C2, in1=yy[:], op0=ALU.mult, op1=ALU.add), reads=["rr", "yy"], writes=["yy"])
            if shift:
                S.op("dve", lambda e: e.tensor_scalar(out=yy[:], in0=yy[:], scalar1=math.pi / 2, scalar2=None, op0=ALU.add), reads=["yy"], writes=["yy"])
            S.op("dve", lambda e: e.tensor_scalar(out=yy[:], in0=yy[:], scalar1=3.141592, scalar2=-3.141592, op0=ALU.min, op1=ALU.max), reads=["yy"], writes=["yy"])
            S.op("act", lambda e, dst=dst: e.activation(out=dst[:], in_=yy[:], func=A.Sin), reads=["yy"], writes=[name])

        cstage = carve(128)
        dma(cstage[0:64, :], conv_w.rearrange("t (c p) -> (t c) p", p=128), [], ["cstage"], "c1")
        dma(cstage[64:80, :], conv_b.rearrange("o (c p) -> (o c) p", p=128), [], ["cstage"], "c1")
        S.op("pe", lambda e: e.matmul(ps_cb[:, 0:80], lhsT=cstage[0:80, :], rhs=identf[0:80, 0:80], start=True, stop=True), reads=["cstage", "identf"], writes=["ps_cb"])
        S.op("act", lambda e: e.copy(out=cwb[:], in_=ps_cb[:, 0:80]), reads=["ps_cb"], writes=["cwb"])

        stage = carve(3 * 3080).rearrange("p (s n) -> p s n", s=3)
        conv_engs = ("act", "pool", "dve")

        def cast(eng, out, in_, reads, writes):
            if eng == "act":
                S.op("act", lambda e: e.copy(out=out, in_=in_), reads=reads, writes=writes)
            else:
                S.op(eng, lambda e: e.tensor_copy(out=out, in_=in_), reads=reads, writes=writes)

        ci = 0
        for k in range(8):
            for hf in range(2):
                sidx = ci % 3
                c0 = hf * 3080
                dma(stage[:, sidx, :], w_in[k * 128:(k + 1) * 128, c0:c0 + 3080], [], ["stage%d" % sidx], "stg%d" % sidx)
                cast(("act", "dve", "act", "dve", "pool")[ci % 5], w_in_bf[:, k, c0:c0 + 3080], stage[:, sidx, :], ["stage%d" % sidx], ["w_in"])
                ci += 1

        ssq = stat[:, 0:1]; var = stat[:, 1:2]; rstd = stat[:, 2:3]
        junk0 = carve(1024)
        h_bf0 = carve_bf(1024)
        xt0 = [carve(1024), carve(1024)]

        def rms_to_hT(x_ap, xres, g_bc, gres, rows, junk, jres, h_bf, hT_dst, hres):
            S.op("act", lambda e: e.activation(out=junk[0:rows, :], in_=x_ap, func=A.Square, accum_out=ssq[0:rows, :]),
                 reads=[xres], writes=[jres, "ssq"])
            S.op("dve", lambda e: e.tensor_scalar(out=var[0:rows, :], in0=ssq[0:rows, :], scalar1=1.0 / D, scalar2=EPS, op0=ALU.mult, op1=ALU.add),
                 reads=["ssq"], writes=["var"])
            S.op("act", lambda e: e.activation(out=var[0:rows, :], in_=var[0:rows, :], func=A.Ln), reads=["var"], writes=["var"])
            S.op("act", lambda e: e.activation(out=rstd[0:rows, :], in_=var[0:rows, :], func=A.Exp, scale=-0.5), reads=["var"], writes=["rstd"])
            S.op("dve", lambda e: e.scalar_tensor_tensor(out=h_bf[0:rows, :], in0=x_ap, scalar=rstd[0:rows, :], in1=g_bc[0:rows, :], op0=ALU.mult, op1=ALU.mult),
                 reads=[xres, "rstd", gres], writes=["h_bf"])

            def tr(e):
                for k in range(8):
                    ins = e.transpose(out=ps_tr[:, k * 128:k * 128 + rows], in_=h_bf[0:rows, k * 128:(k + 1) * 128], identity=identb[0:rows, 0:rows])
                return ins
            S.op("pe", tr, reads=["h_bf", "identb"], writes=["ps_tr"])
            dst = hT_dst
            S.op("act", lambda e: e.copy(out=dst[:, :, 0:rows], in_=ps_tr[:].rearrange("p (k t) -> p k t", k=8)[:, :, 0:rows]),
                 reads=["ps_tr"], writes=[hres])

        mem_bc = carve(1024)
        dma(mem_bc[:], mem_norm_g.partition_broadcast(128), [], ["mem_bc"], "c1")
        wkv_bf = carve_bf(8 * 1024).rearrange("p (s n) -> p s n", s=8)
        mkv_f = carve(1024)
        mk_bf = carve_bf(512)
        for mt in range(2):
            dma(xt0[mt][:], mem[mt * 128:(mt + 1) * 128, :], [], ["xt%d" % mt], "x%d" % mt)
        hT2 = carve_bf(2 * 1024).rearrange("p (m k t) -> p m k t", m=2, k=8)
        for mt in range(2):
            rms_to_hT(xt0[mt][:], "xt%d" % mt, mem_bc, "mem_bc", 128, junk0, "junk0", h_bf0, hT2[:, mt], "hT2_%d" % mt)
        for k in range(8):
            sidx = k % 2
            dma(stage[:, sidx, 0:1024], w_mem_kv[k * 128:(k + 1) * 128, :], [], ["stage%d" % sidx], "stg%d" % sidx)
            cast(conv_engs[k % 3], wkv_bf[:, k, :], stage[:, sidx, 0:1024], ["stage%d" % sidx], ["wkv"])

        def mmkv(e):
            for mt in range(2):
                for hf in range(2):
                    for k in range(8):
                        ins = e.matmul(ps_4[:, mt * 1024 + hf * 512: mt * 1024 + (hf + 1) * 512], lhsT=hT2[:, mt, k, :],
                                       rhs=wkv_bf[:, k, hf * 512:(hf + 1) * 512], start=(k == 0), stop=(k == 7))
            return ins
        S.op("pe", mmkv, reads=["hT2_0", "hT2_1", "wkv"], writes=["ps_4"])
        for mt in range(2):
            S.op("act", lambda e, mt=mt: e.copy(out=mkv_f[:], in_=ps_4[:, mt * 1024:(mt + 1) * 1024]), reads=["ps_4"], writes=["mkv_f"])
            dma(mko[mt * 128:(mt + 1) * 128, :], mkv_f[:, 0:512], ["mkv_f"], ["mko"], "o_mk")
            dma(mvo[mt * 128:(mt + 1) * 128, :], mkv_f[:, 512:1024], ["mkv_f"], ["mvo"], "o_mk")
            S.op("dve", lambda e: e.tensor_copy(out=mk_bf[:], in_=mkv_f[:, 0:512]), reads=["mkv_f"], writes=["mk_bf"])
            S.op("dve", lambda e, mt=mt: e.tensor_copy(out=MV[:, mt, :], in_=mkv_f[:, 512:1024]), reads=["mkv_f"], writes=["MV"])

            def trk(e):
                for h in range(4):
                    ins = e.transpose(out=ps_tr[:, h * 128:(h + 1) * 128], in_=mk_bf[:, h * 128:(h + 1) * 128], identity=identb[:])
                return ins
            S.op("pe", trk, reads=["mk_bf", "identb"], writes=["ps_tr"])
            S.op("act", lambda e, mt=mt: e.copy(out=MKT[:, :, mt * 128:(mt + 1) * 128], in_=ps_tr[:, 0:512].rearrange("p (h m) -> p h m", h=4)),
                 reads=["ps_tr"], writes=["MKT"])

        off_A = off[0]
        off[0] = 0
        S.full_barrier()
        t1 = carve(1024); junk = t1
        h_bf = carve_bf(1024)
        hT = carve_bf(1024).rearrange("p (k t) -> p k t", k=8)
        xt1 = carve(1024)
        xt = [xt1, xt1]
        szs = [carve(1024), carve(1024)]
        qf = carve(512); kf = carve(512); vf = carve(512)
        rden = carve(512); ocr = rden
        sgc = carve(512)
        rt = carve(256).rearrange("p (a h i) -> p a h i", a=4, h=8)
        q_bf = carve_bf(512); k_bf = carve_bf(512); sg_bf = carve_bf(512)
        v_aug = carve_bf(528).rearrange("p (h c) -> p h c", c=66)
        dts = carve(128)
        dtbufs = [tuple(dts[:, p_ * 64 + i_ * 16:p_ * 64 + (i_ + 1) * 16] for i_ in range(4)) for p_ in range(2)]
        csb = carve(48); exs = carve(48)
        xbuf = carve(16 * 131).rearrange("p (c t) -> p c t", c=16)
        acc = carve(2048).rearrange("p (c t) -> p c t", c=16)
        acc2 = carve(2048).rearrange("p (c t) -> p c t", c=16)
        xc_bf = carve_bf(2048).rearrange("p (c t) -> p c t", c=16)
        xtok_bf = carve_bf(1024); xdt_bf = carve_bf(1024); xds_bf = xtok_bf
        yn_bf = xds_bf
        Btok_bf = carve_bf(512).rearrange("p (g n) -> p g n", g=4)
        CBm = carve_bf(512).rearrange("p (g l) -> p g l", g=4)
        rhsd = acc
        E_flat = carve_bf(2048)
        E_bf = E_flat.rearrange("p (h l) -> p h l", h=16)
        MT_bf = E_bf
        PT_bf = E_flat[:, 0:1024].rearrange("p (h m t) -> p h m t", h=4, m=2)
        state = carve(1024); state_bf = carve_bf(1024)
        yv = carve(1024)
        gss = stat[:, 4:8]; gvar = stat[:, 8:12]; grstd = stat[:, 12:16]
        catT_sb = carve_bf(1024).rearrange("p (c t) -> p c t", c=8)
        qkT_sb = catT_sb
        qcT = carve_bf(512).rearrange("p (h t) -> p h t", h=4)
        catTc = catT_sb[:, 0:4, :]
        print("arena2 phase A words:", off[0])

        S.op("pool", lambda e: e.memset(state[:], 0.0), writes=["state"])
        S.op("pool", lambda e: e.memset(state_bf[:], 0.0), writes=["state_bf"])
        S.op("pool", lambda e: e.memset(xbuf[:, :, 0:3], 0.0), writes=["xbuf"])
        S.op("pool", lambda e: e.memset(v_aug[:], 1.0), writes=["v_aug"])

        TOK_GROUPS = (("z0", 0), ("z1", 512), ("q", 3088), ("k", 3600), ("v", 4112), ("g", 4624))
        FM_GROUPS = (("xbc0", 1024), ("xbc1", 1536), ("xbc2", 2048), ("xbc3", 2560), ("qc", 5136), ("gc", 5648))
        bank_i = [0]

        def next_bank():
            b = bank_i[0] % 2
            bank_i[0] += 1
            return ps_in[:, b * 512:(b + 1) * 512], "ps_in%d" % b

        def load_x(t):
            b = t % 2
            if t < NT:
                dma(xt[b][:], x[t * 128:(t + 1) * 128, :], [], ["xt"], "x0")
            else:
                dma(xt[b][0:NB, :], xs, [], ["xt"], "x0")

        def tok_group(c0, ncol, rows=128):
            bank, bres = next_bank()

            def mm(e):
                for k in range(8):
                    ins = e.matmul(bank[0:rows, 0:ncol], lhsT=hT[:, k, 0:rows], rhs=w_in_bf[:, k, c0:c0 + ncol], start=(k == 0), stop=(k == 7))
                return ins
            S.op("pe", mm, reads=["hT", "w_in"], writes=[bres])
            return bank, bres

        def fm_group(c0):
            bank, bres = next_bank()

            def mm(e):
                for c in range(4):
                    for k in range(8):
                        ins = e.matmul(bank[:, c * 128:(c + 1) * 128], lhsT=w_in_bf[:, k, c0 + c * 128:c0 + (c + 1) * 128], rhs=hT[:, k, :], start=(k == 0), stop=(k == 7))
                return ins
            S.op("pe", mm, reads=["hT", "w_in"], writes=[bres])
            return bank, bres

        def b3(ap, h, n):
            return ap.unsqueeze(2).broadcast_to([128, h, n])

        def rotary(src, t, rows, rt):
            v3 = src.rearrange("p (h e) -> p h e", h=8)
            x1 = v3[0:rows, :, 0:8]; x2 = v3[0:rows, :, 8:16]
            cb = cosT[0:rows, t, :].unsqueeze(1).broadcast_to([rows, 8, 8])
            sn = sinT[0:rows, t, :].unsqueeze(1).broadcast_to([rows, 8, 8])

            def f(e):
                e.tensor_tensor(out=rt[0:rows, 0], in0=x1, in1=cb, op=ALU.mult)
                e.tensor_tensor(out=rt[0:rows, 1], in0=x2, in1=sn, op=ALU.mult)
                e.tensor_tensor(out=rt[0:rows, 2], in0=x2, in1=cb, op=ALU.mult)
                return e.tensor_tensor(out=rt[0:rows, 3], in0=x1, in1=sn, op=ALU.mult)
            return f, x1, x2

        def do_rotary(src, sres, t, rows, rt):
            f, x1, x2 = rotary(src, t, rows, rt)
            S.op("pool", f, reads=[sres, "cosT", "sinT"], writes=["rt"])
            S.op("pool", lambda e: e.tensor_tensor(out=x1, in0=rt[0:rows, 0], in1=rt[0:rows, 1], op=ALU.subtract), reads=["rt"], writes=[sres])
            S.op("pool", lambda e: e.tensor_tensor(out=x2, in0=rt[0:rows, 2], in1=rt[0:rows, 3], op=ALU.add), reads=["rt"], writes=[sres])

        def pre_a(t):
            b = t % 2
            sz = szs[b]; RSZ = "sz%d" % b
            dtx, dte, dtv, adt = dtbufs[b]
            RDX, RDE, RDV, RAD = ["%s%d" % (n_, b) for n_ in ("dtx", "dte", "dtv", "adt")]
            rms_to_hT(xt[b][:], "xt", ln_bc, "ln_bc", 128, h_bf, "h_bf", h_bf, hT, "hT")
            load_x(t + 1)
            yield
            for gi in range(4):
                bank, bres = fm_group(1024 + gi * 512)
                S.op("act", lambda e, bank=bank, gi=gi: e.copy(out=xbuf[:, gi * 4:(gi + 1) * 4, 3:131], in_=bank.rearrange("p (c t) -> p c t", c=4)),
                     reads=[bres], writes=["xbuf"])
                yield
            if t == NT - 1:
                for r_ in range(3):
                    dma(convp[r_:r_ + 1, :].rearrange("o (c p) -> p (o c)", p=128), xbuf[:, :, 128 + r_], ["xbuf"], ["convp"], "o_cv", allow_slow_non_contiguous=True)
            def wv_(tap):
                return cwb[:, tap * 16:(tap + 1) * 16].unsqueeze(2).broadcast_to([128, 16, 128])

            CSPLIT = 10

            def conv_range(eng, c0, c1, sfx):
                nch = c1 - c0
                a = acc[:, c0:c1, :]; a2 = acc2[:, c0:c1, :]
                ra = "acc" + sfx; ra2 = "acc2" + sfx

                def w_(tap):
                    return cwb[:, tap * 16 + c0:tap * 16 + c1].unsqueeze(2).broadcast_to([128, nch, 128])
                S.op(eng, lambda e: e.tensor_tensor(out=a, in0=xbuf[:, c0:c1, 0:128], in1=w_(0), op=ALU.mult), reads=["xbuf", "cwb"], writes=[ra])
                for tap in range(1, 4):
                    S.op(eng, lambda e, tap=tap: e.tensor_tensor(out=a2, in0=xbuf[:, c0:c1, tap:tap + 128], in1=w_(tap), op=ALU.mult), reads=["xbuf", "cwb", ra], writes=[ra2])
                    S.op(eng, lambda e: e.tensor_tensor(out=a, in0=a, in1=a2, op=ALU.add), reads=[ra, ra2], writes=[ra])
                S.op(eng, lambda e: e.tensor_tensor(out=a, in0=a, in1=cwb[:, 64 + c0:64 + c1].unsqueeze(2).broadcast_to([128, nch, 128]), op=ALU.add), reads=[ra, "cwb"], writes=[ra])
            conv_range("dve", 0, CSPLIT, "D")
            conv_range("pool", CSPLIT, 16, "P")

            yield
            def mmdt(e):
                for k in range(8):
                    ins = e.matmul(ps_cb[:, 0:16], lhsT=hT[:, k, :], rhs=w_in_bf[:, k, 3072:3088], start=(k == 0), stop=(k == 7))
                return ins
            S.op("pe", mmdt, reads=["hT", "w_in"], writes=["ps_cb"])
            S.op("dve", lambda e: e.tensor_tensor(out=dtx, in0=ps_cb[:, 0:16], in1=dtb_bc[:], op=ALU.add), reads=["ps_cb", "dtb_bc"], writes=[RDX])
            S.op("act", lambda e: e.activation(out=dte, in_=dtx, func=A.Exp), reads=[RDX], writes=[RDE])
            S.op("act", lambda e: e.activation(out=dtv, in_=dte, func=A.Ln, bias=1.0, scale=1.0), reads=[RDE], writes=[RDV])
            S.op("dve", lambda e: e.tensor_tensor(out=adt, in0=dtv, in1=a_bc[:], op=ALU.mult), reads=[RDV, "a_bc"], writes=[RAD])

            yield

        def pre_b(t):
            b = t % 2
            sz = szs[b]; RSZ = "sz%d" % b
            bank, bres = tok_group(0, 512)
            S.op("act", lambda e, bank=bank: e.activation(out=sz[:, 0:512], in_=bank, func=A.Silu), reads=[bres], writes=[RSZ])
            yield
            bank, bres = tok_group(512, 512)
            S.op("act", lambda e, bank=bank: e.activation(out=sz[:, 512:1024], in_=bank, func=A.Silu), reads=[bres], writes=[RSZ])
            yield
            bank, bres = fm_group(5136)
            S.op("act", lambda e, bank=bank: e.copy(out=qcT[:], in_=bank.rearrange("p (h t) -> p h t", h=4)), reads=[bres], writes=["qcT"])
            yield
            bank, bres = fm_group(5648)
            S.op("act", lambda e, bank=bank: e.activation(out=sgc, in_=bank, func=A.Silu), reads=[bres], writes=["sgc"])
            yield
            bank, bres = tok_group(4624, 512)
            S.op("act", lambda e, bank=bank: e.activation(out=sg_bf, in_=bank, func=A.Silu), reads=[bres], writes=["sg_bf"])
            dma(sg_scr[t * 128:(t + 1) * 128, :], sg_bf, ["sg_bf"], ["sg_scr"], "s_sg")
            yield
            bank, bres = tok_group(3088, 512)
            S.op("act", lambda e, bank=bank: e.copy(out=qf, in_=bank), reads=[bres], writes=["qf"])
            yield
            bank, bres = tok_group(3600, 512)
            S.op("act", lambda e, bank=bank: e.copy(out=kf, in_=bank), reads=[bres], writes=["kf"])
            yield
            bank, bres = tok_group(4112, 512)
            S.op("act", lambda e, bank=bank: e.copy(out=vf, in_=bank), reads=[bres], writes=["vf"])
            yield
            do_rotary(qf, "qf", t, 128, rt)
            do_rotary(kf, "kf", t, 128, rt)
            S.op("pool", lambda e: e.tensor_copy(out=q_bf, in_=qf), reads=["qf"], writes=["q_bf"])
            S.op("pool", lambda e: e.tensor_copy(out=k_bf, in_=kf), reads=["kf"], writes=["k_bf"])
            S.op("pool", lambda e: e.tensor_copy(out=v_aug[:, :, 0:64], in_=vf.rearrange("p (h c) -> p h c", h=8)), reads=["vf"], writes=["v_aug"])
            if t >= 16:
                dma(wk[(t - 16) * 128:(t - 15) * 128, :], kf, ["kf"], ["wk"], "o_wk")
                dma(wv[(t - 16) * 128:(t - 15) * 128, :], vf, ["vf"], ["wv"], "o_wv")
            dma(v_scr[t * 128:(t + 1) * 128, :], v_aug[:].rearrange("p h c -> p (h c)"), ["v_aug"], ["v_scr"], "s_v")

            yield
            def trqk(e):
                for c in range(4):
                    e.transpose(out=ps_tr[:, c * 128:(c + 1) * 128], in_=q_bf[:, c * 128:(c + 1) * 128], identity=identb[:])
                for c in range(4):
                    ins = e.transpose(out=ps_tr[:, (4 + c) * 128:(5 + c) * 128], in_=k_bf[:, c * 128:(c + 1) * 128], identity=identb[:])
                return ins
            S.op("pe", trqk, reads=["q_bf", "k_bf", "identb"], writes=["ps_tr"])
            S.op("act", lambda e: e.copy(out=qkT_sb[:], in_=ps_tr[:].rearrange("p (c t) -> p c t", c=8)), reads=["ps_tr"], writes=["catT_sb"])
            dma(qkT_scr[:, :, t * 128:(t + 1) * 128], qkT_sb[:], ["catT_sb"], ["qkT_scr"], "s_cat")

            yield

        def pre_sample():
            rms_to_hT(xt[0][0:NB, :], "xt", ln_bc, "ln_bc", NB, h_bf, "h_bf", h_bf, hT, "hT")
            yield
            stg = acc2[0:NB, 0:4, :].rearrange("p c t -> p (c t)")
            for gi in range(13):
                c0 = gi * 512
                ncol = min(512, NIN - c0)
                bank, bres = tok_group(c0, ncol, rows=NB)
                S.op("act", lambda e, bank=bank, ncol=ncol: e.copy(out=stg[:, 0:ncol], in_=bank[0:NB, 0:ncol]), reads=[bres], writes=["acc2D"])
                dma(us_scr[:, c0:c0 + ncol], stg[:, 0:ncol], ["acc2D"], ["us_scr"], "s_us")
                yield

        def cross(t):

            def mmcs_(e):
                for h in range(4):
                    for mt in range(2):
                        ins = e.matmul(ps_big[:, (h * 2 + mt) * 128:(h * 2 + mt + 1) * 128], lhsT=MKT[:, h, mt * 128:(mt + 1) * 128], rhs=qcT[:, h, :], start=True, stop=True)
                return ins
            S.op("pe", mmcs_, reads=["MKT", "qcT"], writes=["ps_big"])
            S.op("act", lambda e: e.activation(out=PT_bf[:], in_=ps_big.rearrange("p (h m t) -> p h m t", h=4, m=2), func=A.Exp, scale=128 ** -0.5),
                 reads=["ps_big"], writes=["E_bf"])

            def mmcv(e):
                for h in range(4):
                    for mt in range(2):
                        e.matmul(ps_y[:, h * 128:(h + 1) * 128], lhsT=MV[:, mt, h * 128:(h + 1) * 128], rhs=PT_bf[:, h, mt, :], start=(mt == 0), stop=(mt == 1))
                for mt in range(2):
                    ins = e.matmul(ps_y[:, 512:1024], lhsT=onesb[:], rhs=PT_bf[:, :, mt, :], start=(mt == 0), stop=(mt == 1))
                return ins
            S.op("pe", mmcv, reads=["MV", "E_bf", "onesb"], writes=["ps_y"])
            S.op("act", lambda e: e.activation(out=rden, in_=ps_y[:, 512:1024], func=A.Ln), reads=["ps_y"], writes=["rden"])
            S.op("act", lambda e: e.activation(out=rden, in_=rden, func=A.Exp, scale=-1.0), reads=["rden"], writes=["rden"])
            S.op("dve", lambda e: e.tensor_tensor(out=ocr, in0=ps_y[:, 0:512], in1=rden, op=ALU.mult), reads=["ps_y", "rden"], writes=["rden"])
            S.op("dve", lambda e: e.tensor_tensor(out=catTc[:].rearrange("p h t -> p (h t)"), in0=ocr, in1=sgc, op=ALU.mult), reads=["rden", "sgc"], writes=["catT_sb"])
            dma(cat_scr[t, :, 8:12, :], catTc, ["catT_sb"], ["cat_scr"], "s_cat")


        def ssd(t):
            b = t % 2
            sz = szs[b]; RSZ = "sz%d" % b
            dtx, dte, dtv, adt = dtbufs[b]
            RDV = "dtv%d" % b; RAD = "adt%d" % b
            S.op("act", lambda e: e.activation(out=xc_bf[:], in_=acc[:], func=A.Silu), reads=["accD", "accP"], writes=["xc_bf"])
            S.op("pool", lambda e: e.tensor_copy(out=xbuf[:, :, 0:3], in_=xbuf[:, :, 128:131]), reads=["xbuf"], writes=["xbuf"])
            S.op("dve", lambda e: e.tensor_tensor(out=rhsd[:], in0=tri[:].unsqueeze(1).broadcast_to([128, 16, 128]), in1=b3(adt, 16, 128), op=ALU.mult),
                 reads=["tri", RAD, "xc_bf"], writes=["accD", "accP"])
            def trx(e):
                for c in range(8):
                    ins = e.transpose(out=ps_tr[:, c * 128:(c + 1) * 128], in_=xc_bf[:, c, :], identity=identb[:])
                return ins
            S.op("pe", trx, reads=["xc_bf", "identb"], writes=["ps_tr"])
            S.op("act", lambda e: e.copy(out=xtok_bf, in_=ps_tr[:]), reads=["ps_tr"], writes=["xtok_bf"])
            S.op("pool", lambda e: e.tensor_tensor(out=yv.rearrange("p (h q) -> p h q", h=16), in0=xtok_bf.rearrange("p (h q) -> p h q", h=16), in1=b3(dsk_bc[:], 16, 64), op=ALU.mult),
                 reads=["xtok_bf", "dsk_bc"], writes=["yv"])
            S.op("dve", lambda e: e.tensor_tensor(out=xdt_bf.rearrange("p (h q) -> p h q", h=16), in0=xtok_bf.rearrange("p (h q) -> p h q", h=16), in1=b3(dtv, 16, 64), op=ALU.mult),
                 reads=["xtok_bf", RDV], writes=["xdt_bf"])

            def mmcs(e):
                e.matmul(ps_cb[:, 0:16], lhsT=tri[:], rhs=adt, start=True, stop=True)
                return e.matmul(ps_cb[:, 16:32], lhsT=onesf[:], rhs=adt, start=True, stop=True)
            S.op("pe", mmcs, reads=["tri", "onesf", RAD], writes=["ps_cb"])
            S.op("act", lambda e: e.copy(out=csb[:, 0:32], in_=ps_cb[:, 0:32]), reads=["ps_cb"], writes=["csb"])
            for half in range(2):
                pd = ps_big if half == 0 else ps_y
                pdn = "ps_big" if half == 0 else "ps_y"

                def mmd(e, half=half, pd=pd):
                    for j in range(2):
                        h0 = half * 8 + j * 4
                        ins = e.matmul(pd[:, j * 512:(j + 1) * 512], lhsT=ustr[:], rhs=rhsd[:, h0:h0 + 4, :], start=True, stop=True)
                    return ins
                S.op("pe", mmd, reads=["ustr", "accD", "accP"], writes=[pdn])
                S.op("act", lambda e, half=half, pd=pd: e.activation(out=E_bf[:, half * 8:(half + 1) * 8, :], in_=pd.rearrange("p (h l) -> p h l", h=8), func=A.Exp),
                     reads=[pdn], writes=["E_bf"])
            yield
            S.op("dve", lambda e: e.tensor_tensor(out=csb[:, 32:48], in0=csb[:, 16:32], in1=csb[:, 0:16], op=ALU.subtract), reads=["csb"], writes=["csb"])
            S.op("act", lambda e: e.activation(out=exs[:], in_=csb[:], func=A.Exp), reads=["csb"], writes=["exs"])
            ecs = exs[:, 0:16]; dec = exs[:, 16:32]; dsv = exs[:, 32:48]
            S.op("pool", lambda e: e.tensor_tensor(out=state.rearrange("p (h q) -> p h q", h=16), in0=state.rearrange("p (h q) -> p h q", h=16), in1=b3(dec, 16, 64), op=ALU.mult),
                 reads=["state", "exs"], writes=["state"])
            S.op("dve", lambda e: e.tensor_tensor(out=xds_bf.rearrange("p (h q) -> p h q", h=16), in0=xdt_bf.rearrange("p (h q) -> p h q", h=16), in1=b3(dsv, 16, 64), op=ALU.mult),
                 reads=["xdt_bf", "exs"], writes=["xtok_bf"])
            yield

            def trb(e):
                for g in range(4):
                    ins = e.transpose(out=ps_tr[:, g * 128:(g + 1) * 128], in_=xc_bf[:, 8 + g, :], identity=identb[:])
                return ins
            S.op("pe", trb, reads=["xc_bf", "identb"], writes=["ps_tr"])
            S.op("act", lambda e: e.copy(out=Btok_bf[:], in_=ps_tr[:, 0:512].rearrange("p (g n) -> p g n", g=4)), reads=["ps_tr"], writes=["Btok_bf"])

            def mmcb(e):
                for g in range(4):
                    ins = e.matmul(ps_cb[:, g * 128:(g + 1) * 128], lhsT=xc_bf[:, 8 + g, :], rhs=xc_bf[:, 12 + g, :], start=True, stop=True)
                return ins
            S.op("pe", mmcb, reads=["xc_bf"], writes=["ps_cb"])
            S.op("dve", lambda e: e.tensor_tensor(out=CBm[:], in0=ps_cb[:].rearrange("p (g l) -> p g l", g=4), in1=tri[:].unsqueeze(1).broadcast_to([128, 4, 128]), op=ALU.mult),
                 reads=["ps_cb", "tri"], writes=["CBm"])
            yield
            S.op("dve", lambda e: e.tensor_tensor(out=MT_bf[:].rearrange("p (g r) l -> p g r l", g=4), in0=E_bf[:].rearrange("p (g r) l -> p g r l", g=4),
                                                  in1=CBm[:].unsqueeze(2).broadcast_to([128, 4, 4, 128]), op=ALU.mult),
                 reads=["E_bf", "CBm"], writes=["E_bf"])
            yield

            def mmyd(e):
                for h in range(16):
                    ins = e.matmul(ps_y[:, h * 64:(h + 1) * 64], lhsT=MT_bf[:, h, :], rhs=xdt_bf[:, h * 64:(h + 1) * 64], start=True, stop=True)
                return ins
            S.op("pe", mmyd, reads=["E_bf", "xdt_bf"], writes=["ps_y"])

            def mmyo(e):
                for g in range(4):
                    ins = e.matmul(ps_big[:, g * 256:(g + 1) * 256], lhsT=xc_bf[:, 12 + g, :], rhs=state_bf[:, g * 256:(g + 1) * 256], start=True, stop=True)
                return ins
            S.op("pe", mmyo, reads=["xc_bf", "state_bf"], writes=["ps_big"])
            S.op("dve", lambda e: e.tensor_tensor(out=t1.rearrange("p (h q) -> p h q", h=16), in0=ps_big.rearrange("p (h q) -> p h q", h=16), in1=b3(ecs, 16, 64), op=ALU.mult),
                 reads=["ps_big", "exs"], writes=["t1"])
            S.op("dve", lambda e: e.tensor_tensor(out=yv, in0=yv, in1=t1, op=ALU.add), reads=["yv", "t1"], writes=["yv"])
            S.op("dve", lambda e: e.tensor_tensor(out=yv, in0=ps_y, in1=yv, op=ALU.add), reads=["ps_y", "yv"], writes=["yv"])
            S.op("dve", lambda e: e.tensor_tensor(out=yv, in0=yv, in1=sz, op=ALU.mult), reads=["yv", RSZ], writes=["yv"])
            yield
            def mmst(e):
                for g in range(4):
                    ins = e.matmul(ps_y[:, g * 256:(g + 1) * 256], lhsT=Btok_bf[:, g, :], rhs=xds_bf[:, g * 256:(g + 1) * 256], start=True, stop=True)
                return ins
            S.op("pe", mmst, reads=["Btok_bf", "xtok_bf"], writes=["ps_y"])
            S.op("dve", lambda e: e.tensor_tensor(out=state, in0=ps_y, in1=state, op=ALU.add), reads=["ps_y", "state"], writes=["state"])
            S.op("act", lambda e: e.copy(out=state_bf, in_=state), reads=["state"], writes=["state_bf"])
            cross(t)
            yield

            for g in range(4):
                S.op("act", lambda e, g=g: e.activation(out=junk[:, 0:256], in_=yv[:, g * 256:(g + 1) * 256], func=A.Square, accum_out=gss[:, g:g + 1]),
                     reads=["yv"], writes=["t1", "gss"])
            S.op("dve", lambda e: e.tensor_scalar(out=gvar, in0=gss, scalar1=1.0 / 256, scalar2=EPS, op0=ALU.mult, op1=ALU.add), reads=["gss"], writes=["gvar"])
            S.op("act", lambda e: e.activation(out=gvar, in_=gvar, func=A.Ln), reads=["gvar"], writes=["gvar"])
            S.op("act", lambda e: e.activation(out=grstd, in_=gvar, func=A.Exp, scale=-0.5), reads=["gvar"], writes=["grstd"])
            S.op("dve", lambda e: e.tensor_tensor(out=yv.rearrange("p (g c) -> p g c", g=4), in0=yv.rearrange("p (g c) -> p g c", g=4), in1=b3(grstd, 4, 256), op=ALU.mult),
                 reads=["yv", "grstd"], writes=["yv"])
            S.op("dve", lambda e: e.tensor_tensor(out=yn_bf, in0=yv, in1=sng_bc[:], op=ALU.mult), reads=["yv", "sng_bc"], writes=["xtok_bf"])
            yield

            def try_(e):
                for c in range(8):
                    ins = e.transpose(out=ps_tr[:, c * 128:(c + 1) * 128], in_=yn_bf[:, c * 128:(c + 1) * 128], identity=identb[:])
                return ins
            S.op("pe", try_, reads=["xtok_bf", "identb"], writes=["ps_tr"])
            S.op("act", lambda e: e.copy(out=catT_sb[:], in_=ps_tr[:].rearrange("p (c t) -> p c t", c=8)), reads=["ps_tr"], writes=["catT_sb"])
            dma(cat_scr[t, :, 0:8, :], catT_sb[:], ["catT_sb"], ["cat_scr"], "s_cat")

            yield

        def drain(g):
            for _ in g:
                pass

        def interleave(g1, g2):
            live = [g1, g2]
            while live:
                for g in list(live):
                    try:
                        next(g)
                    except StopIteration:
                        live.remove(g)

        def chain(*gens):
            for g in gens:
                yield from g

        load_x(0)
        drain(pre_a(0))
        for t in range(NT):
            g1 = ssd(t)
            next(g1)
            interleave(g1, chain(pre_b(t), pre_a(t + 1) if t + 1 < NT else pre_sample()))

        stT = yv

        def trst(e):
            for c in range(8):
                ins = e.matmul(ps_big[:, c * 128:(c + 1) * 128], lhsT=state[:, c * 128:(c + 1) * 128], rhs=identf[:], start=True, stop=True)
            return ins
        S.op("pe", trst, reads=["state", "identf"], writes=["ps_big"])
        S.op("act", lambda e: e.copy(out=stT, in_=ps_big), reads=["ps_big"], writes=["yv"])
        dma(ssmp.rearrange("(c p) n -> p c n", p=128), stT.rearrange("p (c n) -> p c n", c=8), ["yv"], ["ssmp"], "o_ssm")

        if UPTO >= 'S':
            S.full_barrier()
            off[0] = 0
            a1f = arena1[:].bitcast(F32)
            usb = carve(NIN)
            cat_s = carve(2048)
            dz0 = off[0]
            acc_s = carve(2048); tmp_s = carve(2048); xc_s = carve(2048)
            pk = carve(1024)
            y_s = carve(1024); szs = carve(1024)
            prod = arena2[:, dz0:dz0 + 8192]
            dss = carve(64)
            rt_s = carve(256).rearrange("p (a h i) -> p a h i", a=4, h=8)
            xdt_q = carve(128); B_q = carve(128); C_q = carve(128); dA_q = carve(2); y_q = carve(128)
            h0s = [a1f[:, 0:4096], a1f[:, 4096:8192]]
            tq = a1f[:, 8192:12288]
            o_tok = carve(512); sga = carve(512)
            numa = carve(512); numc = carve(512)
            sm = carve(32)
            snew = sm[0:NB, 0:8]; pnew = sm[0:NB, 8:16]; dena = sm[0:NB, 16:24]; denc = sm[0:NB, 24:28]
            sj = carve(128); pj = carve(128); pj_bf = carve_bf(128)
            esel_f = carve(256); Esel = carve_bf(256).rearrange("p (c m) -> p c m", c=16)
            print("arena2 phase S words:", off[0])
            R = NB
            dma(usb[0:R, :], us_scr, ["us_scr"], ["usb"], "l_s")
            cbuf = a1f[0:R, 0:8192].rearrange("p (t c) -> p t c", t=4)
            cw16 = a1f[0:R, 8192:16384].rearrange("p (t c) -> p t c", t=4)
            cb16 = a1f[0:R, 16384:18432]
            dma(cbuf[:, 0:3, :], sconv, [], ["cbuf"], "l_s")
            dma(cbuf[:, 3, :], us_scr[:, 1024:3072], ["us_scr"], ["cbuf"], "l_s")
            for tap in range(4):
                dma(cw16[:, tap, :], conv_w[tap:tap + 1, :].partition_broadcast(R), [], ["cw16"], "l_s")
            dma(cb16, conv_b.partition_broadcast(R), [], ["cb16"], "l_s")
            dma(convs, cbuf[:, 1:4, :], ["cbuf"], ["convs"], "o_cvs")
            a_s = acc_s[0:R, :]; t_s = tmp_s[0:R, :]
            S.op("dve", lambda e: e.tensor_tensor(out=a_s, in0=cbuf[:, 0, :], in1=cw16[:, 0, :], op=ALU.mult), reads=["cbuf", "cw16"], writes=["acc_s"])
            for tap in range(1, 4):
                S.op("pool", lambda e, tap=tap: e.tensor_tensor(out=t_s, in0=cbuf[:, tap, :], in1=cw16[:, tap, :], op=ALU.mult), reads=["cbuf", "cw16", "acc_s"], writes=["tmp_s"])
                S.op("dve", lambda e: e.tensor_tensor(out=a_s, in0=a_s, in1=t_s, op=ALU.add), reads=["acc_s", "tmp_s"], writes=["acc_s"])
            S.op("dve", lambda e: e.tensor_tensor(out=a_s, in0=a_s, in1=cb16, op=ALU.add), reads=["acc_s", "cb16"], writes=["acc_s"])
            S.op("act", lambda e: e.activation(out=xc_s[0:R, :], in_=a_s, func=A.Silu), reads=["acc_s"], writes=["xc_s"])
            dtx_s = dss[0:R, 0:16]; dte_s = dss[0:R, 16:32]; dt_s = dss[0:R, 32:48]; dA_s = dss[0:R, 48:64]
            S.op("dve", lambda e: e.tensor_tensor(out=dtx_s, in0=usb[0:R, 3072:3088], in1=dtb_bc[0:R, :], op=ALU.add), reads=["usb", "dtb_bc"], writes=["dtx_s"])
            S.op("act", lambda e: e.activation(out=dte_s, in_=dtx_s, func=A.Exp), reads=["dtx_s"], writes=["dte_s"])
            S.op("act", lambda e: e.activation(out=dt_s, in_=dte_s, func=A.Ln, bias=1.0, scale=1.0), reads=["dte_s"], writes=["dt_s"])
            S.op("dve", lambda e: e.tensor_tensor(out=dA_s, in0=dt_s, in1=a_bc[0:R, :], op=ALU.mult), reads=["dt_s", "a_bc"], writes=["dA_s"])
            S.op("act", lambda e: e.activation(out=dA_s, in_=dA_s, func=A.Exp), reads=["dA_s"], writes=["dA_s"])
            S.op("dve", lambda e: e.tensor_tensor(out=pk[0:R, :].rearrange("p (h q) -> p h q", h=16), in0=xc_s[0:R, 0:1024].rearrange("p (h q) -> p h q", h=16),
                                                  in1=dt_s.unsqueeze(2).broadcast_to([R, 16, 64]), op=ALU.mult), reads=["xc_s", "dt_s"], writes=["pk"])
            dma(sx_scr, pk[0:R, :], ["pk"], ["sx_scr"], "s_s")
            S.op("pool", lambda e: e.tensor_copy(out=pk[0:R, :].rearrange("p (g d n) -> p g d n", g=4, d=2),
                                                 in_=xc_s[0:R, 1024:1536].rearrange("p (g n) -> p g n", g=4).unsqueeze(2).broadcast_to([R, 4, 2, 128])), reads=["xc_s", "pk"], writes=["pk"])
            dma(sB_scr, pk[0:R, :], ["pk"], ["sB_scr"], "s_s")
            S.op("pool", lambda e: e.tensor_copy(out=pk[0:R, :].rearrange("p (g d n) -> p g d n", g=4, d=2),
                                                 in_=xc_s[0:R, 1536:2048].rearrange("p (g n) -> p g n", g=4).unsqueeze(2).broadcast_to([R, 4, 2, 128])), reads=["xc_s", "pk"], writes=["pk"])
            dma(sC_scr, pk[0:R, :], ["pk"], ["sC_scr"], "s_s")
            dma(sdA_scr, dA_s, ["dA_s"], ["sdA_scr"], "s_s")
            dma(xdt_q, sx_scr.rearrange("b (j f) -> (b j) f", j=8), ["sx_scr"], ["xdt_q"], "l_s2")
            dma(B_q, sB_scr.rearrange("b (j f) -> (b j) f", j=8), ["sB_scr"], ["B_q"], "l_s2")
            dma(C_q, sC_scr.rearrange("b (j f) -> (b j) f", j=8), ["sC_scr"], ["C_q"], "l_s2")
            dma(dA_q, sdA_scr.rearrange("b (j f) -> (b j) f", j=8), ["sdA_scr"], ["dA_q"], "l_s2")
            S.full_barrier()
            sssm3 = sssm.rearrange("(q f) n -> q f n", f=128)
            ssms3 = ssms.rearrange("(q f) n -> q f n", f=128)
            for pc in range(4):
                f0 = pc * 32
                hb = h0s[pc % 2].rearrange("p (f n) -> p f n", f=32); hres = "h0_%d" % (pc % 2)
                tq3 = tq.rearrange("p (f n) -> p f n", f=32)
                dma(hb, sssm3[:, f0:f0 + 32, :], [], [hres], "l_h%d" % (pc % 2))
                S.op("dve", lambda e, f0=f0: e.tensor_tensor(out=tq3, in0=xdt_q[:, f0:f0 + 32].unsqueeze(2).broadcast_to([128, 32, 128]),
                                                              in1=B_q.unsqueeze(1).broadcast_to([128, 32, 128]), op=ALU.mult), reads=["xdt_q", "B_q"], writes=["tq"])
                S.op("dve", lambda e, hb=hb, pc=pc: e.scalar_tensor_tensor(out=hb, in0=hb, scalar=dA_q[:, pc // 2:pc // 2 + 1], in1=tq3, op0=ALU.mult, op1=ALU.add),
                     reads=[hres, "dA_q", "tq"], writes=[hres])
                dma(ssms3[:, f0:f0 + 32, :], hb, [hres], ["ssms"], "o_h%d" % (pc % 2))
                S.op("pool", lambda e, hb=hb: e.tensor_tensor(out=tq3, in0=hb, in1=C_q.unsqueeze(1).broadcast_to([128, 32, 128]), op=ALU.mult), reads=[hres, "C_q", "tq"], writes=["tq"])
                S.op("dve", lambda e, f0=f0: e.reduce_sum(out=y_q[:, f0:f0 + 32], in_=tq3, axis=AX.X), reads=["tq"], writes=["y_q"])
            dma(sx_scr.rearrange("b (j f) -> (b j) f", j=8), y_q, ["y_q", "xdt_q"], ["sx_scr"], "s_s")
            dma(y_s[0:R, :], sx_scr, ["sx_scr"], ["y_s"], "l_s2")
            S.op("pool", lambda e: e.tensor_tensor(out=pk[0:R, :].rearrange("p (h q) -> p h q", h=16), in0=xc_s[0:R, 0:1024].rearrange("p (h q) -> p h q", h=16),
                                                   in1=dsk_bc[0:R, :].unsqueeze(2).broadcast_to([R, 16, 64]), op=ALU.mult), reads=["xc_s", "dsk_bc", "pk"], writes=["pk"])
            S.op("dve", lambda e: e.tensor_tensor(out=y_s[0:R, :], in0=y_s[0:R, :], in1=pk[0:R, :], op=ALU.add), reads=["y_s", "pk"], writes=["y_s"])
            S.op("act", lambda e: e.activation(out=szs[0:R, :], in_=usb[0:R, 0:1024], func=A.Silu), reads=["usb"], writes=["szs"])
            S.op("dve", lambda e: e.tensor_tensor(out=y_s[0:R, :], in0=y_s[0:R, :], in1=szs[0:R, :], op=ALU.mult), reads=["y_s", "szs"], writes=["y_s"])
            for g in range(4):
                S.op("act", lambda e, g=g: e.activation(out=pk[0:R, 0:256], in_=y_s[0:R, g * 256:(g + 1) * 256], func=A.Square, accum_out=gss[0:R, g:g + 1]),
                     reads=["y_s", "pk"], writes=["pk", "gss"])
            S.op("dve", lambda e: e.tensor_scalar(out=gvar[0:R, :], in0=gss[0:R, :], scalar1=1.0 / 256, scalar2=EPS, op0=ALU.mult, op1=ALU.add), reads=["gss"], writes=["gvar"])
            S.op("act", lambda e: e.activation(out=gvar[0:R, :], in_=gvar[0:R, :], func=A.Sqrt), reads=["gvar"], writes=["gvar"])
            S.op("dve", lambda e: e.reciprocal(out=grstd[0:R, :], in_=gvar[0:R, :]), reads=["gvar"], writes=["grstd"])
            S.op("dve", lambda e: e.tensor_tensor(out=y_s[0:R, :].rearrange("p (g c) -> p g c", g=4), in0=y_s[0:R, :].rearrange("p (g c) -> p g c", g=4),
                                                  in1=grstd[0:R, :].unsqueeze(2).broadcast_to([R, 4, 256]), op=ALU.mult), reads=["y_s", "grstd"], writes=["y_s"])
            S.op("dve", lambda e: e.tensor_tensor(out=cat_s[0:R, 0:1024], in0=y_s[0:R, :], in1=sng_bc[0:R, :], op=ALU.mult), reads=["y_s", "sng_bc"], writes=["cat_s"])

            S.full_barrier()
            qs = usb[0:R, 3088:3600]; ks = usb[0:R, 3600:4112]; vs = usb[0:R, 4112:4624]
            do_rotary(usb[:, 3088:3600], "usb", NT, R, rt_s)
            do_rotary(usb[:, 3600:4112], "usb", NT, R, rt_s)
            dma(wks, ks, ["usb"], ["wks"], "o_wks")
            dma(wvs, vs, ["usb"], ["wvs"], "o_wks")
            dma(sq_scr, qs, ["usb"], ["sq_scr"], "s_s")
            dma(sk_scr, usb[0:R, 5136:5648], ["usb"], ["sk_scr"], "s_s")
            Kj = a1f[:, 0:8192]; Vj = a1f[:, 8192:16384]; qbc = a1f[:, 16384:24576]
            prod2A = prod[:, 0:2816].bitcast(BF16)
            prod2B = prod[:, 5632:6912].bitcast(BF16)

            def p2(b_):
                return prod2A[:, b_ * 512:(b_ + 1) * 512] if b_ < 11 else prod2B[:, (b_ - 11) * 512:(b_ - 10) * 512]
            dma(esel_f, c_esel.partition_broadcast(128), [], ["esel_f"], "l_s2")
            S.op("dve", lambda e: e.tensor_copy(out=Esel.rearrange("p c m -> p (c m)"), in_=esel_f), reads=["esel_f"], writes=["Esel"])
            S.op("dve", lambda e: e.tensor_tensor(out=o_tok[0:R, :], in0=qs, in1=ks, op=ALU.mult), reads=["usb"], writes=["o_tok"])
            S.op("dve", lambda e: e.reduce_sum(out=snew, in_=o_tok[0:R, :].rearrange("p (h e) -> p h e", h=8), axis=AX.X), reads=["o_tok"], writes=["snew"])
            S.op("act", lambda e: e.activation(out=pnew, in_=snew, func=A.Exp, scale=0.125), reads=["snew"], writes=["pnew"])
            S.op("dve", lambda e: e.tensor_scalar(out=pnew, in0=pnew, scalar1=3.0, scalar2=None, op0=ALU.mult), reads=["pnew"], writes=["pnew"])
            S.op("dve", lambda e: e.tensor_tensor(out=numa[0:R, :].rearrange("p (h e) -> p h e", h=8), in0=vs.rearrange("p (h e) -> p h e", h=8),
                                                  in1=pnew.unsqueeze(2).broadcast_to([R, 8, 64]), op=ALU.mult), reads=["usb", "pnew"], writes=["numa"])
            S.op("dve", lambda e: e.tensor_copy(out=dena, in_=pnew), reads=["pnew"], writes=["dena"])

            for pi, d in enumerate(DILS):
                rs = slice(2048 - 128 * d, 2048 - d + 1, d)
                if pi == 0:
                    dma(qbc, sq_scr.rearrange("(o b) c -> o (b c)", o=1).partition_broadcast(128), ["sq_scr"], ["qbc"], "l_q")
                dma(Kj.rearrange("p (b c) -> p b c", b=R), cwk[:, rs, :].rearrange("b j c -> j b c"), [], ["KjA", "KjB"], "l_kg")
                dma(Vj.rearrange("p (b c) -> p b c", b=R), cwv[:, rs, :].rearrange("b j c -> j b c"), [], ["Vj"], "l_vg")
                S.op("dve", lambda e: e.tensor_tensor(out=prod[:, 0:5632], in0=Kj[:, 0:5632], in1=qbc[:, 0:5632], op=ALU.mult), reads=["KjA", "qbc"], writes=["prodA"])
                S.op("pool", lambda e: e.tensor_tensor(out=prod[:, 5632:8192], in0=Kj[:, 5632:8192], in1=qbc[:, 5632:8192], op=ALU.mult), reads=["KjB", "qbc"], writes=["prodB"])
                S.op("dve", lambda e: e.reduce_sum(out=sj, in_=prod.rearrange("p (g e) -> p g e", e=64), axis=AX.X), reads=["prodA", "prodB"], writes=["sj"])
                S.op("act", lambda e: e.activation(out=pj, in_=sj, func=A.Exp, scale=0.125), reads=["sj"], writes=["pj"])
                S.op("act", lambda e: e.copy(out=pj_bf, in_=pj), reads=["pj"], writes=["pj_bf"])
                S.op("dve", lambda e: e.tensor_tensor(out=prod2A.rearrange("p (g e) -> p g e", e=64), in0=Vj[:, 0:5632].rearrange("p (g e) -> p g e", e=64),
                                                      in1=pj[:, 0:88].unsqueeze(2).broadcast_to([128, 88, 64]), op=ALU.mult), reads=["Vj", "pj", "sj"], writes=["prodA"])
                S.op("pool", lambda e: e.tensor_tensor(out=prod2B.rearrange("p (g e) -> p g e", e=64), in0=Vj[:, 5632:8192].rearrange("p (g e) -> p g e", e=64),
                                                       in1=pj[:, 88:128].unsqueeze(2).broadcast_to([128, 40, 64]), op=ALU.mult), reads=["Vj", "pj", "sj"], writes=["prodB"])

                def mmn(e):
                    for b_ in range(R):
                        ins = e.matmul(ps_in[0:R, 0:512], lhsT=Esel[:, b_, :], rhs=p2(b_), start=(b_ == 0), stop=(b_ == R - 1))
                    return ins
                S.op("pe", mmn, reads=["prodA", "prodB", "Esel"], writes=["ps_in0"])

                def mmd_(e):
                    for b_ in range(R):
                        ins = e.matmul(ps_cb[0:R, 0:8], lhsT=Esel[:, b_, :], rhs=pj_bf[:, b_ * 8:(b_ + 1) * 8], start=(b_ == 0), stop=(b_ == R - 1))
                    return ins
                S.op("pe", mmd_, reads=["pj_bf", "Esel"], writes=["ps_cb"])
                S.op("dve", lambda e: e.tensor_tensor(out=numa[0:R, :], in0=ps_in[0:R, 0:512], in1=numa[0:R, :], op=ALU.add), reads=["ps_in0", "numa"], writes=["numa"])
                S.op("dve", lambda e: e.tensor_tensor(out=dena, in0=ps_cb[0:R, 0:8], in1=dena, op=ALU.add), reads=["ps_cb", "dena"], writes=["dena"])
            S.op("dve", lambda e: e.reciprocal(out=dena, in_=dena), reads=["dena"], writes=["dena"])
            S.op("dve", lambda e: e.tensor_tensor(out=numa[0:R, :].rearrange("p (h e) -> p h e", h=8), in0=numa[0:R, :].rearrange("p (h e) -> p h e", h=8),
                                                  in1=dena.unsqueeze(2).broadcast_to([R, 8, 64]), op=ALU.mult), reads=["numa", "dena"], writes=["numa"])
            S.op("act", lambda e: e.activation(out=sga[0:R, :], in_=usb[0:R, 4624:5136], func=A.Silu), reads=["usb"], writes=["sga"])
            S.op("dve", lambda e: e.tensor_tensor(out=cat_s[0:R, 1024:1536], in0=numa[0:R, :], in1=sga[0:R, :], op=ALU.mult), reads=["numa", "sga"], writes=["cat_s"])

            dma(qbc, sk_scr.rearrange("(o b) c -> o (b c)", o=1).partition_broadcast(128), ["sk_scr", "prodA", "prodB"], ["qbc"], "l_q")
            for mt in range(2):
                dma(Kj.rearrange("p (b c) -> p b c", b=R), cmk[:, mt * 128:(mt + 1) * 128, :].rearrange("b m c -> m b c"), [], ["KjA", "KjB"], "l_kg")
                dma(Vj.rearrange("p (b c) -> p b c", b=R), cmv[:, mt * 128:(mt + 1) * 128, :].rearrange("b m c -> m b c"), [], ["Vj"], "l_vg")
                S.op("dve", lambda e: e.tensor_tensor(out=prod[:, 0:5632], in0=Kj[:, 0:5632], in1=qbc[:, 0:5632], op=ALU.mult), reads=["KjA", "qbc"], writes=["prodA"])
                S.op("pool", lambda e: e.tensor_tensor(out=prod[:, 5632:8192], in0=Kj[:, 5632:8192], in1=qbc[:, 5632:8192], op=ALU.mult), reads=["KjB", "qbc"], writes=["prodB"])
                S.op("dve", lambda e: e.reduce_sum(out=sj[:, 0:64], in_=prod.rearrange("p (g e) -> p g e", e=128), axis=AX.X), reads=["prodA", "prodB"], writes=["sj"])
                S.op("act", lambda e: e.activation(out=pj[:, 0:64], in_=sj[:, 0:64], func=A.Exp, scale=128 ** -0.5), reads=["sj"], writes=["pj"])
                S.op("act", lambda e: e.copy(out=pj_bf[:, 0:64], in_=pj[:, 0:64]), reads=["pj"], writes=["pj_bf"])
                S.op("dve", lambda e: e.tensor_tensor(out=prod2A.rearrange("p (g e) -> p g e", e=128), in0=Vj[:, 0:5632].rearrange("p (g e) -> p g e", e=128),
                                                      in1=pj[:, 0:44].unsqueeze(2).broadcast_to([128, 44, 128]), op=ALU.mult), reads=["Vj", "pj", "sj"], writes=["prodA"])
                S.op("pool", lambda e: e.tensor_tensor(out=prod2B.rearrange("p (g e) -> p g e", e=128), in0=Vj[:, 5632:8192].rearrange("p (g e) -> p g e", e=128),
                                                       in1=pj[:, 44:64].unsqueeze(2).broadcast_to([128, 20, 128]), op=ALU.mult), reads=["Vj", "pj", "sj"], writes=["prodB"])

                def mmnc(e):
                    for b_ in range(R):
                        ins = e.matmul(ps_in[0:R, 512:1024], lhsT=Esel[:, b_, :], rhs=p2(b_), start=(b_ == 0), stop=(b_ == R - 1))
                    return ins
                S.op("pe", mmnc, reads=["prodA", "prodB", "Esel"], writes=["ps_in1"])

                def mmdc(e):
                    for b_ in range(R):
                        ins = e.matmul(ps_cb[0:R, 8:12], lhsT=Esel[:, b_, :], rhs=pj_bf[:, b_ * 4:(b_ + 1) * 4], start=(b_ == 0), stop=(b_ == R - 1))
                    return ins
                S.op("pe", mmdc, reads=["pj_bf", "Esel"], writes=["ps_cb"])
                if mt == 0:
                    S.op("dve", lambda e: e.tensor_copy(out=numc[0:R, :], in_=ps_in[0:R, 512:1024]), reads=["ps_in1"], writes=["numc"])
                    S.op("dve", lambda e: e.tensor_copy(out=denc, in_=ps_cb[0:R, 8:12]), reads=["ps_cb"], writes=["denc"])
                else:
                    S.op("dve", lambda e: e.tensor_tensor(out=numc[0:R, :], in0=ps_in[0:R, 512:1024], in1=numc[0:R, :], op=ALU.add), reads=["ps_in1", "numc"], writes=["numc"])
                    S.op("dve", lambda e: e.tensor_tensor(out=denc, in0=ps_cb[0:R, 8:12], in1=denc, op=ALU.add), reads=["ps_cb", "denc"], writes=["denc"])
            S.op("dve", lambda e: e.reciprocal(out=denc, in_=denc), reads=["denc"], writes=["denc"])
            S.op("dve", lambda e: e.tensor_tensor(out=numc[0:R, :].rearrange("p (h e) -> p h e", h=4), in0=numc[0:R, :].rearrange("p (h e) -> p h e", h=4),
                                                  in1=denc.unsqueeze(2).broadcast_to([R, 4, 128]), op=ALU.mult), reads=["numc", "denc"], writes=["numc"])
            S.op("act", lambda e: e.activation(out=sga[0:R, :], in_=usb[0:R, 5648:6160], func=A.Silu), reads=["usb", "cat_s"], writes=["sga"])
            S.op("dve", lambda e: e.tensor_tensor(out=cat_s[0:R, 1536:2048], in0=numc[0:R, :], in1=sga[0:R, :], op=ALU.mult), reads=["numc", "sga"], writes=["cat_s"])
            dma(cat_s_scr, cat_s[0:R, :], ["cat_s"], ["cat_s_scr"], "s_s")

        if UPTO >= 'B':
            S.full_barrier()
            off[0] = 0
            QKT = carve_bf(8 * SEQ).rearrange("p (c t) -> p c t", c=8)
            P_bfs = [carve_bf(2048).rearrange("p (h j q) -> p h j q", h=8, j=2) for _ in range(2)]
            o_sbs = [carve(520).rearrange("p (h c) -> p h c", h=8) for _ in range(2)]
            for c_ in range(8):
                dma(QKT[:, c_, :], qkT_scr[:, c_, :], ["qkT_scr"], ["QKT"], "l_qk")

            def qsel(pb, c, n_, d_, r_):
                s0 = 128 * d_ * n_ + r_
                return QKT[pb:pb + 64, c, s0:s0 + 127 * d_ + 1:d_]
            for di, d in enumerate(DILS):
                nb = 32 // d
                for r in range(d):
                    src = v_scr[r:SEQ - d + r + 1:d, :].rearrange("(n p) c -> p n c", p=128)
                    dma(V3[:, di, r * nb:(r + 1) * nb, :], src, ["v_scr"], ["V3_%d" % di], "l_v", eng=("sp" if r % 2 == 0 else "act"))
            blk = 0
            for di, d in enumerate(DILS):
                nb = 32 // d
                for r in range(d):
                    for n in range(nb):
                        st = 128 * d * n + r
                        kts = [n - 1, n] if n > 0 else [n]
                        nk = len(kts)
                        P_bf = P_bfs[blk % 2]; pres = "P%d" % (blk % 2)
                        o_sb = o_sbs[blk % 2]; ores = "o%d" % (blk % 2)
                        blk += 1

                        b2 = (blk - 1) % 2
                        ps4v = ps_4.rearrange("p (h j q) -> p h j q", h=8, j=2)
                        for par in range(2):
                            def mms(e, kts=kts, d=d, r=r, n=n, par=par):
                                for c in range(4):
                                    hh = par * 4 + c
                                    for j, kn in enumerate(kts):
                                        ins = e.matmul(ps_4[:, (hh * 2 + j) * 128:(hh * 2 + j + 1) * 128],
                                                       lhsT=qsel(par * 64, 4 + c, kn, d, r),
                                                       rhs=qsel(par * 64, c, n, d, r), start=True, stop=True)
                                return ins
                            S.op("pe", mms, reads=["QKT"], writes=["ps4_%d" % par])
                        for par in range(2):
                            S.op("act", lambda e, P_bf=P_bf, nk=nk, par=par: e.activation(out=P_bf[:, par * 4:(par + 1) * 4, 0:nk, :], in_=ps4v[:, par * 4:(par + 1) * 4, 0:nk, :], func=A.Exp, scale=0.125),
                                 reads=["ps4_%d" % par], writes=["P%d_%d" % (b2, par)])
                        if nk == 2:
                            mk_ = mask2[:].unsqueeze(1).broadcast_to([128, 4, 2, 128])
                        else:
                            mk_ = mask2[:, 1:2, :].unsqueeze(1).broadcast_to([128, 4, 1, 128])
                        for par in range(2):
                            S.op("dve", lambda e, P_bf=P_bf, nk=nk, mk_=mk_, par=par: e.tensor_tensor(out=P_bf[:, par * 4:(par + 1) * 4, 0:nk, :], in0=P_bf[:, par * 4:(par + 1) * 4, 0:nk, :], in1=mk_, op=ALU.mult),
                                 reads=["P%d_%d" % (b2, par), "mask2a", "mask2b"], writes=["P%d_%d" % (b2, par)])
                        for par in range(2):
                            def mmo(e, P_bf=P_bf, kts=kts, di=di, r=r, nb=nb, nk=nk, par=par):
                                for c in range(4):
                                    h = 2 * c + par
                                    for j, kn in enumerate(kts):
                                        ins = e.matmul(ps_in[:, par * 512 + c * 128:par * 512 + c * 128 + 65], lhsT=P_bf[:, par * 4 + c, j, :],
                                                       rhs=V3[:, di, r * nb + kn, h * 66:h * 66 + 65], start=(j == 0), stop=(j == nk - 1))
                                return ins
                            S.op("pe", mmo, reads=["P%d_%d" % (b2, par), "V3_%d" % di], writes=["ps_in%d" % par])
                        o_v = o_sb[:].rearrange("p (c q) k -> p c q k", q=2)
                        for par in range(2):
                            S.op("dve", lambda e, o_v=o_v, par=par: e.tensor_copy(out=o_v[:, :, par, :], in_=ps_in[:, par * 512:(par + 1) * 512].rearrange("p (c k) -> p c k", c=4)[:, :, 0:65]),
                                 reads=["ps_in%d" % par], writes=[ores])
                        dma(o_scr[di, st:st + 127 * d + 1:d, :], o_sb[:].rearrange("p h c -> p (h c)"), [ores], ["o_scr"], "s_o%d" % b2)

        if UPTO >= 'C':
            S.full_barrier()
            off[0] = 0
            xt2 = [carve(1024), carve(1024)]
            o3s = [carve(3 * 520).rearrange("p (d c) -> p d c", d=3) for _ in range(2)]
            sg_ts = [carve_bf(512) for _ in range(2)]
            catTs = [carve_bf(16 * 128).rearrange("p (c t) -> p c t", c=16) for _ in range(2)]
            osums = [carve(520).rearrange("p (h c) -> p h c", h=8) for _ in range(2)]
            rdns = [carve(8) for _ in range(2)]
            oatts = [carve(512) for _ in range(2)]
            oattbfs = [carve_bf(512) for _ in range(2)]
            hps = [carve(1024) for _ in range(2)]
            youts = [carve(1024) for _ in range(2)]
            wst = carve(2048).rearrange("p (s n) -> p s n", s=2)
            cat_s_bf = carve_bf(2048)
            statC = carve(8)
            pso = [ps_in[:, :], ps_4[:, 0:1024]]
            fng_bc = ln_bc
            dma(fng_bc[:], final_norm_g.partition_broadcast(128), [], ["ln_bc"], "c0")
            for j in range(16):
                sidx = j % 2
                dma(wst[:, sidx, :], w_out[j * 128:(j + 1) * 128, :], [], ["wst%d" % sidx], "stg%d" % sidx)
                cast(conv_engs[j % 3], w_out_bf[:, j, :], wst[:, sidx, :], ["wst%d" % sidx], ["w_out"])

            def final_norm_out(b, rows, out_ap, okey):
                hp_ap = hps[b][0:rows, :]; yo = youts[b][0:rows, :]
                sq = statC[0:rows, 3 * b:3 * b + 1]; vr = statC[0:rows, 3 * b + 1:3 * b + 2]; rs_ = statC[0:rows, 3 * b + 2:3 * b + 3]
                S.op("act", lambda e: e.activation(out=yo, in_=hp_ap, func=A.Square, accum_out=sq), reads=["hp%d" % b], writes=["yout%d" % b, "sq%d" % b])
                S.op("dve", lambda e: e.tensor_scalar(out=vr, in0=sq, scalar1=1.0 / D, scalar2=EPS, op0=ALU.mult, op1=ALU.add), reads=["sq%d" % b], writes=["vr%d" % b])
                S.op("act", lambda e: e.activation(out=vr, in_=vr, func=A.Sqrt), reads=["vr%d" % b], writes=["vr%d" % b])
                S.op("dve", lambda e: e.reciprocal(out=rs_, in_=vr), reads=["vr%d" % b], writes=["rs%d" % b])
                S.op("dve", lambda e: e.scalar_tensor_tensor(out=yo, in0=hp_ap, scalar=rs_, in1=fng_bc[0:rows, :], op0=ALU.mult, op1=ALU.mult),
                     reads=["hp%d" % b, "rs%d" % b, "ln_bc", "yout%d" % b], writes=["yout%d" % b])
                dma(out_ap, yo, ["yout%d" % b], [okey], okey + str(b), eng="act_store")

            WMAP = list(range(8)) + [12, 13, 14, 15] + [8, 9, 10, 11]

            def out_proj(b, rows, wmap=tuple(WMAP)):
                def mm(e):
                    for hf in range(2):
                        for j in range(16):
                            ins = e.matmul(pso[b][0:rows, hf * 512:(hf + 1) * 512], lhsT=catTs[b][:, j, 0:rows], rhs=w_out_bf[:, wmap[j], hf * 512:(hf + 1) * 512], start=(j == 0), stop=(j == 15))
                    return ins
                S.op("pe", mm, reads=["catT%d" % b, "w_out"], writes=["po%d" % b])

            def stageP(t):
                b = t % 2
                tsl = slice(t * 128, (t + 1) * 128)
                o3 = o3s[b]; sg_t = sg_ts[b]; catT_t = catTs[b]; osum = osums[b]; rdn = rdns[b]; oatt = oatts[b]; oatt_bf = oattbfs[b]
                dma(xt2[b][:], x[tsl, :], [], ["xt2_%d" % b], "x%d" % b)
                dma(o3[:], o_scr[:, tsl, :].rearrange("d p c -> p d c"), [], ["o3_%d" % b], "l_o3%d" % b)
                dma(sg_t, sg_scr[tsl, :], [], ["sg%d" % b], "l_sg%d" % b, eng="act")
                dma(catT_t[:, 0:12, :], cat_scr[t], [], ["catT%d" % b], "l_cat%d" % b, eng="act")
                o3v = o3.rearrange("p d (h c) -> p d h c", h=8)
                S.op("dve", lambda e: e.tensor_tensor(out=osum[:], in0=o3v[:, 0], in1=o3v[:, 1], op=ALU.add), reads=["o3_%d" % b], writes=["osum%d" % b])
                S.op("dve", lambda e: e.tensor_tensor(out=osum[:], in0=osum[:], in1=o3v[:, 2], op=ALU.add), reads=["o3_%d" % b, "osum%d" % b], writes=["osum%d" % b])
                S.op("dve", lambda e: e.reciprocal(out=rdn, in_=osum[:, :, 64]), reads=["osum%d" % b], writes=["rdn%d" % b])
                S.op("dve", lambda e: e.tensor_tensor(out=oatt.rearrange("p (h c) -> p h c", h=8), in0=osum[:, :, 0:64], in1=b3(rdn, 8, 64), op=ALU.mult), reads=["osum%d" % b, "rdn%d" % b], writes=["oatt%d" % b])
                S.op("dve", lambda e: e.tensor_tensor(out=oatt_bf, in0=oatt, in1=sg_t, op=ALU.mult), reads=["oatt%d" % b, "sg%d" % b], writes=["oattbf%d" % b])

            def stageP2(t):
                b = t % 2
                catT_t = catTs[b]; oatt_bf = oattbfs[b]

                def tra(e):
                    for c in range(4):
                        ins = e.transpose(out=ps_tr[:, b * 512 + c * 128:b * 512 + (c + 1) * 128], in_=oatt_bf[:, c * 128:(c + 1) * 128], identity=identb[:])
                    return ins
                S.op("pe", tra, reads=["oattbf%d" % b, "identb"], writes=["ptr%d" % b])
                S.op("act", lambda e: e.copy(out=catT_t[:, 12:16, :], in_=ps_tr[:, b * 512:(b + 1) * 512].rearrange("p (c t) -> p c t", c=4)), reads=["ptr%d" % b], writes=["catT%d" % b])

            def stageQ1(t):
                out_proj(t % 2, 128)

            def stageQ2(t):
                b = t % 2
                S.op("dve", lambda e: e.tensor_tensor(out=hps[b][:], in0=pso[b], in1=xt2[b][:], op=ALU.add), reads=["po%d" % b, "xt2_%d" % b], writes=["hp%d" % b])
                final_norm_out(b, 128, yp[t * 128:(t + 1) * 128, :], "o_y")

            stageP(0)
            stageP2(0)
            for t in range(NT):
                if t + 1 < NT:
                    stageP(t + 1)
                stageQ1(t)
                if t + 1 < NT:
                    stageP2(t + 1)
                stageQ2(t)

            if UPTO >= 'S':
                for q4 in range(4):
                    dma(oatts[0][0:NB, 0:512], cat_s_scr[:, q4 * 512:(q4 + 1) * 512], ["cat_s_scr"], ["oatt0"], "l_sg0")
                    S.op("dve", lambda e, q4=q4: e.tensor_copy(out=cat_s_bf[0:NB, q4 * 512:(q4 + 1) * 512], in_=oatts[0][0:NB, 0:512]), reads=["oatt0"], writes=["cat_s_bf"])

                def trs(e):
                    for j in range(16):
                        ins = e.transpose(out=ps_tr[:, j * 16:(j + 1) * 16], in_=cat_s_bf[0:NB, j * 128:(j + 1) * 128], identity=identb[0:NB, 0:NB])
                    return ins
                S.op("pe", trs, reads=["cat_s_bf", "identb"], writes=["ptr0"])
                S.op("act", lambda e: e.copy(out=catTs[0][:, :, 0:NB], in_=ps_tr[:, 0:256].rearrange("p (c t) -> p c t", c=16)), reads=["ptr0"], writes=["catT0"])
                out_proj(0, NB, wmap=tuple(range(16)))
                dma(xt2[0][0:NB, :], xs, [], ["xt2_0"], "x0")
                S.op("dve", lambda e: e.tensor_tensor(out=hps[0][0:NB, :], in0=pso[0][0:NB, :], in1=xt2[0][0:NB, :], op=ALU.add), reads=["po0", "xt2_0"], writes=["hp0"])
                final_norm_out(0, NB, ys, "o_ys")
        S.barrier_all("sp")
        S.emit()
    return nc


_CACHE = {}


def _consts():
    k = np.arange(128)
    tri = (k[:, None] <= k[None, :]).astype(np.float32)
    return dict(c_ident=np.eye(128, dtype=np.float32), c_tri=tri, c_triT=np.ascontiguousarray(tri.T),
                c_posf=(128 * np.arange(32)[None, :] + k[:, None]).astype(np.float32),
                c_inv=(500000.0 ** (-np.arange(8, dtype=np.float32) / 8)).astype(np.float32)[None, :],
                c_esel=np.eye(16, dtype=np.float32).reshape(1, 256))


def kernel(x_prompt, x_sample, mem_prompt, cache_win_k, cache_win_v, cache_mem_k, cache_mem_v,
           state_conv, state_ssm, pos_sample, ln_g, w_in, conv_w, conv_b, dt_bias, a_log, d_skip,
           ssd_norm_g, mem_norm_g, w_mem_kv, w_out, final_norm_g):
    f = lambda a: np.ascontiguousarray(np.asarray(a, dtype=np.float32))
    if "nc" not in _CACHE:
        _CACHE["nc"] = build_program()
    nc = _CACHE["nc"]
    cs = _consts()
    in_maps = []
    for i in range(8):
        sl = slice(NB * i, NB * (i + 1))
        d = dict(x=f(x_prompt[i]), xs=f(x_sample[sl, 0]), mem=f(mem_prompt[i]),
                 cwk=f(cache_win_k[0, sl]).reshape(NB, 2048, 512), cwv=f(cache_win_v[0, sl]).reshape(NB, 2048, 512),
                 cmk=f(cache_mem_k[0, sl]).reshape(NB, 256, 512), cmv=f(cache_mem_v[0, sl]).reshape(NB, 256, 512),
                 sconv=f(state_conv[0, sl]), sssm=f(state_ssm[0, sl]).reshape(-1, 128),
                 pos=np.ascontiguousarray(np.asarray(pos_sample[sl], dtype=np.int32)),
                 ln_g=f(ln_g), w_in=f(w_in[0]), conv_w=f(conv_w[0]), conv_b=f(conv_b), dt_bias=f(dt_bias),
                 a_log=f(a_log), d_skip=f(d_skip), ssd_norm_g=f(ssd_norm_g), mem_norm_g=f(mem_norm_g),
                 w_mem_kv=f(w_mem_kv[0]), w_out=f(w_out[0]), final_norm_g=f(final_norm_g)[None, :])
        d.update(cs)
        in_maps.append(d)
    res = run_bass_kernel_spmd(nc, in_maps, core_ids=list(range(8)))
    R = res.results
    cat = lambda name: np.stack([np.asarray(r[name], dtype=np.float32) for r in R], axis=0)
    y_prompt = cat("yp")
    y_sample = cat("ys").reshape(128, 1, 1024)
    win_k_prompt = cat("wk").reshape(1, 8, 2048, 8, 64)
    win_v_prompt = cat("wv").reshape(1, 8, 2048, 8, 64)
    mem_k_prompt = cat("mko").reshape(1, 8, 256, 4, 128)
    mem_v_prompt = cat("mvo").reshape(1, 8, 256, 4, 128)
    conv_prompt = cat("convp").reshape(1, 8, 3, 2048)
    ssm_prompt = cat("ssmp").reshape(1, 8, 16, 64, 128)
    win_k_sample = cat("wks").reshape(1, 128, 1, 8, 64)
    win_v_sample = cat("wvs").reshape(1, 128, 1, 8, 64)
    conv_sample = cat("convs").reshape(1, 128, 3, 2048)
    ssm_sample = cat("ssms").reshape(1, 128, 16, 64, 128)
    return (y_prompt, y_sample, win_k_prompt, win_v_prompt, mem_k_prompt, mem_v_prompt, conv_prompt, ssm_prompt,
            win_k_sample, win_v_sample, conv_sample, ssm_sample)
```

```python
import math
from contextlib import ExitStack
import numpy as np
import concourse.bass as bass
import concourse.mybir as mybir
from concourse.bass_utils import run_bass_kernel_spmd

F32 = mybir.dt.float32
BF16 = mybir.dt.bfloat16
I32 = mybir.dt.int32
ALU = mybir.AluOpType
AF = mybir.ActivationFunctionType
AX = mybir.AxisListType

COMPUTE = ("pe", "act", "dve", "pool")
ENGINES = ("pe", "act", "dve", "pool", "sp")

D = 1024
SEQ = 4096
NT = SEQ // 128
NIN = 6160
NB = 16
EPS = 1e-6
DILS = (1, 4, 16)


class Sched:
    def __init__(self, nc):
        self.nc = nc
        self.ops = {e: [] for e in ENGINES}
        self.cnt = {e: 0 for e in COMPUTE}
        self.last_write = {}
        self.readers = {}
        self.dma_cnt = {}
        self.waited = {e: {} for e in ENGINES}

    def _tok_val(self, tok):
        kind, name, val = tok
        if kind == "dma":
            return ("dma:" + name, 16 * self.dma_cnt[name])
        return (name, val)

    def op(self, eng, fn, reads=(), writes=(), dma=None):
        waits = {}

        def need(tok, raw):
            if tok is None:
                return
            kind, name, val = tok
            if kind == "eng" and name == eng and (eng == "pe" or not raw):
                return
            s, v = self._tok_val(tok)
            if waits.get(s, 0) < v:
                waits[s] = v

        for r in reads:
            need(self.last_write.get(r), True)
        for w in writes:
            need(self.last_write.get(w), True)
            for t in self.readers.get(w, ()):
                need(t, False)
        wl = []
        for s, v in waits.items():
            if self.waited[eng].get(s, 0) < v:
                self.waited[eng][s] = v
                wl.append((s, v))
        if dma is not None:
            self.dma_cnt[dma] = self.dma_cnt.get(dma, 0) + 1
            tok = ("dma", dma, None)
            inc = ("dma:" + dma, 16)
        else:
            self.cnt[eng] += 1
            tok = ("eng", eng, self.cnt[eng])
            inc = (eng, 1)
        self.ops[eng].append((wl, fn, inc))
        for r in reads:
            self.readers.setdefault(r, []).append(tok)
        for w in writes:
            self.last_write[w] = tok
            self.readers[w] = []
        return tok

    def barrier_all(self, eng="sp"):
        wl = []
        for e in COMPUTE:
            if e != eng and self.cnt[e] > 0 and self.waited[eng].get(e, 0) < self.cnt[e]:
                self.waited[eng][e] = self.cnt[e]
                wl.append((e, self.cnt[e]))
        for k, c in self.dma_cnt.items():
            s = "dma:" + k
            if self.waited[eng].get(s, 0) < 16 * c:
                self.waited[eng][s] = 16 * c
                wl.append((s, 16 * c))
        self.ops[eng].append((wl, None, None))

    def full_barrier(self):
        for e in ENGINES:
            self.barrier_all(e)

    def emit(self):
        nc = self.nc
        names = list(COMPUTE) + ["dma:" + k for k in self.dma_cnt]
        with ExitStack() as es:
            sems = {n: es.enter_context(nc.semaphore(("s_" + n).replace(":", "_")))
                    for n in names}
            block = es.enter_context(nc.Block())

            def run(engname):
                def body(eng):
                    for wl, fn, inc in self.ops[engname]:
                        for s, v in wl:
                            eng.wait_ge(sems[s], v)
                        if fn is not None:
                            ins = fn(eng)
                            ins.then_inc(sems[inc[0]], inc[1])
                return body

            block.tensor(run("pe"))
            block.scalar(run("act"))
            block.vector(run("dve"))
            block.gpsimd(run("pool"))
            block.sync(run("sp"))


import os
UPTO = os.environ.get('K_UPTO', 'S')
BSTOP = int(os.environ.get('K_BSTOP', '9'))


def build_program():
    nc = bass.Bass("TRN2", target_bir_lowering=False)
    S = Sched(nc)

    def din(name, shape, dt=F32):
        return nc.dram_tensor(name, list(shape), dt, kind="ExternalInput").ap()

    def dout(name, shape, dt=F32):
        return nc.dram_tensor(name, list(shape), dt, kind="ExternalOutput").ap()

    def dscr(name, shape, dt):
        return nc.dram_tensor(name, list(shape), dt).ap()

    x = din("x", [SEQ, D])
    xs = din("xs", [NB, D])
    mem = din("mem", [256, D])
    cwk = din("cwk", [NB, 2048, 512])
    cwv = din("cwv", [NB, 2048, 512])
    cmk = din("cmk", [NB, 256, 512])
    cmv = din("cmv", [NB, 256, 512])
    sconv = din("sconv", [NB, 3, 2048])
    sssm = din("sssm", [NB * 16 * 64, 128])
    pos = din("pos", [NB, 1], I32)
    ln_g = din("ln_g", [1, D])
    w_in = din("w_in", [D, NIN])
    conv_w = din("conv_w", [4, 2048])
    conv_b = din("conv_b", [1, 2048])
    dt_bias = din("dt_bias", [1, 16])
    a_log = din("a_log", [1, 16])
    d_skip = din("d_skip", [1, 16])
    ssd_norm_g = din("ssd_norm_g", [1, D])
    mem_norm_g = din("mem_norm_g", [1, D])
    w_mem_kv = din("w_mem_kv", [D, D])
    w_out = din("w_out", [2048, D])
    final_norm_g = din("final_norm_g", [1, D])
    c_ident = din("c_ident", [128, 128])
    c_tri = din("c_tri", [128, 128])
    c_triT = din("c_triT", [128, 128])
    c_posf = din("c_posf", [128, NT])
    c_inv = din("c_inv", [1, 8])
    c_esel = din("c_esel", [1, 256])

    yp = dout("yp", [SEQ, D])
    ys = dout("ys", [NB, D])
    wk = dout("wk", [2048, 512])
    wv = dout("wv", [2048, 512])
    mko = dout("mko", [256, 512])
    mvo = dout("mvo", [256, 512])
    convp = dout("convp", [3, 2048])
    ssmp = dout("ssmp", [1024, 128])
    wks = dout("wks", [NB, 512])
    wvs = dout("wvs", [NB, 512])
    convs = dout("convs", [NB, 3, 2048])
    ssms = dout("ssms", [NB * 16 * 64, 128])

    cat_scr = dscr("cat_scr", [NT, 128, 12, 128], BF16)
    qkT_scr = dscr("qkT_scr", [128, 8, SEQ], BF16)
    v_scr = dscr("v_scr", [SEQ, 528], BF16)
    sg_scr = dscr("sg_scr", [SEQ, 512], BF16)
    o_scr = dscr("o_scr", [3, SEQ, 520], F32)
    s_scr = dscr("s_scr", [NB, 4096], F32)
    us_scr = dscr("us_scr", [NB, NIN], F32)
    cat_s_scr = dscr("cat_s_scr", [NB, 2048], F32)
    sx_scr = dscr("sx_scr", [NB, 1024], F32)
    sB_scr = dscr("sB_scr", [NB, 1024], F32)
    sC_scr = dscr("sC_scr", [NB, 1024], F32)
    sdA_scr = dscr("sdA_scr", [NB, 16], F32)
    sq_scr = dscr("sq_scr", [NB, 512], F32)
    sk_scr = dscr("sk_scr", [NB, 512], F32)
    sv_scr = dscr("sv_scr", [NB, 512], F32)

    with ExitStack() as es:
        def sb(name, shape, dt=F32):
            return es.enter_context(nc.sbuf_tensor(name, list(shape), dt))

        def ps(name, shape, dt=F32):
            return es.enter_context(nc.psum_tensor(name, list(shape), dt))

        ps_tr = ps("ps_tr", [128, 1024], BF16)
        ps_cb = ps("ps_cb", [128, 512], F32)
        ps_in = ps("ps_in", [128, 1024], F32)
        ps_4 = ps("ps_4", [128, 2048], F32)
        ps_big = ps_4[:, 0:1024]
        ps_y = ps_4[:, 1024:2048]

        arena1 = sb("arena1", [128, 50688], BF16)
        w_in_bf = arena1[:, 0:8 * NIN].rearrange("p (k n) -> p k n", n=NIN)
        V3 = arena1[:, 0:3 * 32 * 528].rearrange("p (d t c) -> p d t c", d=3, t=32)
        w_out_bf = arena1[:, 0:16 * 1024].rearrange("p (k n) -> p k n", n=1024)

        identf = sb("identf", [128, 128]); identb = sb("identb", [128, 128], BF16)
        tri = sb("tri", [128, 128]); ustr = sb("ustr", [128, 128]); onesf = sb("onesf", [128, 128])
        onesb = sb("onesb", [128, 128], BF16)
        mask2 = sb("mask2", [128, 2, 128], BF16)
        tmpc = sb("tmpc", [128, 128])
        ln_bc = sb("ln_bc", [128, D]); sng_bc = sb("sng_bc", [128, D])
        dtb_bc = sb("dtb_bc", [128, 16]); a_bc = sb("a_bc", [128, 16]); dsk_bc = sb("dsk_bc", [128, 16])
        cwb = sb("cwb", [128, 80])
        posf = sb("posf", [128, 33]); posi = sb("posi", [128, 1], I32)
        inv_bc = sb("inv_bc", [128, 8])
        cosT = sb("cosT", [128, 33, 8]); sinT = sb("sinT", [128, 33, 8])
        ang = sb("ang", [128, 33, 8]); rr = sb("rr", [128, 33, 8]); yy = sb("yy", [128, 33, 8])
        MKT = sb("MKT", [128, 4, 256], BF16); MV = sb("MV", [128, 2, 512], BF16)
        stat = sb("stat", [128, 16])

        A2 = 22316
        arena2 = sb("arena2", [128, A2], F32)
        off = [0]

        def carve(n_f32):
            a = off[0]
            off[0] += n_f32
            assert off[0] <= A2, off[0]
            return arena2[:, a:a + n_f32]

        def carve_bf(n_bf16):
            n = (n_bf16 + 1) // 2
            return carve(n).bitcast(BF16)[:, 0:n_bf16]

        A = AF

        def dma(out, in_, reads, writes, key, eng="sp", **kw):
            q = "act" if eng == "act_store" else "sp"
            return S.op(q, lambda e: e.dma_start(out=out, in_=in_, **kw), reads=reads, writes=writes, dma=key)

        dma(identf[:], c_ident, [], ["identf"], "c0")
        dma(tri[:], c_tri, [], ["tri"], "c0")
        dma(tmpc[:], c_triT, [], ["tmpc"], "c0")
        dma(ln_bc[:], ln_g.partition_broadcast(128), [], ["ln_bc"], "c0")
        dma(sng_bc[:], ssd_norm_g.partition_broadcast(128), [], ["sng_bc"], "c0")
        dma(dtb_bc[:], dt_bias.partition_broadcast(128), [], ["dtb_bc"], "c0")
        dma(a_bc[:], a_log.partition_broadcast(128), [], ["a_bc"], "c0")
        dma(dsk_bc[:], d_skip.partition_broadcast(128), [], ["dsk_bc"], "c0")
        dma(inv_bc[:], c_inv.partition_broadcast(128), [], ["inv_bc"], "c0")
        dma(posf[:, 0:NT], c_posf, [], ["posf"], "c0")
        S.op("pool", lambda e: e.memset(posi[:], 0), writes=["posi"])
        dma(posi[0:NB, :], pos, ["posi"], ["posi"], "c0")
        S.op("dve", lambda e: e.tensor_copy(out=posf[:, NT:NT + 1], in_=posi[:]), reads=["posi", "posf"], writes=["posf"])
        S.op("dve", lambda e: e.tensor_copy(out=identb[:], in_=identf[:]), reads=["identf"], writes=["identb"])
        S.op("dve", lambda e: e.tensor_scalar(out=ustr[:], in0=tri[:], scalar1=-1.0, scalar2=1.0, op0=ALU.mult, op1=ALU.add), reads=["tri"], writes=["ustr"])
        S.op("pool", lambda e: e.memset(onesf[:], 1.0), writes=["onesf"])
        S.op("pool", lambda e: e.memset(onesb[:], 1.0), writes=["onesb"])
        S.op("dve", lambda e: e.tensor_copy(out=mask2[:, 0, :], in_=tmpc[:]), reads=["tmpc"], writes=["mask2a"])
        S.op("dve", lambda e: e.tensor_copy(out=mask2[:, 1, :], in_=tri[:]), reads=["tri"], writes=["mask2b"])
        S.op("act", lambda e: e.activation(out=a_bc[:], in_=a_bc[:], func=A.Exp), reads=["a_bc"], writes=["a_bc"])
        S.op("dve", lambda e: e.tensor_scalar(out=a_bc[:], in0=a_bc[:], scalar1=-1.0, scalar2=None, op0=ALU.mult), reads=["a_bc"], writes=["a_bc"])

        MAGIC = 12582912.0
        C1 = 6.28125
        C2 = 2 * math.pi - C1
        S.op("dve", lambda e: e.tensor_tensor(out=ang[:], in0=posf[:].unsqueeze(2).broadcast_to([128, 33, 8]),
                                              in1=inv_bc[:].unsqueeze(1).broadcast_to([128, 33, 8]), op=ALU.mult),
             reads=["posf", "inv_bc"], writes=["ang"])
        for (dst, shift, name) in ((sinT, 0.0, "sinT"), (cosT, 0.25, "cosT")):
            S.op("dve", lambda e, shift=shift: e.tensor_scalar(out=rr[:], in0=ang[:], scalar1=1.0 / (2 * math.pi), scalar2=shift, op0=ALU.mult, op1=ALU.add), reads=["ang"], writes=["rr"])
            S.op("dve", lambda e: e.tensor_scalar(out=rr[:], in0=rr[:], scalar1=MAGIC, scalar2=None, op0=ALU.add), reads=["rr"], writes=["rr"])
            S.op("dve", lambda e: e.tensor_scalar(out=rr[:], in0=rr[:], scalar1=-MAGIC, scalar2=None, op0=ALU.add), reads=["rr"], writes=["rr"])
            S.op("dve", lambda e: e.scalar_tensor_tensor(out=yy[:], in0=rr[:], scalar=-C1, in1=ang[:], op0=ALU.mult, op1=ALU.add), reads=["rr", "ang"], writes=["yy"])
            S.op("dve", lambda e: e.scalar_tensor_tensor(out=yy[:], in0=rr[:], scalar=-C2, in1=yy[:], op0=ALU.mult, op1=ALU.add), reads=["rr", "yy"], writes=["yy"])
            if shift:
                S.op("dve", lambda e: e.tensor_scalar(out=yy[:], in0=yy[:], scalar1=math.pi / 2, scalar2=None, op0=ALU.add), reads=["yy"], writes=["yy"])
            S.op("dve", lambda e: e.tensor_scalar(out=yy[:], in0=yy[:], scalar1=3.141592, scalar2=-3.141592, op0=ALU.min, op1=ALU.max), reads=["yy"], writes=["yy"])
            S.op("act", lambda e, dst=dst: e.activation(out=dst[:], in_=yy[:], func=A.Sin), reads=["yy"], writes=[name])

        cstage = carve(128)
        dma(cstage[0:64, :], conv_w.rearrange("t (c p) -> (t c) p", p=128), [], ["cstage"], "c1")
        dma(cstage[64:80, :], conv_b.rearrange("o (c p) -> (o c) p", p=128), [], ["cstage"], "c1")
        S.op("pe", lambda e: e.matmul(ps_cb[:, 0:80], lhsT=cstage[0:80, :], rhs=identf[0:80, 0:80], start=True, stop=True), reads=["cstage", "identf"], writes=["ps_cb"])
        S.op("act", lambda e: e.copy(out=cwb[:], in_=ps_cb[:, 0:80]), reads=["ps_cb"], writes=["cwb"])

        stage = carve(3 * 3080).rearrange("p (s n) -> p s n", s=3)
        conv_engs = ("act", "pool", "dve")

        def cast(eng, out, in_, reads, writes):
            if eng == "act":
                S.op("act", lambda e: e.copy(out=out, in_=in_), reads=reads, writes=writes)
            else:
                S.op(eng, lambda e: e.tensor_copy(out=out, in_=in_), reads=reads, writes=writes)

        ci = 0
        for k in range(8):
            for hf in range(2):
                sidx = ci % 3
                c0 = hf * 3080
                dma(stage[:, sidx, :], w_in[k * 128:(k + 1) * 128, c0:c0 + 3080], [], ["stage%d" % sidx], "stg%d" % sidx)
                cast(("act", "dve", "act", "dve", "pool")[ci % 5], w_in_bf[:, k, c0:c0 + 3080], stage[:, sidx, :], ["stage%d" % sidx], ["w_in"])
                ci += 1

        ssq = stat[:, 0:1]; var = stat[:, 1:2]; rstd = stat[:, 2:3]
        junk0 = carve(1024)
        h_bf0 = carve_bf(1024)
        xt0 = [carve(1024), carve(1024)]

        def rms_to_hT(x_ap, xres, g_bc, gres, rows, junk, jres, h_bf, hT_dst, hres):
            S.op("act", lambda e: e.activation(out=junk[0:rows, :], in_=x_ap, func=A.Square, accum_out=ssq[0:rows, :]),
                 reads=[xres], writes=[jres, "ssq"])
            S.op("dve", lambda e: e.tensor_scalar(out=var[0:rows, :], in0=ssq[0:rows, :], scalar1=1.0 / D, scalar2=EPS, op0=ALU.mult, op1=ALU.add),
                 reads=["ssq"], writes=["var"])
            S.op("act", lambda e: e.activation(out=var[0:rows, :], in_=var[0:rows, :], func=A.Ln), reads=["var"], writes=["var"])
            S.op("act", lambda e: e.activation(out=rstd[0:rows, :], in_=var[0:rows, :], func=A.Exp, scale=-0.5), reads=["var"], writes=["rstd"])
            S.op("dve", lambda e: e.scalar_tensor_tensor(out=h_bf[0:rows, :], in0=x_ap, scalar=rstd[0:rows, :], in1=g_bc[0:rows, :], op0=ALU.mult, op1=ALU.mult),
                 reads=[xres, "rstd", gres], writes=["h_bf"])

            def tr(e):
                for k in range(8):
                    ins = e.transpose(out=ps_tr[:, k * 128:k * 128 + rows], in_=h_bf[0:rows, k * 128:(k + 1) * 128], identity=identb[0:rows, 0:rows])
                return ins
            S.op("pe", tr, reads=["h_bf", "identb"], writes=["ps_tr"])
            dst = hT_dst
            S.op("act", lambda e: e.copy(out=dst[:, :, 0:rows], in_=ps_tr[:].rearrange("p (k t) -> p k t", k=8)[:, :, 0:rows]),
                 reads=["ps_tr"], writes=[hres])

        mem_bc = carve(1024)
        dma(mem_bc[:], mem_norm_g.partition_broadcast(128), [], ["mem_bc"], "c1")
        wkv_bf = carve_bf(8 * 1024).rearrange("p (s n) -> p s n", s=8)
        mkv_f = carve(1024)
        mk_bf = carve_bf(512)
        for mt in range(2):
            dma(xt0[mt][:], mem[mt * 128:(mt + 1) * 128, :], [], ["xt%d" % mt], "x%d" % mt)
        hT2 = carve_bf(2 * 1024).rearrange("p (m k t) -> p m k t", m=2, k=8)
        for mt in range(2):
            rms_to_hT(xt0[mt][:], "xt%d" % mt, mem_bc, "mem_bc", 128, junk0, "junk0", h_bf0, hT2[:, mt], "hT2_%d" % mt)
        for k in range(8):
            sidx = k % 2
            dma(stage[:, sidx, 0:1024], w_mem_kv[k * 128:(k + 1) * 128, :], [], ["stage%d" % sidx], "stg%d" % sidx)
            cast(conv_engs[k % 3], wkv_bf[:, k, :], stage[:, sidx, 0:1024], ["stage%d" % sidx], ["wkv"])

        def mmkv(e):
            for mt in range(2):
                for hf in range(2):
                    for k in range(8):
                        ins = e.matmul(ps_4[:, mt * 1024 + hf * 512: mt * 1024 + (hf + 1) * 512], lhsT=hT2[:, mt, k, :],
                                       rhs=wkv_bf[:, k, hf * 512:(hf + 1) * 512], start=(k == 0), stop=(k == 7))
            return ins
        S.op("pe", mmkv, reads=["hT2_0", "hT2_1", "wkv"], writes=["ps_4"])
        for mt in range(2):
            S.op("act", lambda e, mt=mt: e.copy(out=mkv_f[:], in_=ps_4[:, mt * 1024:(mt + 1) * 1024]), reads=["ps_4"], writes=["mkv_f"])
            dma(mko[mt * 128:(mt + 1) * 128, :], mkv_f[:, 0:512], ["mkv_f"], ["mko"], "o_mk")
            dma(mvo[mt * 128:(mt + 1) * 128, :], mkv_f[:, 512:1024], ["mkv_f"], ["mvo"], "o_mk")
            S.op("dve", lambda e: e.tensor_copy(out=mk_bf[:], in_=mkv_f[:, 0:512]), reads=["mkv_f"], writes=["mk_bf"])
            S.op("dve", lambda e, mt=mt: e.tensor_copy(out=MV[:, mt, :], in_=mkv_f[:, 512:1024]), reads=["mkv_f"], writes=["MV"])

            def trk(e):
                for h in range(4):
                    ins = e.transpose(out=ps_tr[:, h * 128:(h + 1) * 128], in_=mk_bf[:, h * 128:(h + 1) * 128], identity=identb[:])
                return ins
            S.op("pe", trk, reads=["mk_bf", "identb"], writes=["ps_tr"])
            S.op("act", lambda e, mt=mt: e.copy(out=MKT[:, :, mt * 128:(mt + 1) * 128], in_=ps_tr[:, 0:512].rearrange("p (h m) -> p h m", h=4)),
                 reads=["ps_tr"], writes=["MKT"])

        off_A = off[0]
        off[0] = 0
        S.full_barrier()
        t1 = carve(1024); junk = t1
        h_bf = carve_bf(1024)
        hT = carve_bf(1024).rearrange("p (k t) -> p k t", k=8)
        xt1 = carve(1024)
        xt = [xt1, xt1]
        szs = [carve(1024), carve(1024)]
        qf = carve(512); kf = carve(512); vf = carve(512)
        rden = carve(512); ocr = rden
        sgc = carve(512)
        rt = carve(256).rearrange("p (a h i) -> p a h i", a=4, h=8)
        q_bf = carve_bf(512); k_bf = carve_bf(512); sg_bf = carve_bf(512)
        v_aug = carve_bf(528).rearrange("p (h c) -> p h c", c=66)
        dts = carve(128)
        dtbufs = [tuple(dts[:, p_ * 64 + i_ * 16:p_ * 64 + (i_ + 1) * 16] for i_ in range(4)) for p_ in range(2)]
        csb = carve(48); exs = carve(48)
        xbuf = carve(16 * 131).rearrange("p (c t) -> p c t", c=16)
        acc = carve(2048).rearrange("p (c t) -> p c t", c=16)
        acc2 = carve(2048).rearrange("p (c t) -> p c t", c=16)
        xc_bf = carve_bf(2048).rearrange("p (c t) -> p c t", c=16)
        xtok_bf = carve_bf(1024); xdt_bf = carve_bf(1024); xds_bf = xtok_bf
        yn_bf = xds_bf
        Btok_bf = carve_bf(512).rearrange("p (g n) -> p g n", g=4)
        CBm = carve_bf(512).rearrange("p (g l) -> p g l", g=4)
        rhsd = acc
        E_flat = carve_bf(2048)
        E_bf = E_flat.rearrange("p (h l) -> p h l", h=16)
        MT_bf = E_bf
        PT_bf = E_flat[:, 0:1024].rearrange("p (h m t) -> p h m t", h=4, m=2)
        state = carve(1024); state_bf = carve_bf(1024)
        yv = carve(1024)
        gss = stat[:, 4:8]; gvar = stat[:, 8:12]; grstd = stat[:, 12:16]
        catT_sb = carve_bf(1024).rearrange("p (c t) -> p c t", c=8)
        qkT_sb = catT_sb
        qcT = carve_bf(512).rearrange("p (h t) -> p h t", h=4)
        catTc = catT_sb[:, 0:4, :]
        print("arena2 phase A words:", off[0])

        S.op("pool", lambda e: e.memset(state[:], 0.0), writes=["state"])
        S.op("pool", lambda e: e.memset(state_bf[:], 0.0), writes=["state_bf"])
        S.op("pool", lambda e: e.memset(xbuf[:, :, 0:3], 0.0), writes=["xbuf"])
        S.op("pool", lambda e: e.memset(v_aug[:], 1.0), writes=["v_aug"])

        TOK_GROUPS = (("z0", 0), ("z1", 512), ("q", 3088), ("k", 3600), ("v", 4112), ("g", 4624))
        FM_GROUPS = (("xbc0", 1024), ("xbc1", 1536), ("xbc2", 2048), ("xbc3", 2560), ("qc", 5136), ("gc", 5648))
        bank_i = [0]

        def next_bank():
            b = bank_i[0] % 2
            bank_i[0] += 1
            return ps_in[:, b * 512:(b + 1) * 512], "ps_in%d" % b

        def load_x(t):
            b = t % 2
            if t < NT:
                dma(xt[b][:], x[t * 128:(t + 1) * 128, :], [], ["xt"], "x0")
            else:
                dma(xt[b][0:NB, :], xs, [], ["xt"], "x0")

        def tok_group(c0, ncol, rows=128):
            bank, bres = next_bank()

            def mm(e):
                for k in range(8):
                    ins = e.matmul(bank[0:rows, 0:ncol], lhsT=hT[:, k, 0:rows], rhs=w_in_bf[:, k, c0:c0 + ncol], start=(k == 0), stop=(k == 7))
                return ins
            S.op("pe", mm, reads=["hT", "w_in"], writes=[bres])
            return bank, bres

        def fm_group(c0):
            bank, bres = next_bank()

            def mm(e):
                for c in range(4):
                    for k in range(8):
                        ins = e.matmul(bank[:, c * 128:(c + 1) * 128], lhsT=w_in_bf[:, k, c0 + c * 128:c0 + (c + 1) * 128], rhs=hT[:, k, :], start=(k == 0), stop=(k == 7))
                return ins
            S.op("pe", mm, reads=["hT", "w_in"], writes=[bres])
            return bank, bres

        def b3(ap, h, n):
            return ap.unsqueeze(2).broadcast_to([128, h, n])

        def rotary(src, t, rows, rt):
            v3 = src.rearrange("p (h e) -> p h e", h=8)
            x1 = v3[0:rows, :, 0:8]; x2 = v3[0:rows, :, 8:16]
            cb = cosT[0:rows, t, :].unsqueeze(1).broadcast_to([rows, 8, 8])
            sn = sinT[0:rows, t, :].unsqueeze(1).broadcast_to([rows, 8, 8])

            def f(e):
                e.tensor_tensor(out=rt[0:rows, 0], in0=x1, in1=cb, op=ALU.mult)
                e.tensor_tensor(out=rt[0:rows, 1], in0=x2, in1=sn, op=ALU.mult)
                e.tensor_tensor(out=rt[0:rows, 2], in0=x2, in1=cb, op=ALU.mult)
                return e.tensor_tensor(out=rt[0:rows, 3], in0=x1, in1=sn, op=ALU.mult)
            return f, x1, x2

        def do_rotary(src, sres, t, rows, rt):
            f, x1, x2 = rotary(src, t, rows, rt)
            S.op("pool", f, reads=[sres, "cosT", "sinT"], writes=["rt"])
            S.op("pool", lambda e: e.tensor_tensor(out=x1, in0=rt[0:rows, 0], in1=rt[0:rows, 1], op=ALU.subtract), reads=["rt"], writes=[sres])
            S.op("pool", lambda e: e.tensor_tensor(out=x2, in0=rt[0:rows, 2], in1=rt[0:rows, 3], op=ALU.add), reads=["rt"], writes=[sres])

        def pre(t):
            b = t % 2
            sz = szs[b]; RSZ = "sz%d" % b
            dtx, dte, dtv, adt = dtbufs[b]
            RDX, RDE, RDV, RAD = ["%s%d" % (n_, b) for n_ in ("dtx", "dte", "dtv", "adt")]
            rms_to_hT(xt[b][:], "xt", ln_bc, "ln_bc", 128, h_bf, "h_bf", h_bf, hT, "hT")
            load_x(t + 1)
            yield
            for gi in range(4):
                bank, bres = fm_group(1024 + gi * 512)
                S.op("act", lambda e, bank=bank, gi=gi: e.copy(out=xbuf[:, gi * 4:(gi + 1) * 4, 3:131], in_=bank.rearrange("p (c t) -> p c t", c=4)),
                     reads=[bres], writes=["xbuf"])
                yield
            if t == NT - 1:
                for r_ in range(3):
                    dma(convp[r_:r_ + 1, :].rearrange("o (c p) -> p (o c)", p=128), xbuf[:, :, 128 + r_], ["xbuf"], ["convp"], "o_cv", allow_slow_non_contiguous=True)
            def wv_(tap):
                return cwb[:, tap * 16:(tap + 1) * 16].unsqueeze(2).broadcast_to([128, 16, 128])

            CSPLIT = 6

            def conv_range(eng, c0, c1, sfx):
                nch = c1 - c0
                a = acc[:, c0:c1, :]; a2 = acc2[:, c0:c1, :]
                ra = "acc" + sfx; ra2 = "acc2" + sfx

                def w_(tap):
                    return cwb[:, tap * 16 + c0:tap * 16 + c1].unsqueeze(2).broadcast_to([128, nch, 128])
                S.op(eng, lambda e: e.tensor_tensor(out=a, in0=xbuf[:, c0:c1, 0:128], in1=w_(0), op=ALU.mult), reads=["xbuf", "cwb"], writes=[ra])
                for tap in range(1, 4):
                    S.op(eng, lambda e, tap=tap: e.tensor_tensor(out=a2, in0=xbuf[:, c0:c1, tap:tap + 128], in1=w_(tap), op=ALU.mult), reads=["xbuf", "cwb", ra], writes=[ra2])
                    S.op(eng, lambda e: e.tensor_tensor(out=a, in0=a, in1=a2, op=ALU.add), reads=[ra, ra2], writes=[ra])
                S.op(eng, lambda e: e.tensor_tensor(out=a, in0=a, in1=cwb[:, 64 + c0:64 + c1].unsqueeze(2).broadcast_to([128, nch, 128]), op=ALU.add), reads=[ra, "cwb"], writes=[ra])
            conv_range("dve", 0, CSPLIT, "D")
            conv_range("pool", CSPLIT, 16, "P")

            yield
            bank, bres = tok_group(0, 512)
            S.op("act", lambda e, bank=bank: e.activation(out=sz[:, 0:512], in_=bank, func=A.Silu), reads=[bres], writes=[RSZ])
            yield
            bank, bres = tok_group(512, 512)
            S.op("act", lambda e, bank=bank: e.activation(out=sz[:, 512:1024], in_=bank, func=A.Silu), reads=[bres], writes=[RSZ])
            yield
            bank, bres = fm_group(5648)
            S.op("act", lambda e, bank=bank: e.activation(out=sgc, in_=bank, func=A.Silu), reads=[bres], writes=["sgc"])
            yield
            bank, bres = tok_group(4624, 512)
            S.op("act", lambda e, bank=bank: e.activation(out=sg_bf, in_=bank, func=A.Silu), reads=[bres], writes=["sg_bf"])
            dma(sg_scr[t * 128:(t + 1) * 128, :], sg_bf, ["sg_bf"], ["sg_scr"], "s_sg")
            yield
            def mmdt(e):
                for k in range(8):
                    ins = e.matmul(ps_cb[:, 0:16], lhsT=hT[:, k, :], rhs=w_in_bf[:, k, 3072:3088], start=(k == 0), stop=(k == 7))
                return ins
            S.op("pe", mmdt, reads=["hT", "w_in"], writes=["ps_cb"])
            S.op("dve", lambda e: e.tensor_tensor(out=dtx, in0=ps_cb[:, 0:16], in1=dtb_bc[:], op=ALU.add), reads=["ps_cb", "dtb_bc"], writes=[RDX])
            S.op("act", lambda e: e.activation(out=dte, in_=dtx, func=A.Exp), reads=[RDX], writes=[RDE])
            S.op("act", lambda e: e.activation(out=dtv, in_=dte, func=A.Ln, bias=1.0, scale=1.0), reads=[RDE], writes=[RDV])
            S.op("dve", lambda e: e.tensor_tensor(out=adt, in0=dtv, in1=a_bc[:], op=ALU.mult), reads=[RDV, "a_bc"], writes=[RAD])

            yield
            bank, bres = fm_group(5136)
            S.op("act", lambda e, bank=bank: e.copy(out=qcT[:], in_=bank.rearrange("p (h t) -> p h t", h=4)), reads=[bres], writes=["qcT"])
            yield
            bank, bres = tok_group(3088, 512)
            S.op("act", lambda e, bank=bank: e.copy(out=qf, in_=bank), reads=[bres], writes=["qf"])
            yield
            bank, bres = tok_group(3600, 512)
            S.op("act", lambda e, bank=bank: e.copy(out=kf, in_=bank), reads=[bres], writes=["kf"])
            yield
            bank, bres = tok_group(4112, 512)
            S.op("act", lambda e, bank=bank: e.copy(out=vf, in_=bank), reads=[bres], writes=["vf"])
            yield
            do_rotary(qf, "qf", t, 128, rt)
            do_rotary(kf, "kf", t, 128, rt)
            S.op("pool", lambda e: e.tensor_copy(out=q_bf, in_=qf), reads=["qf"], writes=["q_bf"])
            S.op("pool", lambda e: e.tensor_copy(out=k_bf, in_=kf), reads=["kf"], writes=["k_bf"])
            S.op("pool", lambda e: e.tensor_copy(out=v_aug[:, :, 0:64], in_=vf.rearrange("p (h c) -> p h c", h=8)), reads=["vf"], writes=["v_aug"])
            if t >= 16:
                dma(wk[(t - 16) * 128:(t - 15) * 128, :], kf, ["kf"], ["wk"], "o_wk")
                dma(wv[(t - 16) * 128:(t - 15) * 128, :], vf, ["vf"], ["wv"], "o_wv")
            dma(v_scr[t * 128:(t + 1) * 128, :], v_aug[:].rearrange("p h c -> p (h c)"), ["v_aug"], ["v_scr"], "s_v")

            yield
            def trqk(e):
                for c in range(4):
                    e.transpose(out=ps_tr[:, c * 128:(c + 1) * 128], in_=q_bf[:, c * 128:(c + 1) * 128], identity=identb[:])
                for c in range(4):
                    ins = e.transpose(out=ps_tr[:, (4 + c) * 128:(5 + c) * 128], in_=k_bf[:, c * 128:(c + 1) * 128], identity=identb[:])
                return ins
            S.op("pe", trqk, reads=["q_bf", "k_bf", "identb"], writes=["ps_tr"])
            S.op("act", lambda e: e.copy(out=qkT_sb[:], in_=ps_tr[:].rearrange("p (c t) -> p c t", c=8)), reads=["ps_tr"], writes=["catT_sb"])
            dma(qkT_scr[:, :, t * 128:(t + 1) * 128], qkT_sb[:], ["catT_sb"], ["qkT_scr"], "s_cat")

            yield

        def pre_sample():
            rms_to_hT(xt[0][0:NB, :], "xt", ln_bc, "ln_bc", NB, h_bf, "h_bf", h_bf, hT, "hT")
            yield
            stg = acc2[0:NB, 0:4, :].rearrange("p c t -> p (c t)")
            for gi in range(13):
                c0 = gi * 512
                ncol = min(512, NIN - c0)
                bank, bres = tok_group(c0, ncol, rows=NB)
                S.op("act", lambda e, bank=bank, ncol=ncol: e.copy(out=stg[:, 0:ncol], in_=bank[0:NB, 0:ncol]), reads=[bres], writes=["acc2D"])
                dma(us_scr[:, c0:c0 + ncol], stg[:, 0:ncol], ["acc2D"], ["us_scr"], "s_us")
                yield

        def cross(t):

            def mmcs_(e):
                for h in range(4):
                    for mt in range(2):
                        ins = e.matmul(ps_big[:, (h * 2 + mt) * 128:(h * 2 + mt + 1) * 128], lhsT=MKT[:, h, mt * 128:(mt + 1) * 128], rhs=qcT[:, h, :], start=True, stop=True)
                return ins
            S.op("pe", mmcs_, reads=["MKT", "qcT"], writes=["ps_big"])
            S.op("act", lambda e: e.activation(out=PT_bf[:], in_=ps_big.rearrange("p (h m t) -> p h m t", h=4, m=2), func=A.Exp, scale=128 ** -0.5),
                 reads=["ps_big"], writes=["E_bf"])

            def mmcv(e):
                for h in range(4):
                    for mt in range(2):
                        e.matmul(ps_y[:, h * 128:(h + 1) * 128], lhsT=MV[:, mt, h * 128:(h + 1) * 128], rhs=PT_bf[:, h, mt, :], start=(mt == 0), stop=(mt == 1))
                for mt in range(2):
                    ins = e.matmul(ps_y[:, 512:1024], lhsT=onesb[:], rhs=PT_bf[:, :, mt, :], start=(mt == 0), stop=(mt == 1))
                return ins
            S.op("pe", mmcv, reads=["MV", "E_bf", "onesb"], writes=["ps_y"])
            S.op("act", lambda e: e.activation(out=rden, in_=ps_y[:, 512:1024], func=A.Ln), reads=["ps_y"], writes=["rden"])
            S.op("act", lambda e: e.activation(out=rden, in_=rden, func=A.Exp, scale=-1.0), reads=["rden"], writes=["rden"])
            S.op("dve", lambda e: e.tensor_tensor(out=ocr, in0=ps_y[:, 0:512], in1=rden, op=ALU.mult), reads=["ps_y", "rden"], writes=["rden"])
            S.op("dve", lambda e: e.tensor_tensor(out=catTc[:].rearrange("p h t -> p (h t)"), in0=ocr, in1=sgc, op=ALU.mult), reads=["rden", "sgc"], writes=["catT_sb"])
            dma(cat_scr[t, :, 8:12, :], catTc, ["catT_sb"], ["cat_scr"], "s_cat")


        def ssd(t):
            b = t % 2
            sz = szs[b]; RSZ = "sz%d" % b
            dtx, dte, dtv, adt = dtbufs[b]
            RDV = "dtv%d" % b; RAD = "adt%d" % b
            S.op("act", lambda e: e.activation(out=xc_bf[:], in_=acc[:], func=A.Silu), reads=["accD", "accP"], writes=["xc_bf"])
            S.op("pool", lambda e: e.tensor_copy(out=xbuf[:, :, 0:3], in_=xbuf[:, :, 128:131]), reads=["xbuf"], writes=["xbuf"])
            S.op("dve", lambda e: e.tensor_tensor(out=rhsd[:], in0=tri[:].unsqueeze(1).broadcast_to([128, 16, 128]), in1=b3(adt, 16, 128), op=ALU.mult),
                 reads=["tri", RAD, "xc_bf"], writes=["accD", "accP"])
            def trx(e):
                for c in range(8):
                    ins = e.transpose(out=ps_tr[:, c * 128:(c + 1) * 128], in_=xc_bf[:, c, :], identity=identb[:])
                return ins
            S.op("pe", trx, reads=["xc_bf", "identb"], writes=["ps_tr"])
            S.op("act", lambda e: e.copy(out=xtok_bf, in_=ps_tr[:]), reads=["ps_tr"], writes=["xtok_bf"])
            S.op("pool", lambda e: e.tensor_tensor(out=yv.rearrange("p (h q) -> p h q", h=16), in0=xtok_bf.rearrange("p (h q) -> p h q", h=16), in1=b3(dsk_bc[:], 16, 64), op=ALU.mult),
                 reads=["xtok_bf", "dsk_bc"], writes=["yv"])
            S.op("dve", lambda e: e.tensor_tensor(out=xdt_bf.rearrange("p (h q) -> p h q", h=16), in0=xtok_bf.rearrange("p (h q) -> p h q", h=16), in1=b3(dtv, 16, 64), op=ALU.mult),
                 reads=["xtok_bf", RDV], writes=["xdt_bf"])

            def mmcs(e):
                e.matmul(ps_cb[:, 0:16], lhsT=tri[:], rhs=adt, start=True, stop=True)
                return e.matmul(ps_cb[:, 16:32], lhsT=onesf[:], rhs=adt, start=True, stop=True)
            S.op("pe", mmcs, reads=["tri", "onesf", RAD], writes=["ps_cb"])
            S.op("act", lambda e: e.copy(out=csb[:, 0:32], in_=ps_cb[:, 0:32]), reads=["ps_cb"], writes=["csb"])
            for half in range(2):
                pd = ps_big if half == 0 else ps_y
                pdn = "ps_big" if half == 0 else "ps_y"

                def mmd(e, half=half, pd=pd):
                    for j in range(2):
                        h0 = half * 8 + j * 4
                        ins = e.matmul(pd[:, j * 512:(j + 1) * 512], lhsT=ustr[:], rhs=rhsd[:, h0:h0 + 4, :], start=True, stop=True)
                    return ins
                S.op("pe", mmd, reads=["ustr", "accD", "accP"], writes=[pdn])
                S.op("act", lambda e, half=half, pd=pd: e.activation(out=E_bf[:, half * 8:(half + 1) * 8, :], in_=pd.rearrange("p (h l) -> p h l", h=8), func=A.Exp),
                     reads=[pdn], writes=["E_bf"])
            yield
            S.op("dve", lambda e: e.tensor_tensor(out=csb[:, 32:48], in0=csb[:, 16:32], in1=csb[:, 0:16], op=ALU.subtract), reads=["csb"], writes=["csb"])
            S.op("act", lambda e: e.activation(out=exs[:], in_=csb[:], func=A.Exp), reads=["csb"], writes=["exs"])
            ecs = exs[:, 0:16]; dec = exs[:, 16:32]; dsv = exs[:, 32:48]
            S.op("pool", lambda e: e.tensor_tensor(out=state.rearrange("p (h q) -> p h q", h=16), in0=state.rearrange("p (h q) -> p h q", h=16), in1=b3(dec, 16, 64), op=ALU.mult),
                 reads=["state", "exs"], writes=["state"])
            S.op("dve", lambda e: e.tensor_tensor(out=xds_bf.rearrange("p (h q) -> p h q", h=16), in0=xdt_bf.rearrange("p (h q) -> p h q", h=16), in1=b3(dsv, 16, 64), op=ALU.mult),
                 reads=["xdt_bf", "exs"], writes=["xtok_bf"])
            yield

            def trb(e):
                for g in range(4):
                    ins = e.transpose(out=ps_tr[:, g * 128:(g + 1) * 128], in_=xc_bf[:, 8 + g, :], identity=identb[:])
                return ins
            S.op("pe", trb, reads=["xc_bf", "identb"], writes=["ps_tr"])
            S.op("act", lambda e: e.copy(out=Btok_bf[:], in_=ps_tr[:, 0:512].rearrange("p (g n) -> p g n", g=4)), reads=["ps_tr"], writes=["Btok_bf"])

            def mmcb(e):
                for g in range(4):
                    ins = e.matmul(ps_cb[:, g * 128:(g + 1) * 128], lhsT=xc_bf[:, 8 + g, :], rhs=xc_bf[:, 12 + g, :], start=True, stop=True)
                return ins
            S.op("pe", mmcb, reads=["xc_bf"], writes=["ps_cb"])
            S.op("dve", lambda e: e.tensor_tensor(out=CBm[:], in0=ps_cb[:].rearrange("p (g l) -> p g l", g=4), in1=tri[:].unsqueeze(1).broadcast_to([128, 4, 128]), op=ALU.mult),
                 reads=["ps_cb", "tri"], writes=["CBm"])
            yield
            S.op("dve", lambda e: e.tensor_tensor(out=MT_bf[:].rearrange("p (g r) l -> p g r l", g=4), in0=E_bf[:].rearrange("p (g r) l -> p g r l", g=4),
                                                  in1=CBm[:].unsqueeze(2).broadcast_to([128, 4, 4, 128]), op=ALU.mult),
                 reads=["E_bf", "CBm"], writes=["E_bf"])
            yield

            def mmyd(e):
                for h in range(16):
                    ins = e.matmul(ps_y[:, h * 64:(h + 1) * 64], lhsT=MT_bf[:, h, :], rhs=xdt_bf[:, h * 64:(h + 1) * 64], start=True, stop=True)
                return ins
            S.op("pe", mmyd, reads=["E_bf", "xdt_bf"], writes=["ps_y"])

            def mmyo(e):
                for g in range(4):
                    ins = e.matmul(ps_big[:, g * 256:(g + 1) * 256], lhsT=xc_bf[:, 12 + g, :], rhs=state_bf[:, g * 256:(g + 1) * 256], start=True, stop=True)
                return ins
            S.op("pe", mmyo, reads=["xc_bf", "state_bf"], writes=["ps_big"])
            S.op("dve", lambda e: e.tensor_tensor(out=t1.rearrange("p (h q) -> p h q", h=16), in0=ps_big.rearrange("p (h q) -> p h q", h=16), in1=b3(ecs, 16, 64), op=ALU.mult),
                 reads=["ps_big", "exs"], writes=["t1"])
            S.op("dve", lambda e: e.tensor_tensor(out=yv, in0=yv, in1=t1, op=ALU.add), reads=["yv", "t1"], writes=["yv"])
            S.op("dve", lambda e: e.tensor_tensor(out=yv, in0=ps_y, in1=yv, op=ALU.add), reads=["ps_y", "yv"], writes=["yv"])
            S.op("dve", lambda e: e.tensor_tensor(out=yv, in0=yv, in1=sz, op=ALU.mult), reads=["yv", RSZ], writes=["yv"])
            yield
            def mmst(e):
                for g in range(4):
                    ins = e.matmul(ps_y[:, g * 256:(g + 1) * 256], lhsT=Btok_bf[:, g, :], rhs=xds_bf[:, g * 256:(g + 1) * 256], start=True, stop=True)
                return ins
            S.op("pe", mmst, reads=["Btok_bf", "xtok_bf"], writes=["ps_y"])
            S.op("dve", lambda e: e.tensor_tensor(out=state, in0=ps_y, in1=state, op=ALU.add), reads=["ps_y", "state"], writes=["state"])
            S.op("act", lambda e: e.copy(out=state_bf, in_=state), reads=["state"], writes=["state_bf"])
            yield

            for g in range(4):
                S.op("act", lambda e, g=g: e.activation(out=junk[:, 0:256], in_=yv[:, g * 256:(g + 1) * 256], func=A.Square, accum_out=gss[:, g:g + 1]),
                     reads=["yv"], writes=["t1", "gss"])
            S.op("dve", lambda e: e.tensor_scalar(out=gvar, in0=gss, scalar1=1.0 / 256, scalar2=EPS, op0=ALU.mult, op1=ALU.add), reads=["gss"], writes=["gvar"])
            S.op("act", lambda e: e.activation(out=gvar, in_=gvar, func=A.Ln), reads=["gvar"], writes=["gvar"])
            S.op("act", lambda e: e.activation(out=grstd, in_=gvar, func=A.Exp, scale=-0.5), reads=["gvar"], writes=["grstd"])
            S.op("dve", lambda e: e.tensor_tensor(out=yv.rearrange("p (g c) -> p g c", g=4), in0=yv.rearrange("p (g c) -> p g c", g=4), in1=b3(grstd, 4, 256), op=ALU.mult),
                 reads=["yv", "grstd"], writes=["yv"])
            S.op("dve", lambda e: e.tensor_tensor(out=yn_bf, in0=yv, in1=sng_bc[:], op=ALU.mult), reads=["yv", "sng_bc"], writes=["xtok_bf"])
            yield

            def try_(e):
                for c in range(8):
                    ins = e.transpose(out=ps_tr[:, c * 128:(c + 1) * 128], in_=yn_bf[:, c * 128:(c + 1) * 128], identity=identb[:])
                return ins
            S.op("pe", try_, reads=["xtok_bf", "identb"], writes=["ps_tr"])
            S.op("act", lambda e: e.copy(out=catT_sb[:], in_=ps_tr[:].rearrange("p (c t) -> p c t", c=8)), reads=["ps_tr"], writes=["catT_sb"])
            dma(cat_scr[t, :, 0:8, :], catT_sb[:], ["catT_sb"], ["cat_scr"], "s_cat")

            yield

        def drain(g):
            for _ in g:
                pass

        def interleave(g1, g2):
            live = [g1, g2]
            while live:
                for g in list(live):
                    try:
                        next(g)
                    except StopIteration:
                        live.remove(g)

        load_x(0)
        drain(pre(0))
        cross(0)
        for t in range(NT):
            g1 = ssd(t)
            next(g1)
            interleave(g1, pre(t + 1) if t + 1 < NT else pre_sample())
            if t + 1 < NT:
                cross(t + 1)

        stT = yv

        def trst(e):
            for c in range(8):
                ins = e.matmul(ps_big[:, c * 128:(c + 1) * 128], lhsT=state[:, c * 128:(c + 1) * 128], rhs=identf[:], start=True, stop=True)
            return ins
        S.op("pe", trst, reads=["state", "identf"], writes=["ps_big"])
        S.op("act", lambda e: e.copy(out=stT, in_=ps_big), reads=["ps_big"], writes=["yv"])
        dma(ssmp.rearrange("(c p) n -> p c n", p=128), stT.rearrange("p (c n) -> p c n", c=8), ["yv"], ["ssmp"], "o_ssm")

        if UPTO >= 'S':
            S.full_barrier()
            off[0] = 0
            a1f = arena1[:].bitcast(F32)
            usb = carve(NIN)
            cat_s = carve(2048)
            dz0 = off[0]
            acc_s = carve(2048); tmp_s = carve(2048); xc_s = carve(2048)
            pk = carve(1024)
            y_s = carve(1024); szs = carve(1024)
            prod = arena2[:, dz0:dz0 + 8192]
            dss = carve(64)
            rt_s = carve(256).rearrange("p (a h i) -> p a h i", a=4, h=8)
            xdt_q = carve(128); B_q = carve(128); C_q = carve(128); dA_q = carve(2); y_q = carve(128)
            h0s = [a1f[:, 0:4096], a1f[:, 4096:8192]]
            tq = a1f[:, 8192:12288]
            o_tok = carve(512); sga = carve(512)
            numa = carve(512); numc = carve(512)
            sm = carve(32)
            snew = sm[0:NB, 0:8]; pnew = sm[0:NB, 8:16]; dena = sm[0:NB, 16:24]; denc = sm[0:NB, 24:28]
            sj = carve(128); pj = carve(128); pj_bf = carve_bf(128)
            esel_f = carve(256); Esel = carve_bf(256).rearrange("p (c m) -> p c m", c=16)
            print("arena2 phase S words:", off[0])
            R = NB
            dma(usb[0:R, :], us_scr, ["us_scr"], ["usb"], "l_s")
            cbuf = a1f[0:R, 0:8192].rearrange("p (t c) -> p t c", t=4)
            cw16 = a1f[0:R, 8192:16384].rearrange("p (t c) -> p t c", t=4)
            cb16 = a1f[0:R, 16384:18432]
            dma(cbuf[:, 0:3, :], sconv, [], ["cbuf"], "l_s")
            dma(cbuf[:, 3, :], us_scr[:, 1024:3072], ["us_scr"], ["cbuf"], "l_s")
            for tap in range(4):
                dma(cw16[:, tap, :], conv_w[tap:tap + 1, :].partition_broadcast(R), [], ["cw16"], "l_s")
            dma(cb16, conv_b.partition_broadcast(R), [], ["cb16"], "l_s")
            dma(convs, cbuf[:, 1:4, :], ["cbuf"], ["convs"], "o_cvs")
            a_s = acc_s[0:R, :]; t_s = tmp_s[0:R, :]
            S.op("dve", lambda e: e.tensor_tensor(out=a_s, in0=cbuf[:, 0, :], in1=cw16[:, 0, :], op=ALU.mult), reads=["cbuf", "cw16"], writes=["acc_s"])
            for tap in range(1, 4):
                S.op("pool", lambda e, tap=tap: e.tensor_tensor(out=t_s, in0=cbuf[:, tap, :], in1=cw16[:, tap, :], op=ALU.mult), reads=["cbuf", "cw16", "acc_s"], writes=["tmp_s"])
                S.op("dve", lambda e: e.tensor_tensor(out=a_s, in0=a_s, in1=t_s, op=ALU.add), reads=["acc_s", "tmp_s"], writes=["acc_s"])
            S.op("dve", lambda e: e.tensor_tensor(out=a_s, in0=a_s, in1=cb16, op=ALU.add), reads=["acc_s", "cb16"], writes=["acc_s"])
            S.op("act", lambda e: e.activation(out=xc_s[0:R, :], in_=a_s, func=A.Silu), reads=["acc_s"], writes=["xc_s"])
            dtx_s = dss[0:R, 0:16]; dte_s = dss[0:R, 16:32]; dt_s = dss[0:R, 32:48]; dA_s = dss[0:R, 48:64]
            S.op("dve", lambda e: e.tensor_tensor(out=dtx_s, in0=usb[0:R, 3072:3088], in1=dtb_bc[0:R, :], op=ALU.add), reads=["usb", "dtb_bc"], writes=["dtx_s"])
            S.op("act", lambda e: e.activation(out=dte_s, in_=dtx_s, func=A.Exp), reads=["dtx_s"], writes=["dte_s"])
            S.op("act", lambda e: e.activation(out=dt_s, in_=dte_s, func=A.Ln, bias=1.0, scale=1.0), reads=["dte_s"], writes=["dt_s"])
            S.op("dve", lambda e: e.tensor_tensor(out=dA_s, in0=dt_s, in1=a_bc[0:R, :], op=ALU.mult), reads=["dt_s", "a_bc"], writes=["dA_s"])
            S.op("act", lambda e: e.activation(out=dA_s, in_=dA_s, func=A.Exp), reads=["dA_s"], writes=["dA_s"])
            S.op("dve", lambda e: e.tensor_tensor(out=pk[0:R, :].rearrange("p (h q) -> p h q", h=16), in0=xc_s[0:R, 0:1024].rearrange("p (h q) -> p h q", h=16),
                                                  in1=dt_s.unsqueeze(2).broadcast_to([R, 16, 64]), op=ALU.mult), reads=["xc_s", "dt_s"], writes=["pk"])
            dma(sx_scr, pk[0:R, :], ["pk"], ["sx_scr"], "s_s")
            S.op("pool", lambda e: e.tensor_copy(out=pk[0:R, :].rearrange("p (g d n) -> p g d n", g=4, d=2),
                                                 in_=xc_s[0:R, 1024:1536].rearrange("p (g n) -> p g n", g=4).unsqueeze(2).broadcast_to([R, 4, 2, 128])), reads=["xc_s", "pk"], writes=["pk"])
            dma(sB_scr, pk[0:R, :], ["pk"], ["sB_scr"], "s_s")
            S.op("pool", lambda e: e.tensor_copy(out=pk[0:R, :].rearrange("p (g d n) -> p g d n", g=4, d=2),
                                                 in_=xc_s[0:R, 1536:2048].rearrange("p (g n) -> p g n", g=4).unsqueeze(2).broadcast_to([R, 4, 2, 128])), reads=["xc_s", "pk"], writes=["pk"])
            dma(sC_scr, pk[0:R, :], ["pk"], ["sC_scr"], "s_s")
            dma(sdA_scr, dA_s, ["dA_s"], ["sdA_scr"], "s_s")
            dma(xdt_q, sx_scr.rearrange("b (j f) -> (b j) f", j=8), ["sx_scr"], ["xdt_q"], "l_s2")
            dma(B_q, sB_scr.rearrange("b (j f) -> (b j) f", j=8), ["sB_scr"], ["B_q"], "l_s2")
            dma(C_q, sC_scr.rearrange("b (j f) -> (b j) f", j=8), ["sC_scr"], ["C_q"], "l_s2")
            dma(dA_q, sdA_scr.rearrange("b (j f) -> (b j) f", j=8), ["sdA_scr"], ["dA_q"], "l_s2")
            S.full_barrier()
            sssm3 = sssm.rearrange("(q f) n -> q f n", f=128)
            ssms3 = ssms.rearrange("(q f) n -> q f n", f=128)
            for pc in range(4):
                f0 = pc * 32
                hb = h0s[pc % 2].rearrange("p (f n) -> p f n", f=32); hres = "h0_%d" % (pc % 2)
                tq3 = tq.rearrange("p (f n) -> p f n", f=32)
                dma(hb, sssm3[:, f0:f0 + 32, :], [], [hres], "l_h%d" % (pc % 2))
                S.op("dve", lambda e, f0=f0: e.tensor_tensor(out=tq3, in0=xdt_q[:, f0:f0 + 32].unsqueeze(2).broadcast_to([128, 32, 128]),
                                                              in1=B_q.unsqueeze(1).broadcast_to([128, 32, 128]), op=ALU.mult), reads=["xdt_q", "B_q"], writes=["tq"])
                S.op("dve", lambda e, hb=hb, pc=pc: e.scalar_tensor_tensor(out=hb, in0=hb, scalar=dA_q[:, pc // 2:pc // 2 + 1], in1=tq3, op0=ALU.mult, op1=ALU.add),
                     reads=[hres, "dA_q", "tq"], writes=[hres])
                dma(ssms3[:, f0:f0 + 32, :], hb, [hres], ["ssms"], "o_h%d" % (pc % 2))
                S.op("pool", lambda e, hb=hb: e.tensor_tensor(out=tq3, in0=hb, in1=C_q.unsqueeze(1).broadcast_to([128, 32, 128]), op=ALU.mult), reads=[hres, "C_q", "tq"], writes=["tq"])
                S.op("dve", lambda e, f0=f0: e.reduce_sum(out=y_q[:, f0:f0 + 32], in_=tq3, axis=AX.X), reads=["tq"], writes=["y_q"])
            dma(sx_scr.rearrange("b (j f) -> (b j) f", j=8), y_q, ["y_q", "xdt_q"], ["sx_scr"], "s_s")
            dma(y_s[0:R, :], sx_scr, ["sx_scr"], ["y_s"], "l_s2")
            S.op("pool", lambda e: e.tensor_tensor(out=pk[0:R, :].rearrange("p (h q) -> p h q", h=16), in0=xc_s[0:R, 0:1024].rearrange("p (h q) -> p h q", h=16),
                                                   in1=dsk_bc[0:R, :].unsqueeze(2).broadcast_to([R, 16, 64]), op=ALU.mult), reads=["xc_s", "dsk_bc", "pk"], writes=["pk"])
            S.op("dve", lambda e: e.tensor_tensor(out=y_s[0:R, :], in0=y_s[0:R, :], in1=pk[0:R, :], op=ALU.add), reads=["y_s", "pk"], writes=["y_s"])
            S.op("act", lambda e: e.activation(out=szs[0:R, :], in_=usb[0:R, 0:1024], func=A.Silu), reads=["usb"], writes=["szs"])
            S.op("dve", lambda e: e.tensor_tensor(out=y_s[0:R, :], in0=y_s[0:R, :], in1=szs[0:R, :], op=ALU.mult), reads=["y_s", "szs"], writes=["y_s"])
            for g in range(4):
                S.op("act", lambda e, g=g: e.activation(out=pk[0:R, 0:256], in_=y_s[0:R, g * 256:(g + 1) * 256], func=A.Square, accum_out=gss[0:R, g:g + 1]),
                     reads=["y_s", "pk"], writes=["pk", "gss"])
            S.op("dve", lambda e: e.tensor_scalar(out=gvar[0:R, :], in0=gss[0:R, :], scalar1=1.0 / 256, scalar2=EPS, op0=ALU.mult, op1=ALU.add), reads=["gss"], writes=["gvar"])
            S.op("act", lambda e: e.activation(out=gvar[0:R, :], in_=gvar[0:R, :], func=A.Sqrt), reads=["gvar"], writes=["gvar"])
            S.op("dve", lambda e: e.reciprocal(out=grstd[0:R, :], in_=gvar[0:R, :]), reads=["gvar"], writes=["grstd"])
            S.op("dve", lambda e: e.tensor_tensor(out=y_s[0:R, :].rearrange("p (g c) -> p g c", g=4), in0=y_s[0:R, :].rearrange("p (g c) -> p g c", g=4),
                                                  in1=grstd[0:R, :].unsqueeze(2).broadcast_to([R, 4, 256]), op=ALU.mult), reads=["y_s", "grstd"], writes=["y_s"])
            S.op("dve", lambda e: e.tensor_tensor(out=cat_s[0:R, 0:1024], in0=y_s[0:R, :], in1=sng_bc[0:R, :], op=ALU.mult), reads=["y_s", "sng_bc"], writes=["cat_s"])

            S.full_barrier()
            qs = usb[0:R, 3088:3600]; ks = usb[0:R, 3600:4112]; vs = usb[0:R, 4112:4624]
            do_rotary(usb[:, 3088:3600], "usb", NT, R, rt_s)
            do_rotary(usb[:, 3600:4112], "usb", NT, R, rt_s)
            dma(wks, ks, ["usb"], ["wks"], "o_wks")
            dma(wvs, vs, ["usb"], ["wvs"], "o_wks")
            dma(sq_scr, qs, ["usb"], ["sq_scr"], "s_s")
            dma(sk_scr, usb[0:R, 5136:5648], ["usb"], ["sk_scr"], "s_s")
            Kj = a1f[:, 0:8192]; Vj = a1f[:, 8192:16384]; qbc = a1f[:, 16384:24576]
            prod2A = prod[:, 0:2816].bitcast(BF16)
            prod2B = prod[:, 5632:6912].bitcast(BF16)

            def p2(b_):
                return prod2A[:, b_ * 512:(b_ + 1) * 512] if b_ < 11 else prod2B[:, (b_ - 11) * 512:(b_ - 10) * 512]
            dma(esel_f, c_esel.partition_broadcast(128), [], ["esel_f"], "l_s2")
            S.op("dve", lambda e: e.tensor_copy(out=Esel.rearrange("p c m -> p (c m)"), in_=esel_f), reads=["esel_f"], writes=["Esel"])
            S.op("dve", lambda e: e.tensor_tensor(out=o_tok[0:R, :], in0=qs, in1=ks, op=ALU.mult), reads=["usb"], writes=["o_tok"])
            S.op("dve", lambda e: e.reduce_sum(out=snew, in_=o_tok[0:R, :].rearrange("p (h e) -> p h e", h=8), axis=AX.X), reads=["o_tok"], writes=["snew"])
            S.op("act", lambda e: e.activation(out=pnew, in_=snew, func=A.Exp, scale=0.125), reads=["snew"], writes=["pnew"])
            S.op("dve", lambda e: e.tensor_scalar(out=pnew, in0=pnew, scalar1=3.0, scalar2=None, op0=ALU.mult), reads=["pnew"], writes=["pnew"])
            S.op("dve", lambda e: e.tensor_tensor(out=numa[0:R, :].rearrange("p (h e) -> p h e", h=8), in0=vs.rearrange("p (h e) -> p h e", h=8),
                                                  in1=pnew.unsqueeze(2).broadcast_to([R, 8, 64]), op=ALU.mult), reads=["usb", "pnew"], writes=["numa"])
            S.op("dve", lambda e: e.tensor_copy(out=dena, in_=pnew), reads=["pnew"], writes=["dena"])

            for pi, d in enumerate(DILS):
                rs = slice(2048 - 128 * d, 2048 - d + 1, d)
                if pi == 0:
                    dma(qbc, sq_scr.rearrange("(o b) c -> o (b c)", o=1).partition_broadcast(128), ["sq_scr"], ["qbc"], "l_q")
                dma(Kj.rearrange("p (b c) -> p b c", b=R), cwk[:, rs, :].rearrange("b j c -> j b c"), [], ["KjA", "KjB"], "l_kg")
                dma(Vj.rearrange("p (b c) -> p b c", b=R), cwv[:, rs, :].rearrange("b j c -> j b c"), [], ["Vj"], "l_vg")
                S.op("dve", lambda e: e.tensor_tensor(out=prod[:, 0:5632], in0=Kj[:, 0:5632], in1=qbc[:, 0:5632], op=ALU.mult), reads=["KjA", "qbc"], writes=["prodA"])
                S.op("pool", lambda e: e.tensor_tensor(out=prod[:, 5632:8192], in0=Kj[:, 5632:8192], in1=qbc[:, 5632:8192], op=ALU.mult), reads=["KjB", "qbc"], writes=["prodB"])
                S.op("dve", lambda e: e.reduce_sum(out=sj, in_=prod.rearrange("p (g e) -> p g e", e=64), axis=AX.X), reads=["prodA", "prodB"], writes=["sj"])
                S.op("act", lambda e: e.activation(out=pj, in_=sj, func=A.Exp, scale=0.125), reads=["sj"], writes=["pj"])
                S.op("act", lambda e: e.copy(out=pj_bf, in_=pj), reads=["pj"], writes=["pj_bf"])
                S.op("dve", lambda e: e.tensor_tensor(out=prod2A.rearrange("p (g e) -> p g e", e=64), in0=Vj[:, 0:5632].rearrange("p (g e) -> p g e", e=64),
                                                      in1=pj[:, 0:88].unsqueeze(2).broadcast_to([128, 88, 64]), op=ALU.mult), reads=["Vj", "pj", "sj"], writes=["prodA"])
                S.op("pool", lambda e: e.tensor_tensor(out=prod2B.rearrange("p (g e) -> p g e", e=64), in0=Vj[:, 5632:8192].rearrange("p (g e) -> p g e", e=64),
                                                       in1=pj[:, 88:128].unsqueeze(2).broadcast_to([128, 40, 64]), op=ALU.mult), reads=["Vj", "pj", "sj"], writes=["prodB"])

                def mmn(e):
                    for b_ in range(R):
                        ins = e.matmul(ps_in[0:R, 0:512], lhsT=Esel[:, b_, :], rhs=p2(b_), start=(b_ == 0), stop=(b_ == R - 1))
                    return ins
                S.op("pe", mmn, reads=["prodA", "prodB", "Esel"], writes=["ps_in0"])

                def mmd_(e):
                    for b_ in range(R):
                        ins = e.matmul(ps_cb[0:R, 0:8], lhsT=Esel[:, b_, :], rhs=pj_bf[:, b_ * 8:(b_ + 1) * 8], start=(b_ == 0), stop=(b_ == R - 1))
                    return ins
                S.op("pe", mmd_, reads=["pj_bf", "Esel"], writes=["ps_cb"])
                S.op("dve", lambda e: e.tensor_tensor(out=numa[0:R, :], in0=ps_in[0:R, 0:512], in1=numa[0:R, :], op=ALU.add), reads=["ps_in0", "numa"], writes=["numa"])
                S.op("dve", lambda e: e.tensor_tensor(out=dena, in0=ps_cb[0:R, 0:8], in1=dena, op=ALU.add), reads=["ps_cb", "dena"], writes=["dena"])
            S.op("dve", lambda e: e.reciprocal(out=dena, in_=dena), reads=["dena"], writes=["dena"])
            S.op("dve", lambda e: e.tensor_tensor(out=numa[0:R, :].rearrange("p (h e) -> p h e", h=8), in0=numa[0:R, :].rearrange("p (h e) -> p h e", h=8),
                                                  in1=dena.unsqueeze(2).broadcast_to([R, 8, 64]), op=ALU.mult), reads=["numa", "dena"], writes=["numa"])
            S.op("act", lambda e: e.activation(out=sga[0:R, :], in_=usb[0:R, 4624:5136], func=A.Silu), reads=["usb"], writes=["sga"])
            S.op("dve", lambda e: e.tensor_tensor(out=cat_s[0:R, 1024:1536], in0=numa[0:R, :], in1=sga[0:R, :], op=ALU.mult), reads=["numa", "sga"], writes=["cat_s"])

            dma(qbc, sk_scr.rearrange("(o b) c -> o (b c)", o=1).partition_broadcast(128), ["sk_scr", "prodA", "prodB"], ["qbc"], "l_q")
            for mt in range(2):
                dma(Kj.rearrange("p (b c) -> p b c", b=R), cmk[:, mt * 128:(mt + 1) * 128, :].rearrange("b m c -> m b c"), [], ["KjA", "KjB"], "l_kg")
                dma(Vj.rearrange("p (b c) -> p b c", b=R), cmv[:, mt * 128:(mt + 1) * 128, :].rearrange("b m c -> m b c"), [], ["Vj"], "l_vg")
                S.op("dve", lambda e: e.tensor_tensor(out=prod[:, 0:5632], in0=Kj[:, 0:5632], in1=qbc[:, 0:5632], op=ALU.mult), reads=["KjA", "qbc"], writes=["prodA"])
                S.op("pool", lambda e: e.tensor_tensor(out=prod[:, 5632:8192], in0=Kj[:, 5632:8192], in1=qbc[:, 5632:8192], op=ALU.mult), reads=["KjB", "qbc"], writes=["prodB"])
                S.op("dve", lambda e: e.reduce_sum(out=sj[:, 0:64], in_=prod.rearrange("p (g e) -> p g e", e=128), axis=AX.X), reads=["prodA", "prodB"], writes=["sj"])
                S.op("act", lambda e: e.activation(out=pj[:, 0:64], in_=sj[:, 0:64], func=A.Exp, scale=128 ** -0.5), reads=["sj"], writes=["pj"])
                S.op("act", lambda e: e.copy(out=pj_bf[:, 0:64], in_=pj[:, 0:64]), reads=["pj"], writes=["pj_bf"])
                S.op("dve", lambda e: e.tensor_tensor(out=prod2A.rearrange("p (g e) -> p g e", e=128), in0=Vj[:, 0:5632].rearrange("p (g e) -> p g e", e=128),
                                                      in1=pj[:, 0:44].unsqueeze(2).broadcast_to([128, 44, 128]), op=ALU.mult), reads=["Vj", "pj", "sj"], writes=["prodA"])
                S.op("pool", lambda e: e.tensor_tensor(out=prod2B.rearrange("p (g e) -> p g e", e=128), in0=Vj[:, 5632:8192].rearrange("p (g e) -> p g e", e=128),
                                                       in1=pj[:, 44:64].unsqueeze(2).broadcast_to([128, 20, 128]), op=ALU.mult), reads=["Vj", "pj", "sj"], writes=["prodB"])

                def mmnc(e):
                    for b_ in range(R):
                        ins = e.matmul(ps_in[0:R, 512:1024], lhsT=Esel[:, b_, :], rhs=p2(b_), start=(b_ == 0), stop=(b_ == R - 1))
                    return ins
                S.op("pe", mmnc, reads=["prodA", "prodB", "Esel"], writes=["ps_in1"])

                def mmdc(e):
                    for b_ in range(R):
                        ins = e.matmul(ps_cb[0:R, 8:12], lhsT=Esel[:, b_, :], rhs=pj_bf[:, b_ * 4:(b_ + 1) * 4], start=(b_ == 0), stop=(b_ == R - 1))
                    return ins
                S.op("pe", mmdc, reads=["pj_bf", "Esel"], writes=["ps_cb"])
                if mt == 0:
                    S.op("dve", lambda e: e.tensor_copy(out=numc[0:R, :], in_=ps_in[0:R, 512:1024]), reads=["ps_in1"], writes=["numc"])
                    S.op("dve", lambda e: e.tensor_copy(out=denc, in_=ps_cb[0:R, 8:12]), reads=["ps_cb"], writes=["denc"])
                else:
                    S.op("dve", lambda e: e.tensor_tensor(out=numc[0:R, :], in0=ps_in[0:R, 512:1024], in1=numc[0:R, :], op=ALU.add), reads=["ps_in1", "numc"], writes=["numc"])
                    S.op("dve", lambda e: e.tensor_tensor(out=denc, in0=ps_cb[0:R, 8:12], in1=denc, op=ALU.add), reads=["ps_cb", "denc"], writes=["denc"])
            S.op("dve", lambda e: e.reciprocal(out=denc, in_=denc), reads=["denc"], writes=["denc"])
            S.op("dve", lambda e: e.tensor_tensor(out=numc[0:R, :].rearrange("p (h e) -> p h e", h=4), in0=numc[0:R, :].rearrange("p (h e) -> p h e", h=4),
                                                  in1=denc.unsqueeze(2).broadcast_to([R, 4, 128]), op=ALU.mult), reads=["numc", "denc"], writes=["numc"])
            S.op("act", lambda e: e.activation(out=sga[0:R, :], in_=usb[0:R, 5648:6160], func=A.Silu), reads=["usb", "cat_s"], writes=["sga"])
            S.op("dve", lambda e: e.tensor_tensor(out=cat_s[0:R, 1536:2048], in0=numc[0:R, :], in1=sga[0:R, :], op=ALU.mult), reads=["numc", "sga"], writes=["cat_s"])
            dma(cat_s_scr, cat_s[0:R, :], ["cat_s"], ["cat_s_scr"], "s_s")

        if UPTO >= 'B':
            S.full_barrier()
            off[0] = 0
            QKT = carve_bf(8 * SEQ).rearrange("p (c t) -> p c t", c=8)
            P_bfs = [carve_bf(2048).rearrange("p (h j q) -> p h j q", h=8, j=2) for _ in range(2)]
            o_sbs = [carve(520).rearrange("p (h c) -> p h c", h=8) for _ in range(2)]
            for c_ in range(8):
                dma(QKT[:, c_, :], qkT_scr[:, c_, :], ["qkT_scr"], ["QKT"], "l_qk")

            def qsel(pb, c, n_, d_, r_):
                s0 = 128 * d_ * n_ + r_
                return QKT[pb:pb + 64, c, s0:s0 + 127 * d_ + 1:d_]
            for di, d in enumerate(DILS):
                nb = 32 // d
                for r in range(d):
                    src = v_scr[r:SEQ - d + r + 1:d, :].rearrange("(n p) c -> p n c", p=128)
                    dma(V3[:, di, r * nb:(r + 1) * nb, :], src, ["v_scr"], ["V3_%d" % di], "l_v", eng=("sp" if r % 2 == 0 else "act"))
            blk = 0
            for di, d in enumerate(DILS):
                nb = 32 // d
                for r in range(d):
                    for n in range(nb):
                        st = 128 * d * n + r
                        kts = [n - 1, n] if n > 0 else [n]
                        nk = len(kts)
                        P_bf = P_bfs[blk % 2]; pres = "P%d" % (blk % 2)
                        o_sb = o_sbs[blk % 2]; ores = "o%d" % (blk % 2)
                        blk += 1

                        b2 = (blk - 1) % 2
                        ps4v = ps_4.rearrange("p (h j q) -> p h j q", h=8, j=2)
                        for par in range(2):
                            def mms(e, kts=kts, d=d, r=r, n=n, par=par):
                                for c in range(4):
                                    hh = par * 4 + c
                                    for j, kn in enumerate(kts):
                                        ins = e.matmul(ps_4[:, (hh * 2 + j) * 128:(hh * 2 + j + 1) * 128],
                                                       lhsT=qsel(par * 64, 4 + c, kn, d, r),
                                                       rhs=qsel(par * 64, c, n, d, r), start=True, stop=True)
                                return ins
                            S.op("pe", mms, reads=["QKT"], writes=["ps4_%d" % par])
                        for par in range(2):
                            S.op("act", lambda e, P_bf=P_bf, nk=nk, par=par: e.activation(out=P_bf[:, par * 4:(par + 1) * 4, 0:nk, :], in_=ps4v[:, par * 4:(par + 1) * 4, 0:nk, :], func=A.Exp, scale=0.125),
                                 reads=["ps4_%d" % par], writes=["P%d_%d" % (b2, par)])
                        if nk == 2:
                            mk_ = mask2[:].unsqueeze(1).broadcast_to([128, 4, 2, 128])
                        else:
                            mk_ = mask2[:, 1:2, :].unsqueeze(1).broadcast_to([128, 4, 1, 128])
                        for par in range(2):
                            S.op("dve", lambda e, P_bf=P_bf, nk=nk, mk_=mk_, par=par: e.tensor_tensor(out=P_bf[:, par * 4:(par + 1) * 4, 0:nk, :], in0=P_bf[:, par * 4:(par + 1) * 4, 0:nk, :], in1=mk_, op=ALU.mult),
                                 reads=["P%d_%d" % (b2, par), "mask2a", "mask2b"], writes=["P%d_%d" % (b2, par)])
                        for par in range(2):
                            def mmo(e, P_bf=P_bf, kts=kts, di=di, r=r, nb=nb, nk=nk, par=par):
                                for c in range(4):
                                    h = 2 * c + par
                                    for j, kn in enumerate(kts):
                                        ins = e.matmul(ps_in[:, par * 512 + c * 128:par * 512 + c * 128 + 65], lhsT=P_bf[:, par * 4 + c, j, :],
                                                       rhs=V3[:, di, r * nb + kn, h * 66:h * 66 + 65], start=(j == 0), stop=(j == nk - 1))
                                return ins
                            S.op("pe", mmo, reads=["P%d_%d" % (b2, par), "V3_%d" % di], writes=["ps_in%d" % par])
                        o_v = o_sb[:].rearrange("p (c q) k -> p c q k", q=2)
                        for par in range(2):
                            S.op("dve", lambda e, o_v=o_v, par=par: e.tensor_copy(out=o_v[:, :, par, :], in_=ps_in[:, par * 512:(par + 1) * 512].rearrange("p (c k) -> p c k", c=4)[:, :, 0:65]),
                                 reads=["ps_in%d" % par], writes=[ores])
                        dma(o_scr[di, st:st + 127 * d + 1:d, :], o_sb[:].rearrange("p h c -> p (h c)"), [ores], ["o_scr"], "s_o%d" % b2)

        if UPTO >= 'C':
            S.full_barrier()
            off[0] = 0
            xt2 = [carve(1024), carve(1024)]
            o3s = [carve(3 * 520).rearrange("p (d c) -> p d c", d=3) for _ in range(2)]
            sg_ts = [carve_bf(512) for _ in range(2)]
            catTs = [carve_bf(16 * 128).rearrange("p (c t) -> p c t", c=16) for _ in range(2)]
            osums = [carve(520).rearrange("p (h c) -> p h c", h=8) for _ in range(2)]
            rdns = [carve(8) for _ in range(2)]
            oatts = [carve(512) for _ in range(2)]
            oattbfs = [carve_bf(512) for _ in range(2)]
            hps = [carve(1024) for _ in range(2)]
            youts = [carve(1024) for _ in range(2)]
            wst = carve(2048).rearrange("p (s n) -> p s n", s=2)
            cat_s_bf = carve_bf(2048)
            statC = carve(8)
            pso = [ps_in[:, :], ps_4[:, 0:1024]]
            fng_bc = ln_bc
            dma(fng_bc[:], final_norm_g.partition_broadcast(128), [], ["ln_bc"], "c0")
            for j in range(16):
                sidx = j % 2
                dma(wst[:, sidx, :], w_out[j * 128:(j + 1) * 128, :], [], ["wst%d" % sidx], "stg%d" % sidx)
                cast(conv_engs[j % 3], w_out_bf[:, j, :], wst[:, sidx, :], ["wst%d" % sidx], ["w_out"])

            def final_norm_out(b, rows, out_ap, okey):
                hp_ap = hps[b][0:rows, :]; yo = youts[b][0:rows, :]
                sq = statC[0:rows, 3 * b:3 * b + 1]; vr = statC[0:rows, 3 * b + 1:3 * b + 2]; rs_ = statC[0:rows, 3 * b + 2:3 * b + 3]
                S.op("act", lambda e: e.activation(out=yo, in_=hp_ap, func=A.Square, accum_out=sq), reads=["hp%d" % b], writes=["yout%d" % b, "sq%d" % b])
                S.op("dve", lambda e: e.tensor_scalar(out=vr, in0=sq, scalar1=1.0 / D, scalar2=EPS, op0=ALU.mult, op1=ALU.add), reads=["sq%d" % b], writes=["vr%d" % b])
                S.op("act", lambda e: e.activation(out=vr, in_=vr, func=A.Sqrt), reads=["vr%d" % b], writes=["vr%d" % b])
                S.op("dve", lambda e: e.reciprocal(out=rs_, in_=vr), reads=["vr%d" % b], writes=["rs%d" % b])
                S.op("dve", lambda e: e.scalar_tensor_tensor(out=yo, in0=hp_ap, scalar=rs_, in1=fng_bc[0:rows, :], op0=ALU.mult, op1=ALU.mult),
                     reads=["hp%d" % b, "rs%d" % b, "ln_bc", "yout%d" % b], writes=["yout%d" % b])
                dma(out_ap, yo, ["yout%d" % b], [okey], okey + str(b), eng="act_store")

            WMAP = list(range(8)) + [12, 13, 14, 15] + [8, 9, 10, 11]

            def out_proj(b, rows, wmap=tuple(WMAP)):
                def mm(e):
                    for hf in range(2):
                        for j in range(16):
                            ins = e.matmul(pso[b][0:rows, hf * 512:(hf + 1) * 512], lhsT=catTs[b][:, j, 0:rows], rhs=w_out_bf[:, wmap[j], hf * 512:(hf + 1) * 512], start=(j == 0), stop=(j == 15))
                    return ins
                S.op("pe", mm, reads=["catT%d" % b, "w_out"], writes=["po%d" % b])

            def stageP(t):
                b = t % 2
                tsl = slice(t * 128, (t + 1) * 128)
                o3 = o3s[b]; sg_t = sg_ts[b]; catT_t = catTs[b]; osum = osums[b]; rdn = rdns[b]; oatt = oatts[b]; oatt_bf = oattbfs[b]
                dma(xt2[b][:], x[tsl, :], [], ["xt2_%d" % b], "x%d" % b)
                dma(o3[:], o_scr[:, tsl, :].rearrange("d p c -> p d c"), [], ["o3_%d" % b], "l_o3%d" % b)
                dma(sg_t, sg_scr[tsl, :], [], ["sg%d" % b], "l_sg%d" % b, eng="act")
                dma(catT_t[:, 0:12, :], cat_scr[t], [], ["catT%d" % b], "l_cat%d" % b, eng="act")
                o3v = o3.rearrange("p d (h c) -> p d h c", h=8)
                S.op("dve", lambda e: e.tensor_tensor(out=osum[:], in0=o3v[:, 0], in1=o3v[:, 1], op=ALU.add), reads=["o3_%d" % b], writes=["osum%d" % b])
                S.op("dve", lambda e: e.tensor_tensor(out=osum[:], in0=osum[:], in1=o3v[:, 2], op=ALU.add), reads=["o3_%d" % b, "osum%d" % b], writes=["osum%d" % b])
                S.op("dve", lambda e: e.reciprocal(out=rdn, in_=osum[:, :, 64]), reads=["osum%d" % b], writes=["rdn%d" % b])
                S.op("dve", lambda e: e.tensor_tensor(out=oatt.rearrange("p (h c) -> p h c", h=8), in0=osum[:, :, 0:64], in1=b3(rdn, 8, 64), op=ALU.mult), reads=["osum%d" % b, "rdn%d" % b], writes=["oatt%d" % b])
                S.op("dve", lambda e: e.tensor_tensor(out=oatt_bf, in0=oatt, in1=sg_t, op=ALU.mult), reads=["oatt%d" % b, "sg%d" % b], writes=["oattbf%d" % b])

            def stageP2(t):
                b = t % 2
                catT_t = catTs[b]; oatt_bf = oattbfs[b]

                def tra(e):
                    for c in range(4):
                        ins = e.transpose(out=ps_tr[:, b * 512 + c * 128:b * 512 + (c + 1) * 128], in_=oatt_bf[:, c * 128:(c + 1) * 128], identity=identb[:])
                    return ins
                S.op("pe", tra, reads=["oattbf%d" % b, "identb"], writes=["ptr%d" % b])
                S.op("act", lambda e: e.copy(out=catT_t[:, 12:16, :], in_=ps_tr[:, b * 512:(b + 1) * 512].rearrange("p (c t) -> p c t", c=4)), reads=["ptr%d" % b], writes=["catT%d" % b])

            def stageQ1(t):
                out_proj(t % 2, 128)

            def stageQ2(t):
                b = t % 2
                S.op("dve", lambda e: e.tensor_tensor(out=hps[b][:], in0=pso[b], in1=xt2[b][:], op=ALU.add), reads=["po%d" % b, "xt2_%d" % b], writes=["hp%d" % b])
                final_norm_out(b, 128, yp[t * 128:(t + 1) * 128, :], "o_y")

            stageP(0)
            stageP2(0)
            for t in range(NT):
                if t + 1 < NT:
                    stageP(t + 1)
                stageQ1(t)
                if t + 1 < NT:
                    stageP2(t + 1)
                stageQ2(t)

            if UPTO >= 'S':
                for q4 in range(4):
                    dma(oatts[0][0:NB, 0:512], cat_s_scr[:, q4 * 512:(q4 + 1) * 512], ["cat_s_scr"], ["oatt0"], "l_sg0")
                    S.op("dve", lambda e, q4=q4: e.tensor_copy(out=cat_s_bf[0:NB, q4 * 512:(q4 + 1) * 512], in_=oatts[0][0:NB, 0:512]), reads=["oatt0"], writes=["cat_s_bf"])

                def trs(e):
                    for j in range(16):
                        ins = e.transpose(out=ps_tr[:, j * 16:(j + 1) * 16], in_=cat_s_bf[0:NB, j * 128:(j + 1) * 128], identity=identb[0:NB, 0:NB])
                    return ins
                S.op("pe", trs, reads=["cat_s_bf", "identb"], writes=["ptr0"])
                S.op("act", lambda e: e.copy(out=catTs[0][:, :, 0:NB], in_=ps_tr[:, 0:256].rearrange("p (c t) -> p c t", c=16)), reads=["ptr0"], writes=["catT0"])
                out_proj(0, NB, wmap=tuple(range(16)))
                dma(xt2[0][0:NB, :], xs, [], ["xt2_0"], "x0")
                S.op("dve", lambda e: e.tensor_tensor(out=hps[0][0:NB, :], in0=pso[0][0:NB, :], in1=xt2[0][0:NB, :], op=ALU.add), reads=["po0", "xt2_0"], writes=["hp0"])
                final_norm_out(0, NB, ys, "o_ys")
        S.barrier_all("sp")
        S.emit()
    return nc


_CACHE = {}


def _consts():
    k = np.arange(128)
    tri = (k[:, None] <= k[None, :]).astype(np.float32)
    return dict(c_ident=np.eye(128, dtype=np.float32), c_tri=tri, c_triT=np.ascontiguousarray(tri.T),
                c_posf=(128 * np.arange(32)[None, :] + k[:, None]).astype(np.float32),
                c_inv=(500000.0 ** (-np.arange(8, dtype=np.float32) / 8)).astype(np.float32)[None, :],
                c_esel=np.eye(16, dtype=np.float32).reshape(1, 256))


def kernel(x_prompt, x_sample, mem_prompt, cache_win_k, cache_win_v, cache_mem_k, cache_mem_v,
           state_conv, state_ssm, pos_sample, ln_g, w_in, conv_w, conv_b, dt_bias, a_log, d_skip,
           ssd_norm_g, mem_norm_g, w_mem_kv, w_out, final_norm_g):
    f = lambda a: np.ascontiguousarray(np.asarray(a, dtype=np.float32))
    if "nc" not in _CACHE:
        _CACHE["nc"] = build_program()
    nc = _CACHE["nc"]
    cs = _consts()
    in_maps = []
    for i in range(8):
        sl = slice(NB * i, NB * (i + 1))
        d = dict(x=f(x_prompt[i]), xs=f(x_sample[sl, 0]), mem=f(mem_prompt[i]),
                 cwk=f(cache_win_k[0, sl]).reshape(NB, 2048, 512), cwv=f(cache_win_v[0, sl]).reshape(NB, 2048, 512),
                 cmk=f(cache_mem_k[0, sl]).reshape(NB, 256, 512), cmv=f(cache_mem_v[0, sl]).reshape(NB, 256, 512),
                 sconv=f(state_conv[0, sl]), sssm=f(state_ssm[0, sl]).reshape(-1, 128),
                 pos=np.ascontiguousarray(np.asarray(pos_sample[sl], dtype=np.int32)),
                 ln_g=f(ln_g), w_in=f(w_in[0]), conv_w=f(conv_w[0]), conv_b=f(conv_b), dt_bias=f(dt_bias),
                 a_log=f(a_log), d_skip=f(d_skip), ssd_norm_g=f(ssd_norm_g), mem_norm_g=f(mem_norm_g),
                 w_mem_kv=f(w_mem_kv[0]), w_out=f(w_out[0]), final_norm_g=f(final_norm_g)[None, :])
        d.update(cs)
        in_maps.append(d)
    res = run_bass_kernel_spmd(nc, in_maps, core_ids=list(range(8)))
    R = res.results
    cat = lambda name: np.stack([np.asarray(r[name], dtype=np.float32) for r in R], axis=0)
    y_prompt = cat("yp")
    y_sample = cat("ys").reshape(128, 1, 1024)
    win_k_prompt = cat("wk").reshape(1, 8, 2048, 8, 64)
    win_v_prompt = cat("wv").reshape(1, 8, 2048, 8, 64)
    mem_k_prompt = cat("mko").reshape(1, 8, 256, 4, 128)
    mem_v_prompt = cat("mvo").reshape(1, 8, 256, 4, 128)
    conv_prompt = cat("convp").reshape(1, 8, 3, 2048)
    ssm_prompt = cat("ssmp").reshape(1, 8, 16, 64, 128)
    win_k_sample = cat("wks").reshape(1, 128, 1, 8, 64)
    win_v_sample = cat("wvs").reshape(1, 128, 1, 8, 64)
    conv_sample = cat("convs").reshape(1, 128, 3, 2048)
    ssm_sample = cat("ssms").reshape(1, 128, 16, 64, 128)
    return (y_prompt, y_sample, win_k_prompt, win_v_prompt, mem_k_prompt, mem_v_prompt, conv_prompt, ssm_prompt,
            win_k_sample, win_v_sample, conv_sample, ssm_sample)
```

```python
import math
from contextlib import ExitStack
import numpy as np
import concourse.bass as bass
import concourse.mybir as mybir
from concourse.bass_utils import run_bass_kernel_spmd

F32 = mybir.dt.float32
BF16 = mybir.dt.bfloat16
I32 = mybir.dt.int32
ALU = mybir.AluOpType
AF = mybir.ActivationFunctionType
AX = mybir.AxisListType

COMPUTE = ("pe", "act", "dve", "pool")
ENGINES = ("pe", "act", "dve", "pool", "sp")

D = 1024
SEQ = 4096
NT = SEQ // 128
NIN = 6160
NB = 16
EPS = 1e-6
DILS = (1, 4, 16)


class Sched:
    def __init__(self, nc):
        self.nc = nc
        self.ops = {e: [] for e in ENGINES}
        self.cnt = {e: 0 for e in COMPUTE}
        self.last_write = {}
        self.readers = {}
        self.dma_cnt = {}
        self.waited = {e: {} for e in ENGINES}

    def _tok_val(self, tok):
        kind, name, val = tok
        if kind == "dma":
            return ("dma:" + name, 16 * self.dma_cnt[name])
        return (name, val)

    def op(self, eng, fn, reads=(), writes=(), dma=None):
        waits = {}

        def need(tok, raw):
            if tok is None:
                return
            kind, name, val = tok
            if kind == "eng" and name == eng and (eng == "pe" or not raw):
                return
            s, v = self._tok_val(tok)
            if waits.get(s, 0) < v:
                waits[s] = v

        for r in reads:
            need(self.last_write.get(r), True)
        for w in writes:
            need(self.last_write.get(w), True)
            for t in self.readers.get(w, ()):
                need(t, False)
        wl = []
        for s, v in waits.items():
            if self.waited[eng].get(s, 0) < v:
                self.waited[eng][s] = v
                wl.append((s, v))
        if dma is not None:
            self.dma_cnt[dma] = self.dma_cnt.get(dma, 0) + 1
            tok = ("dma", dma, None)
            inc = ("dma:" + dma, 16)
        else:
            self.cnt[eng] += 1
            tok = ("eng", eng, self.cnt[eng])
            inc = (eng, 1)
        self.ops[eng].append((wl, fn, inc))
        for r in reads:
            self.readers.setdefault(r, []).append(tok)
        for w in writes:
            self.last_write[w] = tok
            self.readers[w] = []
        return tok

    def barrier_all(self, eng="sp"):
        wl = []
        for e in COMPUTE:
            if e != eng and self.cnt[e] > 0 and self.waited[eng].get(e, 0) < self.cnt[e]:
                self.waited[eng][e] = self.cnt[e]
                wl.append((e, self.cnt[e]))
        for k, c in self.dma_cnt.items():
            s = "dma:" + k
            if self.waited[eng].get(s, 0) < 16 * c:
                self.waited[eng][s] = 16 * c
                wl.append((s, 16 * c))
        self.ops[eng].append((wl, None, None))

    def full_barrier(self):
        for e in ENGINES:
            self.barrier_all(e)

    def emit(self):
        nc = self.nc
        names = list(COMPUTE) + ["dma:" + k for k in self.dma_cnt]
        with ExitStack() as es:
            sems = {n: es.enter_context(nc.semaphore(("s_" + n).replace(":", "_")))
                    for n in names}
            block = es.enter_context(nc.Block())

            def run(engname):
                def body(eng):
                    for wl, fn, inc in self.ops[engname]:
                        for s, v in wl:
                            eng.wait_ge(sems[s], v)
                        if fn is not None:
                            ins = fn(eng)
                            ins.then_inc(sems[inc[0]], inc[1])
                return body

            block.tensor(run("pe"))
            block.scalar(run("act"))
            block.vector(run("dve"))
            block.gpsimd(run("pool"))
            block.sync(run("sp"))


import os
UPTO = os.environ.get('K_UPTO', 'S')
BSTOP = int(os.environ.get('K_BSTOP', '9'))


def build_program():
    nc = bass.Bass("TRN2", target_bir_lowering=False)
    S = Sched(nc)

    def din(name, shape, dt=F32):
        return nc.dram_tensor(name, list(shape), dt, kind="ExternalInput").ap()

    def dout(name, shape, dt=F32):
        return nc.dram_tensor(name, list(shape), dt, kind="ExternalOutput").ap()

    def dscr(name, shape, dt):
        return nc.dram_tensor(name, list(shape), dt).ap()

    x = din("x", [SEQ, D])
    xs = din("xs", [NB, D])
    mem = din("mem", [256, D])
    cwk = din("cwk", [NB, 2048, 512])
    cwv = din("cwv", [NB, 2048, 512])
    cmk = din("cmk", [NB, 256, 512])
    cmv = din("cmv", [NB, 256, 512])
    sconv = din("sconv", [NB, 3, 2048])
    sssm = din("sssm", [NB * 16 * 64, 128])
    pos = din("pos", [NB, 1], I32)
    ln_g = din("ln_g", [1, D])
    w_in = din("w_in", [D, NIN])
    conv_w = din("conv_w", [4, 2048])
    conv_b = din("conv_b", [1, 2048])
    dt_bias = din("dt_bias", [1, 16])
    a_log = din("a_log", [1, 16])
    d_skip = din("d_skip", [1, 16])
    ssd_norm_g = din("ssd_norm_g", [1, D])
    mem_norm_g = din("mem_norm_g", [1, D])
    w_mem_kv = din("w_mem_kv", [D, D])
    w_out = din("w_out", [2048, D])
    final_norm_g = din("final_norm_g", [1, D])
    c_ident = din("c_ident", [128, 128])
    c_tri = din("c_tri", [128, 128])
    c_triT = din("c_triT", [128, 128])
    c_posf = din("c_posf", [128, NT])
    c_inv = din("c_inv", [1, 8])
    c_esel = din("c_esel", [1, 256])

    yp = dout("yp", [SEQ, D])
    ys = dout("ys", [NB, D])
    wk = dout("wk", [2048, 512])
    wv = dout("wv", [2048, 512])
    mko = dout("mko", [256, 512])
    mvo = dout("mvo", [256, 512])
    convp = dout("convp", [3, 2048])
    ssmp = dout("ssmp", [1024, 128])
    wks = dout("wks", [NB, 512])
    wvs = dout("wvs", [NB, 512])
    convs = dout("convs", [NB, 3, 2048])
    ssms = dout("ssms", [NB * 16 * 64, 128])

    cat_scr = dscr("cat_scr", [NT, 128, 12, 128], BF16)
    qkT_scr = dscr("qkT_scr", [128, 8, SEQ], BF16)
    v_scr = dscr("v_scr", [SEQ, 528], BF16)
    sg_scr = dscr("sg_scr", [SEQ, 512], BF16)
    o_scr = dscr("o_scr", [3, SEQ, 520], F32)
    s_scr = dscr("s_scr", [NB, 4096], F32)
    us_scr = dscr("us_scr", [NB, NIN], F32)
    cat_s_scr = dscr("cat_s_scr", [NB, 2048], F32)
    sx_scr = dscr("sx_scr", [NB, 1024], F32)
    sB_scr = dscr("sB_scr", [NB, 1024], F32)
    sC_scr = dscr("sC_scr", [NB, 1024], F32)
    sdA_scr = dscr("sdA_scr", [NB, 16], F32)
    sq_scr = dscr("sq_scr", [NB, 512], F32)
    sk_scr = dscr("sk_scr", [NB, 512], F32)
    sv_scr = dscr("sv_scr", [NB, 512], F32)

    with ExitStack() as es:
        def sb(name, shape, dt=F32):
            return es.enter_context(nc.sbuf_tensor(name, list(shape), dt))

        def ps(name, shape, dt=F32):
            return es.enter_context(nc.psum_tensor(name, list(shape), dt))

        ps_tr = ps("ps_tr", [128, 1024], BF16)
        ps_cb = ps("ps_cb", [128, 512], F32)
        ps_in = ps("ps_in", [128, 1024], F32)
        ps_4 = ps("ps_4", [128, 2048], F32)
        ps_big = ps_4[:, 0:1024]
        ps_y = ps_4[:, 1024:2048]

        arena1 = sb("arena1", [128, 50688], BF16)
        w_in_bf = arena1[:, 0:8 * NIN].rearrange("p (k n) -> p k n", n=NIN)
        V3 = arena1[:, 0:3 * 32 * 528].rearrange("p (d t c) -> p d t c", d=3, t=32)
        w_out_bf = arena1[:, 0:16 * 1024].rearrange("p (k n) -> p k n", n=1024)

        identf = sb("identf", [128, 128]); identb = sb("identb", [128, 128], BF16)
        tri = sb("tri", [128, 128]); ustr = sb("ustr", [128, 128]); onesf = sb("onesf", [128, 128])
        onesb = sb("onesb", [128, 128], BF16)
        mask2 = sb("mask2", [128, 2, 128], BF16)
        tmpc = sb("tmpc", [128, 128])
        ln_bc = sb("ln_bc", [128, D]); sng_bc = sb("sng_bc", [128, D])
        dtb_bc = sb("dtb_bc", [128, 16]); a_bc = sb("a_bc", [128, 16]); dsk_bc = sb("dsk_bc", [128, 16])
        cwb = sb("cwb", [128, 80])
        posf = sb("posf", [128, 33]); posi = sb("posi", [128, 1], I32)
        inv_bc = sb("inv_bc", [128, 8])
        cosT = sb("cosT", [128, 33, 8]); sinT = sb("sinT", [128, 33, 8])
        ang = sb("ang", [128, 33, 8]); rr = sb("rr", [128, 33, 8]); yy = sb("yy", [128, 33, 8])
        MKT = sb("MKT", [128, 4, 256], BF16); MV = sb("MV", [128, 2, 512], BF16)
        stat = sb("stat", [128, 16])

        A2 = 22316
        arena2 = sb("arena2", [128, A2], F32)
        off = [0]

        def carve(n_f32):
            a = off[0]
            off[0] += n_f32
            assert off[0] <= A2, off[0]
            return arena2[:, a:a + n_f32]

        def carve_bf(n_bf16):
            n = (n_bf16 + 1) // 2
            return carve(n).bitcast(BF16)[:, 0:n_bf16]

        A = AF

        def dma(out, in_, reads, writes, key, eng="sp", **kw):
            q = "act" if eng == "act_store" else "sp"
            return S.op(q, lambda e: e.dma_start(out=out, in_=in_, **kw), reads=reads, writes=writes, dma=key)

        dma(identf[:], c_ident, [], ["identf"], "c0")
        dma(tri[:], c_tri, [], ["tri"], "c0")
        dma(tmpc[:], c_triT, [], ["tmpc"], "c0")
        dma(ln_bc[:], ln_g.partition_broadcast(128), [], ["ln_bc"], "c0")
        dma(sng_bc[:], ssd_norm_g.partition_broadcast(128), [], ["sng_bc"], "c0")
        dma(dtb_bc[:], dt_bias.partition_broadcast(128), [], ["dtb_bc"], "c0")
        dma(a_bc[:], a_log.partition_broadcast(128), [], ["a_bc"], "c0")
        dma(dsk_bc[:], d_skip.partition_broadcast(128), [], ["dsk_bc"], "c0")
        dma(inv_bc[:], c_inv.partition_broadcast(128), [], ["inv_bc"], "c0")
        dma(posf[:, 0:NT], c_posf, [], ["posf"], "c0")
        S.op("pool", lambda e: e.memset(posi[:], 0), writes=["posi"])
        dma(posi[0:NB, :], pos, ["posi"], ["posi"], "c0")
        S.op("dve", lambda e: e.tensor_copy(out=posf[:, NT:NT + 1], in_=posi[:]), reads=["posi", "posf"], writes=["posf"])
        S.op("dve", lambda e: e.tensor_copy(out=identb[:], in_=identf[:]), reads=["identf"], writes=["identb"])
        S.op("dve", lambda e: e.tensor_scalar(out=ustr[:], in0=tri[:], scalar1=-1.0, scalar2=1.0, op0=ALU.mult, op1=ALU.add), reads=["tri"], writes=["ustr"])
        S.op("pool", lambda e: e.memset(onesf[:], 1.0), writes=["onesf"])
        S.op("pool", lambda e: e.memset(onesb[:], 1.0), writes=["onesb"])
        S.op("dve", lambda e: e.tensor_copy(out=mask2[:, 0, :], in_=tmpc[:]), reads=["tmpc"], writes=["mask2a"])
        S.op("dve", lambda e: e.tensor_copy(out=mask2[:, 1, :], in_=tri[:]), reads=["tri"], writes=["mask2b"])
        S.op("act", lambda e: e.activation(out=a_bc[:], in_=a_bc[:], func=A.Exp), reads=["a_bc"], writes=["a_bc"])
        S.op("dve", lambda e: e.tensor_scalar(out=a_bc[:], in0=a_bc[:], scalar1=-1.0, scalar2=None, op0=ALU.mult), reads=["a_bc"], writes=["a_bc"])

        MAGIC = 12582912.0
        C1 = 6.28125
        C2 = 2 * math.pi - C1
        S.op("dve", lambda e: e.tensor_tensor(out=ang[:], in0=posf[:].unsqueeze(2).broadcast_to([128, 33, 8]),
                                              in1=inv_bc[:].unsqueeze(1).broadcast_to([128, 33, 8]), op=ALU.mult),
             reads=["posf", "inv_bc"], writes=["ang"])
        for (dst, shift, name) in ((sinT, 0.0, "sinT"), (cosT, 0.25, "cosT")):
            S.op("dve", lambda e, shift=shift: e.tensor_scalar(out=rr[:], in0=ang[:], scalar1=1.0 / (2 * math.pi), scalar2=shift, op0=ALU.mult, op1=ALU.add), reads=["ang"], writes=["rr"])
            S.op("dve", lambda e: e.tensor_scalar(out=rr[:], in0=rr[:], scalar1=MAGIC, scalar2=None, op0=ALU.add), reads=["rr"], writes=["rr"])
            S.op("dve", lambda e: e.tensor_scalar(out=rr[:], in0=rr[:], scalar1=-MAGIC, scalar2=None, op0=ALU.add), reads=["rr"], writes=["rr"])
            S.op("dve", lambda e: e.scalar_tensor_tensor(out=yy[:], in0=rr[:], scalar=-C1, in1=ang[:], op0=ALU.mult, op1=ALU.add), reads=["rr", "ang"], writes=["yy"])
            S.op("dve", lambda e: e.scalar_tensor_tensor(out=yy[:], in0=rr[:], scalar=-C2, in1=yy[:], op0=ALU.mult, op1=ALU.add), reads=["rr", "yy"], writes=["yy"])
            if shift:
                S.op("dve", lambda e: e.tensor_scalar(out=yy[:], in0=yy[:], scalar1=math.pi / 2, scalar2=None, op0=ALU.add), reads=["yy"], writes=["yy"])
            S.op("dve", lambda e: e.tensor_scalar(out=yy[:], in0=yy[:], scalar1=3.141592, scalar2=-3.141592, op0=ALU.min, op1=ALU.max), reads=["yy"], writes=["yy"])
            S.op("act", lambda e, dst=dst: e.activation(out=dst[:], in_=yy[:], func=A.Sin), reads=["yy"], writes=[name])

        cstage = carve(128)
        dma(cstage[0:64, :], conv_w.rearrange("t (c p) -> (t c) p", p=128), [], ["cstage"], "c1")
        dma(cstage[64:80, :], conv_b.rearrange("o (c p) -> (o c) p", p=128), [], ["cstage"], "c1")
        S.op("pe", lambda e: e.matmul(ps_cb[:, 0:80], lhsT=cstage[0:80, :], rhs=identf[0:80, 0:80], start=True, stop=True), reads=["cstage", "identf"], writes=["ps_cb"])
        S.op("act", lambda e: e.copy(out=cwb[:], in_=ps_cb[:, 0:80]), reads=["ps_cb"], writes=["cwb"])

        stage = carve(3 * 3080).rearrange("p (s n) -> p s n", s=3)
        conv_engs = ("act", "pool", "dve")

        def cast(eng, out, in_, reads, writes):
            if eng == "act":
                S.op("act", lambda e: e.copy(out=out, in_=in_), reads=reads, writes=writes)
            else:
                S.op(eng, lambda e: e.tensor_copy(out=out, in_=in_), reads=reads, writes=writes)

        ci = 0
        for k in range(8):
            for hf in range(2):
                sidx = ci % 3
                c0 = hf * 3080
                dma(stage[:, sidx, :], w_in[k * 128:(k + 1) * 128, c0:c0 + 3080], [], ["stage%d" % sidx], "stg%d" % sidx)
                cast(("act", "dve", "act", "dve", "pool")[ci % 5], w_in_bf[:, k, c0:c0 + 3080], stage[:, sidx, :], ["stage%d" % sidx], ["w_in"])
                ci += 1

        ssq = stat[:, 0:1]; var = stat[:, 1:2]; rstd = stat[:, 2:3]
        junk0 = carve(1024)
        h_bf0 = carve_bf(1024)
        xt0 = [carve(1024), carve(1024)]

        def rms_to_hT(x_ap, xres, g_bc, gres, rows, junk, jres, h_bf, hT_dst, hres):
            S.op("act", lambda e: e.activation(out=junk[0:rows, :], in_=x_ap, func=A.Square, accum_out=ssq[0:rows, :]),
                 reads=[xres], writes=[jres, "ssq"])
            S.op("dve", lambda e: e.tensor_scalar(out=var[0:rows, :], in0=ssq[0:rows, :], scalar1=1.0 / D, scalar2=EPS, op0=ALU.mult, op1=ALU.add),
                 reads=["ssq"], writes=["var"])
            S.op("act", lambda e: e.activation(out=var[0:rows, :], in_=var[0:rows, :], func=A.Ln), reads=["var"], writes=["var"])
            S.op("act", lambda e: e.activation(out=rstd[0:rows, :], in_=var[0:rows, :], func=A.Exp, scale=-0.5), reads=["var"], writes=["rstd"])
            S.op("dve", lambda e: e.scalar_tensor_tensor(out=h_bf[0:rows, :], in0=x_ap, scalar=rstd[0:rows, :], in1=g_bc[0:rows, :], op0=ALU.mult, op1=ALU.mult),
                 reads=[xres, "rstd", gres], writes=["h_bf"])

            def tr(e):
                for k in range(8):
                    ins = e.transpose(out=ps_tr[:, k * 128:k * 128 + rows], in_=h_bf[0:rows, k * 128:(k + 1) * 128], identity=identb[0:rows, 0:rows])
                return ins
            S.op("pe", tr, reads=["h_bf", "identb"], writes=["ps_tr"])
            dst = hT_dst
            S.op("act", lambda e: e.copy(out=dst[:, :, 0:rows], in_=ps_tr[:].rearrange("p (k t) -> p k t", k=8)[:, :, 0:rows]),
                 reads=["ps_tr"], writes=[hres])

        mem_bc = carve(1024)
        dma(mem_bc[:], mem_norm_g.partition_broadcast(128), [], ["mem_bc"], "c1")
        wkv_bf = carve_bf(8 * 1024).rearrange("p (s n) -> p s n", s=8)
        mkv_f = carve(1024)
        mk_bf = carve_bf(512)
        for mt in range(2):
            dma(xt0[mt][:], mem[mt * 128:(mt + 1) * 128, :], [], ["xt%d" % mt], "x%d" % mt)
        hT2 = carve_bf(2 * 1024).rearrange("p (m k t) -> p m k t", m=2, k=8)
        for mt in range(2):
            rms_to_hT(xt0[mt][:], "xt%d" % mt, mem_bc, "mem_bc", 128, junk0, "junk0", h_bf0, hT2[:, mt], "hT2_%d" % mt)
        for k in range(8):
            sidx = k % 2
            dma(stage[:, sidx, 0:1024], w_mem_kv[k * 128:(k + 1) * 128, :], [], ["stage%d" % sidx], "stg%d" % sidx)
            cast(conv_engs[k % 3], wkv_bf[:, k, :], stage[:, sidx, 0:1024], ["stage%d" % sidx], ["wkv"])

        def mmkv(e):
            for mt in range(2):
                for hf in range(2):
                    for k in range(8):
                        ins = e.matmul(ps_4[:, mt * 1024 + hf * 512: mt * 1024 + (hf + 1) * 512], lhsT=hT2[:, mt, k, :],
                                       rhs=wkv_bf[:, k, hf * 512:(hf + 1) * 512], start=(k == 0), stop=(k == 7))
            return ins
        S.op("pe", mmkv, reads=["hT2_0", "hT2_1", "wkv"], writes=["ps_4"])
        for mt in range(2):
            S.op("act", lambda e, mt=mt: e.copy(out=mkv_f[:], in_=ps_4[:, mt * 1024:(mt + 1) * 1024]), reads=["ps_4"], writes=["mkv_f"])
            dma(mko[mt * 128:(mt + 1) * 128, :], mkv_f[:, 0:512], ["mkv_f"], ["mko"], "o_mk")
            dma(mvo[mt * 128:(mt + 1) * 128, :], mkv_f[:, 512:1024], ["mkv_f"], ["mvo"], "o_mk")
            S.op("dve", lambda e: e.tensor_copy(out=mk_bf[:], in_=mkv_f[:, 0:512]), reads=["mkv_f"], writes=["mk_bf"])
            S.op("dve", lambda e, mt=mt: e.tensor_copy(out=MV[:, mt, :], in_=mkv_f[:, 512:1024]), reads=["mkv_f"], writes=["MV"])

            def trk(e):
                for h in range(4):
                    ins = e.transpose(out=ps_tr[:, h * 128:(h + 1) * 128], in_=mk_bf[:, h * 128:(h + 1) * 128], identity=identb[:])
                return ins
            S.op("pe", trk, reads=["mk_bf", "identb"], writes=["ps_tr"])
            S.op("act", lambda e, mt=mt: e.copy(out=MKT[:, :, mt * 128:(mt + 1) * 128], in_=ps_tr[:, 0:512].rearrange("p (h m) -> p h m", h=4)),
                 reads=["ps_tr"], writes=["MKT"])

        off_A = off[0]
        off[0] = 0
        S.full_barrier()
        t1 = carve(1024); junk = t1
        h_bf = carve_bf(1024)
        hT = carve_bf(1024).rearrange("p (k t) -> p k t", k=8)
        xt1 = carve(1024)
        xt = [xt1, xt1]
        szs = [carve(1024), carve(1024)]
        qf = carve(512); kf = carve(512); vf = carve(512)
        rden = carve(512); ocr = rden
        sgc = carve(512)
        rt = carve(256).rearrange("p (a h i) -> p a h i", a=4, h=8)
        q_bf = carve_bf(512); k_bf = carve_bf(512); sg_bf = carve_bf(512)
        v_aug = carve_bf(528).rearrange("p (h c) -> p h c", c=66)
        dts = carve(128)
        dtbufs = [tuple(dts[:, p_ * 64 + i_ * 16:p_ * 64 + (i_ + 1) * 16] for i_ in range(4)) for p_ in range(2)]
        csb = carve(48); exs = carve(48)
        xbuf = carve(16 * 131).rearrange("p (c t) -> p c t", c=16)
        acc = carve(2048).rearrange("p (c t) -> p c t", c=16)
        acc2 = carve(2048).rearrange("p (c t) -> p c t", c=16)
        xc_bf = carve_bf(2048).rearrange("p (c t) -> p c t", c=16)
        xtok_bf = carve_bf(1024); xdt_bf = carve_bf(1024); xds_bf = xtok_bf
        yn_bf = xds_bf
        Btok_bf = carve_bf(512).rearrange("p (g n) -> p g n", g=4)
        CBm = carve_bf(512).rearrange("p (g l) -> p g l", g=4)
        rhsd = acc
        E_flat = carve_bf(2048)
        E_bf = E_flat.rearrange("p (h l) -> p h l", h=16)
        MT_bf = E_bf
        PT_bf = E_flat[:, 0:1024].rearrange("p (h m t) -> p h m t", h=4, m=2)
        state = carve(1024); state_bf = carve_bf(1024)
        yv = carve(1024)
        gss = stat[:, 4:8]; gvar = stat[:, 8:12]; grstd = stat[:, 12:16]
        catT_sb = carve_bf(1024).rearrange("p (c t) -> p c t", c=8)
        qkT_sb = catT_sb
        qcT = carve_bf(512).rearrange("p (h t) -> p h t", h=4)
        catTc = catT_sb[:, 0:4, :]
        print("arena2 phase A words:", off[0])

        S.op("pool", lambda e: e.memset(state[:], 0.0), writes=["state"])
        S.op("pool", lambda e: e.memset(state_bf[:], 0.0), writes=["state_bf"])
        S.op("pool", lambda e: e.memset(xbuf[:, :, 0:3], 0.0), writes=["xbuf"])
        S.op("pool", lambda e: e.memset(v_aug[:], 1.0), writes=["v_aug"])

        TOK_GROUPS = (("z0", 0), ("z1", 512), ("q", 3088), ("k", 3600), ("v", 4112), ("g", 4624))
        FM_GROUPS = (("xbc0", 1024), ("xbc1", 1536), ("xbc2", 2048), ("xbc3", 2560), ("qc", 5136), ("gc", 5648))
        bank_i = [0]

        def next_bank():
            b = bank_i[0] % 2
            bank_i[0] += 1
            return ps_in[:, b * 512:(b + 1) * 512], "ps_in%d" % b

        def load_x(t):
            b = t % 2
            if t < NT:
                dma(xt[b][:], x[t * 128:(t + 1) * 128, :], [], ["xt"], "x0")
            else:
                dma(xt[b][0:NB, :], xs, [], ["xt"], "x0")

        def tok_group(c0, ncol, rows=128):
            bank, bres = next_bank()

            def mm(e):
                for k in range(8):
                    ins = e.matmul(bank[0:rows, 0:ncol], lhsT=hT[:, k, 0:rows], rhs=w_in_bf[:, k, c0:c0 + ncol], start=(k == 0), stop=(k == 7))
                return ins
            S.op("pe", mm, reads=["hT", "w_in"], writes=[bres])
            return bank, bres

        def fm_group(c0):
            bank, bres = next_bank()

            def mm(e):
                for c in range(4):
                    for k in range(8):
                        ins = e.matmul(bank[:, c * 128:(c + 1) * 128], lhsT=w_in_bf[:, k, c0 + c * 128:c0 + (c + 1) * 128], rhs=hT[:, k, :], start=(k == 0), stop=(k == 7))
                return ins
            S.op("pe", mm, reads=["hT", "w_in"], writes=[bres])
            return bank, bres

        def b3(ap, h, n):
            return ap.unsqueeze(2).broadcast_to([128, h, n])

        def rotary(src, t, rows, rt):
            v3 = src.rearrange("p (h e) -> p h e", h=8)
            x1 = v3[0:rows, :, 0:8]; x2 = v3[0:rows, :, 8:16]
            cb = cosT[0:rows, t, :].unsqueeze(1).broadcast_to([rows, 8, 8])
            sn = sinT[0:rows, t, :].unsqueeze(1).broadcast_to([rows, 8, 8])

            def f(e):
                e.tensor_tensor(out=rt[0:rows, 0], in0=x1, in1=cb, op=ALU.mult)
                e.tensor_tensor(out=rt[0:rows, 1], in0=x2, in1=sn, op=ALU.mult)
                e.tensor_tensor(out=rt[0:rows, 2], in0=x2, in1=cb, op=ALU.mult)
                return e.tensor_tensor(out=rt[0:rows, 3], in0=x1, in1=sn, op=ALU.mult)
            return f, x1, x2

        def do_rotary(src, sres, t, rows, rt):
            f, x1, x2 = rotary(src, t, rows, rt)
            S.op("pool", f, reads=[sres, "cosT", "sinT"], writes=["rt"])
            S.op("pool", lambda e: e.tensor_tensor(out=x1, in0=rt[0:rows, 0], in1=rt[0:rows, 1], op=ALU.subtract), reads=["rt"], writes=[sres])
            S.op("pool", lambda e: e.tensor_tensor(out=x2, in0=rt[0:rows, 2], in1=rt[0:rows, 3], op=ALU.add), reads=["rt"], writes=[sres])

        def pre(t):
            b = t % 2
            sz = szs[b]; RSZ = "sz%d" % b
            dtx, dte, dtv, adt = dtbufs[b]
            RDX, RDE, RDV, RAD = ["%s%d" % (n_, b) for n_ in ("dtx", "dte", "dtv", "adt")]
            rms_to_hT(xt[b][:], "xt", ln_bc, "ln_bc", 128, h_bf, "h_bf", h_bf, hT, "hT")
            load_x(t + 1)
            yield
            for gi in range(4):
                bank, bres = fm_group(1024 + gi * 512)
                S.op("act", lambda e, bank=bank, gi=gi: e.copy(out=xbuf[:, gi * 4:(gi + 1) * 4, 3:131], in_=bank.rearrange("p (c t) -> p c t", c=4)),
                     reads=[bres], writes=["xbuf"])
                yield
            if t == NT - 1:
                for r_ in range(3):
                    dma(convp[r_:r_ + 1, :].rearrange("o (c p) -> p (o c)", p=128), xbuf[:, :, 128 + r_], ["xbuf"], ["convp"], "o_cv", allow_slow_non_contiguous=True)
            def wv_(tap):
                return cwb[:, tap * 16:(tap + 1) * 16].unsqueeze(2).broadcast_to([128, 16, 128])

            CSPLIT = 12

            def conv_range(eng, c0, c1, sfx):
                nch = c1 - c0
                a = acc[:, c0:c1, :]; a2 = acc2[:, c0:c1, :]
                ra = "acc" + sfx; ra2 = "acc2" + sfx

                def w_(tap):
                    return cwb[:, tap * 16 + c0:tap * 16 + c1].unsqueeze(2).broadcast_to([128, nch, 128])
                S.op(eng, lambda e: e.tensor_tensor(out=a, in0=xbuf[:, c0:c1, 0:128], in1=w_(0), op=ALU.mult), reads=["xbuf", "cwb"], writes=[ra])
                for tap in range(1, 4):
                    S.op(eng, lambda e, tap=tap: e.tensor_tensor(out=a2, in0=xbuf[:, c0:c1, tap:tap + 128], in1=w_(tap), op=ALU.mult), reads=["xbuf", "cwb", ra], writes=[ra2])
                    S.op(eng, lambda e: e.tensor_tensor(out=a, in0=a, in1=a2, op=ALU.add), reads=[ra, ra2], writes=[ra])
                S.op(eng, lambda e: e.tensor_tensor(out=a, in0=a, in1=cwb[:, 64 + c0:64 + c1].unsqueeze(2).broadcast_to([128, nch, 128]), op=ALU.add), reads=[ra, "cwb"], writes=[ra])
            conv_range("dve", 0, CSPLIT, "D")
            conv_range("pool", CSPLIT, 16, "P")

            yield
            bank, bres = tok_group(0, 512)
            S.op("act", lambda e, bank=bank: e.activation(out=sz[:, 0:512], in_=bank, func=A.Silu), reads=[bres], writes=[RSZ])
            yield
            bank, bres = tok_group(512, 512)
            S.op("act", lambda e, bank=bank: e.activation(out=sz[:, 512:1024], in_=bank, func=A.Silu), reads=[bres], writes=[RSZ])
            yield
            bank, bres = fm_group(5648)
            S.op("act", lambda e, bank=bank: e.activation(out=sgc, in_=bank, func=A.Silu), reads=[bres], writes=["sgc"])
            yield
            bank, bres = tok_group(4624, 512)
            S.op("act", lambda e, bank=bank: e.activation(out=sg_bf, in_=bank, func=A.Silu), reads=[bres], writes=["sg_bf"])
            dma(sg_scr[t * 128:(t + 1) * 128, :], sg_bf, ["sg_bf"], ["sg_scr"], "s_sg")
            yield
            def mmdt(e):
                for k in range(8):
                    ins = e.matmul(ps_cb[:, 0:16], lhsT=hT[:, k, :], rhs=w_in_bf[:, k, 3072:3088], start=(k == 0), stop=(k == 7))
                return ins
            S.op("pe", mmdt, reads=["hT", "w_in"], writes=["ps_cb"])
            S.op("dve", lambda e: e.tensor_tensor(out=dtx, in0=ps_cb[:, 0:16], in1=dtb_bc[:], op=ALU.add), reads=["ps_cb", "dtb_bc"], writes=[RDX])
            S.op("act", lambda e: e.activation(out=dte, in_=dtx, func=A.Exp), reads=[RDX], writes=[RDE])
            S.op("act", lambda e: e.activation(out=dtv, in_=dte, func=A.Ln, bias=1.0, scale=1.0), reads=[RDE], writes=[RDV])
            S.op("dve", lambda e: e.tensor_tensor(out=adt, in0=dtv, in1=a_bc[:], op=ALU.mult), reads=[RDV, "a_bc"], writes=[RAD])

            yield
            bank, bres = fm_group(5136)
            S.op("act", lambda e, bank=bank: e.copy(out=qcT[:], in_=bank.rearrange("p (h t) -> p h t", h=4)), reads=[bres], writes=["qcT"])
            yield
            bank, bres = tok_group(3088, 512)
            S.op("act", lambda e, bank=bank: e.copy(out=qf, in_=bank), reads=[bres], writes=["qf"])
            yield
            bank, bres = tok_group(3600, 512)
            S.op("act", lambda e, bank=bank: e.copy(out=kf, in_=bank), reads=[bres], writes=["kf"])
            yield
            bank, bres = tok_group(4112, 512)
            S.op("act", lambda e, bank=bank: e.copy(out=vf, in_=bank), reads=[bres], writes=["vf"])
            yield
            do_rotary(qf, "qf", t, 128, rt)
            do_rotary(kf, "kf", t, 128, rt)
            S.op("pool", lambda e: e.tensor_copy(out=q_bf, in_=qf), reads=["qf"], writes=["q_bf"])
            S.op("pool", lambda e: e.tensor_copy(out=k_bf, in_=kf), reads=["kf"], writes=["k_bf"])
            S.op("pool", lambda e: e.tensor_copy(out=v_aug[:, :, 0:64], in_=vf.rearrange("p (h c) -> p h c", h=8)), reads=["vf"], writes=["v_aug"])
            if t >= 16:
                dma(wk[(t - 16) * 128:(t - 15) * 128, :], kf, ["kf"], ["wk"], "o_wk")
                dma(wv[(t - 16) * 128:(t - 15) * 128, :], vf, ["vf"], ["wv"], "o_wv")
            dma(v_scr[t * 128:(t + 1) * 128, :], v_aug[:].rearrange("p h c -> p (h c)"), ["v_aug"], ["v_scr"], "s_v")

            yield
            def trqk(e):
                for c in range(4):
                    e.transpose(out=ps_tr[:, c * 128:(c + 1) * 128], in_=q_bf[:, c * 128:(c + 1) * 128], identity=identb[:])
                for c in range(4):
                    ins = e.transpose(out=ps_tr[:, (4 + c) * 128:(5 + c) * 128], in_=k_bf[:, c * 128:(c + 1) * 128], identity=identb[:])
                return ins
            S.op("pe", trqk, reads=["q_bf", "k_bf", "identb"], writes=["ps_tr"])
            S.op("act", lambda e: e.copy(out=qkT_sb[:], in_=ps_tr[:].rearrange("p (c t) -> p c t", c=8)), reads=["ps_tr"], writes=["catT_sb"])
            dma(qkT_scr[:, :, t * 128:(t + 1) * 128], qkT_sb[:], ["catT_sb"], ["qkT_scr"], "s_cat")

            yield

        def pre_sample():
            rms_to_hT(xt[0][0:NB, :], "xt", ln_bc, "ln_bc", NB, h_bf, "h_bf", h_bf, hT, "hT")
            yield
            stg = acc2[0:NB, 0:4, :].rearrange("p c t -> p (c t)")
            for gi in range(13):
                c0 = gi * 512
                ncol = min(512, NIN - c0)
                bank, bres = tok_group(c0, ncol, rows=NB)
                S.op("act", lambda e, bank=bank, ncol=ncol: e.copy(out=stg[:, 0:ncol], in_=bank[0:NB, 0:ncol]), reads=[bres], writes=["acc2D"])
                dma(us_scr[:, c0:c0 + ncol], stg[:, 0:ncol], ["acc2D"], ["us_scr"], "s_us")
                yield

        def cross(t):

            def mmcs_(e):
                for h in range(4):
                    for mt in range(2):
                        ins = e.matmul(ps_big[:, (h * 2 + mt) * 128:(h * 2 + mt + 1) * 128], lhsT=MKT[:, h, mt * 128:(mt + 1) * 128], rhs=qcT[:, h, :], start=True, stop=True)
                return ins
            S.op("pe", mmcs_, reads=["MKT", "qcT"], writes=["ps_big"])
            S.op("act", lambda e: e.activation(out=PT_bf[:], in_=ps_big.rearrange("p (h m t) -> p h m t", h=4, m=2), func=A.Exp, scale=128 ** -0.5),
                 reads=["ps_big"], writes=["E_bf"])

            def mmcv(e):
                for h in range(4):
                    for mt in range(2):
                        e.matmul(ps_y[:, h * 128:(h + 1) * 128], lhsT=MV[:, mt, h * 128:(h + 1) * 128], rhs=PT_bf[:, h, mt, :], start=(mt == 0), stop=(mt == 1))
                for mt in range(2):
                    ins = e.matmul(ps_y[:, 512:1024], lhsT=onesb[:], rhs=PT_bf[:, :, mt, :], start=(mt == 0), stop=(mt == 1))
                return ins
            S.op("pe", mmcv, reads=["MV", "E_bf", "onesb"], writes=["ps_y"])
            S.op("act", lambda e: e.activation(out=rden, in_=ps_y[:, 512:1024], func=A.Ln), reads=["ps_y"], writes=["rden"])
            S.op("act", lambda e: e.activation(out=rden, in_=rden, func=A.Exp, scale=-1.0), reads=["rden"], writes=["rden"])
            S.op("dve", lambda e: e.tensor_tensor(out=ocr, in0=ps_y[:, 0:512], in1=rden, op=ALU.mult), reads=["ps_y", "rden"], writes=["rden"])
            S.op("dve", lambda e: e.tensor_tensor(out=catTc[:].rearrange("p h t -> p (h t)"), in0=ocr, in1=sgc, op=ALU.mult), reads=["rden", "sgc"], writes=["catT_sb"])
            dma(cat_scr[t, :, 8:12, :], catTc, ["catT_sb"], ["cat_scr"], "s_cat")


        def ssd(t):
            b = t % 2
            sz = szs[b]; RSZ = "sz%d" % b
            dtx, dte, dtv, adt = dtbufs[b]
            RDV = "dtv%d" % b; RAD = "adt%d" % b
            S.op("act", lambda e: e.activation(out=xc_bf[:], in_=acc[:], func=A.Silu), reads=["accD", "accP"], writes=["xc_bf"])
            S.op("pool", lambda e: e.tensor_copy(out=xbuf[:, :, 0:3], in_=xbuf[:, :, 128:131]), reads=["xbuf"], writes=["xbuf"])
            S.op("dve", lambda e: e.tensor_tensor(out=rhsd[:], in0=tri[:].unsqueeze(1).broadcast_to([128, 16, 128]), in1=b3(adt, 16, 128), op=ALU.mult),
                 reads=["tri", RAD, "xc_bf"], writes=["accD", "accP"])
            def trx(e):
                for c in range(8):
                    ins = e.transpose(out=ps_tr[:, c * 128:(c + 1) * 128], in_=xc_bf[:, c, :], identity=identb[:])
                return ins
            S.op("pe", trx, reads=["xc_bf", "identb"], writes=["ps_tr"])
            S.op("act", lambda e: e.copy(out=xtok_bf, in_=ps_tr[:]), reads=["ps_tr"], writes=["xtok_bf"])
            S.op("pool", lambda e: e.tensor_tensor(out=yv.rearrange("p (h q) -> p h q", h=16), in0=xtok_bf.rearrange("p (h q) -> p h q", h=16), in1=b3(dsk_bc[:], 16, 64), op=ALU.mult),
                 reads=["xtok_bf", "dsk_bc"], writes=["yv"])
            S.op("dve", lambda e: e.tensor_tensor(out=xdt_bf.rearrange("p (h q) -> p h q", h=16), in0=xtok_bf.rearrange("p (h q) -> p h q", h=16), in1=b3(dtv, 16, 64), op=ALU.mult),
                 reads=["xtok_bf", RDV], writes=["xdt_bf"])

            def mmcs(e):
                e.matmul(ps_cb[:, 0:16], lhsT=tri[:], rhs=adt, start=True, stop=True)
                return e.matmul(ps_cb[:, 16:32], lhsT=onesf[:], rhs=adt, start=True, stop=True)
            S.op("pe", mmcs, reads=["tri", "onesf", RAD], writes=["ps_cb"])
            S.op("act", lambda e: e.copy(out=csb[:, 0:32], in_=ps_cb[:, 0:32]), reads=["ps_cb"], writes=["csb"])
            for half in range(2):
                pd = ps_big if half == 0 else ps_y
                pdn = "ps_big" if half == 0 else "ps_y"

                def mmd(e, half=half, pd=pd):
                    for j in range(2):
                        h0 = half * 8 + j * 4
                        ins = e.matmul(pd[:, j * 512:(j + 1) * 512], lhsT=ustr[:], rhs=rhsd[:, h0:h0 + 4, :], start=True, stop=True)
                    return ins
                S.op("pe", mmd, reads=["ustr", "accD", "accP"], writes=[pdn])
                S.op("act", lambda e, half=half, pd=pd: e.activation(out=E_bf[:, half * 8:(half + 1) * 8, :], in_=pd.rearrange("p (h l) -> p h l", h=8), func=A.Exp),
                     reads=[pdn], writes=["E_bf"])
            yield
            S.op("dve", lambda e: e.tensor_tensor(out=csb[:, 32:48], in0=csb[:, 16:32], in1=csb[:, 0:16], op=ALU.subtract), reads=["csb"], writes=["csb"])
            S.op("act", lambda e: e.activation(out=exs[:], in_=csb[:], func=A.Exp), reads=["csb"], writes=["exs"])
            ecs = exs[:, 0:16]; dec = exs[:, 16:32]; dsv = exs[:, 32:48]
            S.op("pool", lambda e: e.tensor_tensor(out=state.rearrange("p (h q) -> p h q", h=16), in0=state.rearrange("p (h q) -> p h q", h=16), in1=b3(dec, 16, 64), op=ALU.mult),
                 reads=["state", "exs"], writes=["state"])
            S.op("dve", lambda e: e.tensor_tensor(out=xds_bf.rearrange("p (h q) -> p h q", h=16), in0=xdt_bf.rearrange("p (h q) -> p h q", h=16), in1=b3(dsv, 16, 64), op=ALU.mult),
                 reads=["xdt_bf", "exs"], writes=["xtok_bf"])
            yield

            def trb(e):
                for g in range(4):
                    ins = e.transpose(out=ps_tr[:, g * 128:(g + 1) * 128], in_=xc_bf[:, 8 + g, :], identity=identb[:])
                return ins
            S.op("pe", trb, reads=["xc_bf", "identb"], writes=["ps_tr"])
            S.op("act", lambda e: e.copy(out=Btok_bf[:], in_=ps_tr[:, 0:512].rearrange("p (g n) -> p g n", g=4)), reads=["ps_tr"], writes=["Btok_bf"])

            def mmcb(e):
                for g in range(4):
                    ins = e.matmul(ps_cb[:, g * 128:(g + 1) * 128], lhsT=xc_bf[:, 8 + g, :], rhs=xc_bf[:, 12 + g, :], start=True, stop=True)
                return ins
            S.op("pe", mmcb, reads=["xc_bf"], writes=["ps_cb"])
            S.op("dve", lambda e: e.tensor_tensor(out=CBm[:], in0=ps_cb[:].rearrange("p (g l) -> p g l", g=4), in1=tri[:].unsqueeze(1).broadcast_to([128, 4, 128]), op=ALU.mult),
                 reads=["ps_cb", "tri"], writes=["CBm"])
            yield
            S.op("dve", lambda e: e.tensor_tensor(out=MT_bf[:].rearrange("p (g r) l -> p g r l", g=4), in0=E_bf[:].rearrange("p (g r) l -> p g r l", g=4),
                                                  in1=CBm[:].unsqueeze(2).broadcast_to([128, 4, 4, 128]), op=ALU.mult),
                 reads=["E_bf", "CBm"], writes=["E_bf"])
            yield

            def mmyd(e):
                for h in range(16):
                    ins = e.matmul(ps_y[:, h * 64:(h + 1) * 64], lhsT=MT_bf[:, h, :], rhs=xdt_bf[:, h * 64:(h + 1) * 64], start=True, stop=True)
                return ins
            S.op("pe", mmyd, reads=["E_bf", "xdt_bf"], writes=["ps_y"])

            def mmyo(e):
                for g in range(4):
                    ins = e.matmul(ps_big[:, g * 256:(g + 1) * 256], lhsT=xc_bf[:, 12 + g, :], rhs=state_bf[:, g * 256:(g + 1) * 256], start=True, stop=True)
                return ins
            S.op("pe", mmyo, reads=["xc_bf", "state_bf"], writes=["ps_big"])
            S.op("dve", lambda e: e.tensor_tensor(out=t1.rearrange("p (h q) -> p h q", h=16), in0=ps_big.rearrange("p (h q) -> p h q", h=16), in1=b3(ecs, 16, 64), op=ALU.mult),
                 reads=["ps_big", "exs"], writes=["t1"])
            S.op("dve", lambda e: e.tensor_tensor(out=yv, in0=yv, in1=t1, op=ALU.add), reads=["yv", "t1"], writes=["yv"])
            S.op("dve", lambda e: e.tensor_tensor(out=yv, in0=ps_y, in1=yv, op=ALU.add), reads=["ps_y", "yv"], writes=["yv"])
            S.op("dve", lambda e: e.tensor_tensor(out=yv, in0=yv, in1=sz, op=ALU.mult), reads=["yv", RSZ], writes=["yv"])
            yield
            def mmst(e):
                for g in range(4):
                    ins = e.matmul(ps_y[:, g * 256:(g + 1) * 256], lhsT=Btok_bf[:, g, :], rhs=xds_bf[:, g * 256:(g + 1) * 256], start=True, stop=True)
                return ins
            S.op("pe", mmst, reads=["Btok_bf", "xtok_bf"], writes=["ps_y"])
            S.op("dve", lambda e: e.tensor_tensor(out=state, in0=ps_y, in1=state, op=ALU.add), reads=["ps_y", "state"], writes=["state"])
            S.op("act", lambda e: e.copy(out=state_bf, in_=state), reads=["state"], writes=["state_bf"])
            yield

            for g in range(4):
                S.op("act", lambda e, g=g: e.activation(out=junk[:, 0:256], in_=yv[:, g * 256:(g + 1) * 256], func=A.Square, accum_out=gss[:, g:g + 1]),
                     reads=["yv"], writes=["t1", "gss"])
            S.op("dve", lambda e: e.tensor_scalar(out=gvar, in0=gss, scalar1=1.0 / 256, scalar2=EPS, op0=ALU.mult, op1=ALU.add), reads=["gss"], writes=["gvar"])
            S.op("act", lambda e: e.activation(out=gvar, in_=gvar, func=A.Ln), reads=["gvar"], writes=["gvar"])
            S.op("act", lambda e: e.activation(out=grstd, in_=gvar, func=A.Exp, scale=-0.5), reads=["gvar"], writes=["grstd"])
            S.op("dve", lambda e: e.tensor_tensor(out=yv.rearrange("p (g c) -> p g c", g=4), in0=yv.rearrange("p (g c) -> p g c", g=4), in1=b3(grstd, 4, 256), op=ALU.mult),
                 reads=["yv", "grstd"], writes=["yv"])
            S.op("dve", lambda e: e.tensor_tensor(out=yn_bf, in0=yv, in1=sng_bc[:], op=ALU.mult), reads=["yv", "sng_bc"], writes=["xtok_bf"])
            yield

            def try_(e):
                for c in range(8):
                    ins = e.transpose(out=ps_tr[:, c * 128:(c + 1) * 128], in_=yn_bf[:, c * 128:(c + 1) * 128], identity=identb[:])
                return ins
            S.op("pe", try_, reads=["xtok_bf", "identb"], writes=["ps_tr"])
            S.op("act", lambda e: e.copy(out=catT_sb[:], in_=ps_tr[:].rearrange("p (c t) -> p c t", c=8)), reads=["ps_tr"], writes=["catT_sb"])
            dma(cat_scr[t, :, 0:8, :], catT_sb[:], ["catT_sb"], ["cat_scr"], "s_cat")

            yield

        def drain(g):
            for _ in g:
                pass

        def interleave(g1, g2):
            live = [g1, g2]
            while live:
                for g in list(live):
                    try:
                        next(g)
                    except StopIteration:
                        live.remove(g)

        load_x(0)
        drain(pre(0))
        cross(0)
        for t in range(NT):
            g1 = ssd(t)
            next(g1)
            interleave(g1, pre(t + 1) if t + 1 < NT else pre_sample())
            if t + 1 < NT:
                cross(t + 1)

        stT = yv

        def trst(e):
            for c in range(8):
                ins = e.matmul(ps_big[:, c * 128:(c + 1) * 128], lhsT=state[:, c * 128:(c + 1) * 128], rhs=identf[:], start=True, stop=True)
            return ins
        S.op("pe", trst, reads=["state", "identf"], writes=["ps_big"])
        S.op("act", lambda e: e.copy(out=stT, in_=ps_big), reads=["ps_big"], writes=["yv"])
        dma(ssmp.rearrange("(c p) n -> p c n", p=128), stT.rearrange("p (c n) -> p c n", c=8), ["yv"], ["ssmp"], "o_ssm")

        if UPTO >= 'S':
            S.full_barrier()
            off[0] = 0
            a1f = arena1[:].bitcast(F32)
            usb = carve(NIN)
            cat_s = carve(2048)
            dz0 = off[0]
            acc_s = carve(2048); tmp_s = carve(2048); xc_s = carve(2048)
            pk = carve(1024)
            y_s = carve(1024); szs = carve(1024)
            prod = arena2[:, dz0:dz0 + 8192]
            dss = carve(64)
            rt_s = carve(256).rearrange("p (a h i) -> p a h i", a=4, h=8)
            xdt_q = carve(128); B_q = carve(128); C_q = carve(128); dA_q = carve(2); y_q = carve(128)
            h0s = [a1f[:, 0:4096], a1f[:, 4096:8192]]
            tq = a1f[:, 8192:12288]
            o_tok = carve(512); sga = carve(512)
            numa = carve(512); numc = carve(512)
            sm = carve(32)
            snew = sm[0:NB, 0:8]; pnew = sm[0:NB, 8:16]; dena = sm[0:NB, 16:24]; denc = sm[0:NB, 24:28]
            sj = carve(128); pj = carve(128); pj_bf = carve_bf(128)
            esel_f = carve(256); Esel = carve_bf(256).rearrange("p (c m) -> p c m", c=16)
            print("arena2 phase S words:", off[0])
            R = NB
            dma(usb[0:R, :], us_scr, ["us_scr"], ["usb"], "l_s")
            cbuf = a1f[0:R, 0:8192].rearrange("p (t c) -> p t c", t=4)
            cw16 = a1f[0:R, 8192:16384].rearrange("p (t c) -> p t c", t=4)
            cb16 = a1f[0:R, 16384:18432]
            dma(cbuf[:, 0:3, :], sconv, [], ["cbuf"], "l_s")
            dma(cbuf[:, 3, :], us_scr[:, 1024:3072], ["us_scr"], ["cbuf"], "l_s")
            for tap in range(4):
                dma(cw16[:, tap, :], conv_w[tap:tap + 1, :].partition_broadcast(R), [], ["cw16"], "l_s")
            dma(cb16, conv_b.partition_broadcast(R), [], ["cb16"], "l_s")
            dma(convs, cbuf[:, 1:4, :], ["cbuf"], ["convs"], "o_cvs")
            a_s = acc_s[0:R, :]; t_s = tmp_s[0:R, :]
            S.op("dve", lambda e: e.tensor_tensor(out=a_s, in0=cbuf[:, 0, :], in1=cw16[:, 0, :], op=ALU.mult), reads=["cbuf", "cw16"], writes=["acc_s"])
            for tap in range(1, 4):
                S.op("pool", lambda e, tap=tap: e.tensor_tensor(out=t_s, in0=cbuf[:, tap, :], in1=cw16[:, tap, :], op=ALU.mult), reads=["cbuf", "cw16", "acc_s"], writes=["tmp_s"])
                S.op("dve", lambda e: e.tensor_tensor(out=a_s, in0=a_s, in1=t_s, op=ALU.add), reads=["acc_s", "tmp_s"], writes=["acc_s"])
            S.op("dve", lambda e: e.tensor_tensor(out=a_s, in0=a_s, in1=cb16, op=ALU.add), reads=["acc_s", "cb16"], writes=["acc_s"])
            S.op("act", lambda e: e.activation(out=xc_s[0:R, :], in_=a_s, func=A.Silu), reads=["acc_s"], writes=["xc_s"])
            dtx_s = dss[0:R, 0:16]; dte_s = dss[0:R, 16:32]; dt_s = dss[0:R, 32:48]; dA_s = dss[0:R, 48:64]
            S.op("dve", lambda e: e.tensor_tensor(out=dtx_s, in0=usb[0:R, 3072:3088], in1=dtb_bc[0:R, :], op=ALU.add), reads=["usb", "dtb_bc"], writes=["dtx_s"])
            S.op("act", lambda e: e.activation(out=dte_s, in_=dtx_s, func=A.Exp), reads=["dtx_s"], writes=["dte_s"])
            S.op("act", lambda e: e.activation(out=dt_s, in_=dte_s, func=A.Ln, bias=1.0, scale=1.0), reads=["dte_s"], writes=["dt_s"])
            S.op("dve", lambda e: e.tensor_tensor(out=dA_s, in0=dt_s, in1=a_bc[0:R, :], op=ALU.mult), reads=["dt_s", "a_bc"], writes=["dA_s"])
            S.op("act", lambda e: e.activation(out=dA_s, in_=dA_s, func=A.Exp), reads=["dA_s"], writes=["dA_s"])
            S.op("dve", lambda e: e.tensor_tensor(out=pk[0:R, :].rearrange("p (h q) -> p h q", h=16), in0=xc_s[0:R, 0:1024].rearrange("p (h q) -> p h q", h=16),
                                                  in1=dt_s.unsqueeze(2).broadcast_to([R, 16, 64]), op=ALU.mult), reads=["xc_s", "dt_s"], writes=["pk"])
            dma(sx_scr, pk[0:R, :], ["pk"], ["sx_scr"], "s_s")
            S.op("pool", lambda e: e.tensor_copy(out=pk[0:R, :].rearrange("p (g d n) -> p g d n", g=4, d=2),
                                                 in_=xc_s[0:R, 1024:1536].rearrange("p (g n) -> p g n", g=4).unsqueeze(2).broadcast_to([R, 4, 2, 128])), reads=["xc_s", "pk"], writes=["pk"])
            dma(sB_scr, pk[0:R, :], ["pk"], ["sB_scr"], "s_s")
            S.op("pool", lambda e: e.tensor_copy(out=pk[0:R, :].rearrange("p (g d n) -> p g d n", g=4, d=2),
                                                 in_=xc_s[0:R, 1536:2048].rearrange("p (g n) -> p g n", g=4).unsqueeze(2).broadcast_to([R, 4, 2, 128])), reads=["xc_s", "pk"], writes=["pk"])
            dma(sC_scr, pk[0:R, :], ["pk"], ["sC_scr"], "s_s")
            dma(sdA_scr, dA_s, ["dA_s"], ["sdA_scr"], "s_s")
            dma(xdt_q, sx_scr.rearrange("b (j f) -> (b j) f", j=8), ["sx_scr"], ["xdt_q"], "l_s2")
            dma(B_q, sB_scr.rearrange("b (j f) -> (b j) f", j=8), ["sB_scr"], ["B_q"], "l_s2")
            dma(C_q, sC_scr.rearrange("b (j f) -> (b j) f", j=8), ["sC_scr"], ["C_q"], "l_s2")
            dma(dA_q, sdA_scr.rearrange("b (j f) -> (b j) f", j=8), ["sdA_scr"], ["dA_q"], "l_s2")
            S.full_barrier()
            sssm3 = sssm.rearrange("(q f) n -> q f n", f=128)
            ssms3 = ssms.rearrange("(q f) n -> q f n", f=128)
            for pc in range(4):
                f0 = pc * 32
                hb = h0s[pc % 2].rearrange("p (f n) -> p f n", f=32); hres = "h0_%d" % (pc % 2)
                tq3 = tq.rearrange("p (f n) -> p f n", f=32)
                dma(hb, sssm3[:, f0:f0 + 32, :], [], [hres], "l_h%d" % (pc % 2))
                S.op("dve", lambda e, f0=f0: e.tensor_tensor(out=tq3, in0=xdt_q[:, f0:f0 + 32].unsqueeze(2).broadcast_to([128, 32, 128]),
                                                              in1=B_q.unsqueeze(1).broadcast_to([128, 32, 128]), op=ALU.mult), reads=["xdt_q", "B_q"], writes=["tq"])
                S.op("dve", lambda e, hb=hb, pc=pc: e.scalar_tensor_tensor(out=hb, in0=hb, scalar=dA_q[:, pc // 2:pc // 2 + 1], in1=tq3, op0=ALU.mult, op1=ALU.add),
                     reads=[hres, "dA_q", "tq"], writes=[hres])
                dma(ssms3[:, f0:f0 + 32, :], hb, [hres], ["ssms"], "o_h%d" % (pc % 2))
                S.op("pool", lambda e, hb=hb: e.tensor_tensor(out=tq3, in0=hb, in1=C_q.unsqueeze(1).broadcast_to([128, 32, 128]), op=ALU.mult), reads=[hres, "C_q", "tq"], writes=["tq"])
                S.op("dve", lambda e, f0=f0: e.reduce_sum(out=y_q[:, f0:f0 + 32], in_=tq3, axis=AX.X), reads=["tq"], writes=["y_q"])
            dma(sx_scr.rearrange("b (j f) -> (b j) f", j=8), y_q, ["y_q", "xdt_q"], ["sx_scr"], "s_s")
            dma(y_s[0:R, :], sx_scr, ["sx_scr"], ["y_s"], "l_s2")
            S.op("pool", lambda e: e.tensor_tensor(out=pk[0:R, :].rearrange("p (h q) -> p h q", h=16), in0=xc_s[0:R, 0:1024].rearrange("p (h q) -> p h q", h=16),
                                                   in1=dsk_bc[0:R, :].unsqueeze(2).broadcast_to([R, 16, 64]), op=ALU.mult), reads=["xc_s", "dsk_bc", "pk"], writes=["pk"])
            S.op("dve", lambda e: e.tensor_tensor(out=y_s[0:R, :], in0=y_s[0:R, :], in1=pk[0:R, :], op=ALU.add), reads=["y_s", "pk"], writes=["y_s"])
            S.op("act", lambda e: e.activation(out=szs[0:R, :], in_=usb[0:R, 0:1024], func=A.Silu), reads=["usb"], writes=["szs"])
            S.op("dve", lambda e: e.tensor_tensor(out=y_s[0:R, :], in0=y_s[0:R, :], in1=szs[0:R, :], op=ALU.mult), reads=["y_s", "szs"], writes=["y_s"])
            for g in range(4):
                S.op("act", lambda e, g=g: e.activation(out=pk[0:R, 0:256], in_=y_s[0:R, g * 256:(g + 1) * 256], func=A.Square, accum_out=gss[0:R, g:g + 1]),
                     reads=["y_s", "pk"], writes=["pk", "gss"])
            S.op("dve", lambda e: e.tensor_scalar(out=gvar[0:R, :], in0=gss[0:R, :], scalar1=1.0 / 256, scalar2=EPS, op0=ALU.mult, op1=ALU.add), reads=["gss"], writes=["gvar"])
            S.op("act", lambda e: e.activation(out=gvar[0:R, :], in_=gvar[0:R, :], func=A.Sqrt), reads=["gvar"], writes=["gvar"])
            S.op("dve", lambda e: e.reciprocal(out=grstd[0:R, :], in_=gvar[0:R, :]), reads=["gvar"], writes=["grstd"])
            S.op("dve", lambda e: e.tensor_tensor(out=y_s[0:R, :].rearrange("p (g c) -> p g c", g=4), in0=y_s[0:R, :].rearrange("p (g c) -> p g c", g=4),
                                                  in1=grstd[0:R, :].unsqueeze(2).broadcast_to([R, 4, 256]), op=ALU.mult), reads=["y_s", "grstd"], writes=["y_s"])
            S.op("dve", lambda e: e.tensor_tensor(out=cat_s[0:R, 0:1024], in0=y_s[0:R, :], in1=sng_bc[0:R, :], op=ALU.mult), reads=["y_s", "sng_bc"], writes=["cat_s"])

            S.full_barrier()
            qs = usb[0:R, 3088:3600]; ks = usb[0:R, 3600:4112]; vs = usb[0:R, 4112:4624]
            do_rotary(usb[:, 3088:3600], "usb", NT, R, rt_s)
            do_rotary(usb[:, 3600:4112], "usb", NT, R, rt_s)
            dma(wks, ks, ["usb"], ["wks"], "o_wks")
            dma(wvs, vs, ["usb"], ["wvs"], "o_wks")
            dma(sq_scr, qs, ["usb"], ["sq_scr"], "s_s")
            dma(sk_scr, usb[0:R, 5136:5648], ["usb"], ["sk_scr"], "s_s")
            Kj = a1f[:, 0:8192]; Vj = a1f[:, 8192:16384]; qbc = a1f[:, 16384:24576]
            prod2A = prod[:, 0:2816].bitcast(BF16)
            prod2B = prod[:, 5632:6912].bitcast(BF16)

            def p2(b_):
                return prod2A[:, b_ * 512:(b_ + 1) * 512] if b_ < 11 else prod2B[:, (b_ - 11) * 512:(b_ - 10) * 512]
            dma(esel_f, c_esel.partition_broadcast(128), [], ["esel_f"], "l_s2")
            S.op("dve", lambda e: e.tensor_copy(out=Esel.rearrange("p c m -> p (c m)"), in_=esel_f), reads=["esel_f"], writes=["Esel"])
            S.op("dve", lambda e: e.tensor_tensor(out=o_tok[0:R, :], in0=qs, in1=ks, op=ALU.mult), reads=["usb"], writes=["o_tok"])
            S.op("dve", lambda e: e.reduce_sum(out=snew, in_=o_tok[0:R, :].rearrange("p (h e) -> p h e", h=8), axis=AX.X), reads=["o_tok"], writes=["snew"])
            S.op("act", lambda e: e.activation(out=pnew, in_=snew, func=A.Exp, scale=0.125), reads=["snew"], writes=["pnew"])
            S.op("dve", lambda e: e.tensor_scalar(out=pnew, in0=pnew, scalar1=3.0, scalar2=None, op0=ALU.mult), reads=["pnew"], writes=["pnew"])
            S.op("dve", lambda e: e.tensor_tensor(out=numa[0:R, :].rearrange("p (h e) -> p h e", h=8), in0=vs.rearrange("p (h e) -> p h e", h=8),
                                                  in1=pnew.unsqueeze(2).broadcast_to([R, 8, 64]), op=ALU.mult), reads=["usb", "pnew"], writes=["numa"])
            S.op("dve", lambda e: e.tensor_copy(out=dena, in_=pnew), reads=["pnew"], writes=["dena"])

            for pi, d in enumerate(DILS):
                rs = slice(2048 - 128 * d, 2048 - d + 1, d)
                if pi == 0:
                    dma(qbc, sq_scr.rearrange("(o b) c -> o (b c)", o=1).partition_broadcast(128), ["sq_scr"], ["qbc"], "l_q")
                dma(Kj.rearrange("p (b c) -> p b c", b=R), cwk[:, rs, :].rearrange("b j c -> j b c"), [], ["KjA", "KjB"], "l_kg")
                dma(Vj.rearrange("p (b c) -> p b c", b=R), cwv[:, rs, :].rearrange("b j c -> j b c"), [], ["Vj"], "l_vg")
                S.op("dve", lambda e: e.tensor_tensor(out=prod[:, 0:5632], in0=Kj[:, 0:5632], in1=qbc[:, 0:5632], op=ALU.mult), reads=["KjA", "qbc"], writes=["prodA"])
                S.op("pool", lambda e: e.tensor_tensor(out=prod[:, 5632:8192], in0=Kj[:, 5632:8192], in1=qbc[:, 5632:8192], op=ALU.mult), reads=["KjB", "qbc"], writes=["prodB"])
                S.op("dve", lambda e: e.reduce_sum(out=sj, in_=prod.rearrange("p (g e) -> p g e", e=64), axis=AX.X), reads=["prodA", "prodB"], writes=["sj"])
                S.op("act", lambda e: e.activation(out=pj, in_=sj, func=A.Exp, scale=0.125), reads=["sj"], writes=["pj"])
                S.op("act", lambda e: e.copy(out=pj_bf, in_=pj), reads=["pj"], writes=["pj_bf"])
                S.op("dve", lambda e: e.tensor_tensor(out=prod2A.rearrange("p (g e) -> p g e", e=64), in0=Vj[:, 0:5632].rearrange("p (g e) -> p g e", e=64),
                                                      in1=pj[:, 0:88].unsqueeze(2).broadcast_to([128, 88, 64]), op=ALU.mult), reads=["Vj", "pj", "sj"], writes=["prodA"])
                S.op("pool", lambda e: e.tensor_tensor(out=prod2B.rearrange("p (g e) -> p g e", e=64), in0=Vj[:, 5632:8192].rearrange("p (g e) -> p g e", e=64),
                                                       in1=pj[:, 88:128].unsqueeze(2).broadcast_to([128, 40, 64]), op=ALU.mult), reads=["Vj", "pj", "sj"], writes=["prodB"])

                def mmn(e):
                    for b_ in range(R):
                        ins = e.matmul(ps_in[0:R, 0:512], lhsT=Esel[:, b_, :], rhs=p2(b_), start=(b_ == 0), stop=(b_ == R - 1))
                    return ins
                S.op("pe", mmn, reads=["prodA", "prodB", "Esel"], writes=["ps_in0"])

                def mmd_(e):
                    for b_ in range(R):
                        ins = e.matmul(ps_cb[0:R, 0:8], lhsT=Esel[:, b_, :], rhs=pj_bf[:, b_ * 8:(b_ + 1) * 8], start=(b_ == 0), stop=(b_ == R - 1))
                    return ins
                S.op("pe", mmd_, reads=["pj_bf", "Esel"], writes=["ps_cb"])
                S.op("dve", lambda e: e.tensor_tensor(out=numa[0:R, :], in0=ps_in[0:R, 0:512], in1=numa[0:R, :], op=ALU.add), reads=["ps_in0", "numa"], writes=["numa"])
                S.op("dve", lambda e: e.tensor_tensor(out=dena, in0=ps_cb[0:R, 0:8], in1=dena, op=ALU.add), reads=["ps_cb", "dena"], writes=["dena"])
            S.op("dve", lambda e: e.reciprocal(out=dena, in_=dena), reads=["dena"], writes=["dena"])
            S.op("dve", lambda e: e.tensor_tensor(out=numa[0:R, :].rearrange("p (h e) -> p h e", h=8), in0=numa[0:R, :].rearrange("p (h e) -> p h e", h=8),
                                                  in1=dena.unsqueeze(2).broadcast_to([R, 8, 64]), op=ALU.mult), reads=["numa", "dena"], writes=["numa"])
            S.op("act", lambda e: e.activation(out=sga[0:R, :], in_=usb[0:R, 4624:5136], func=A.Silu), reads=["usb"], writes=["sga"])
            S.op("dve", lambda e: e.tensor_tensor(out=cat_s[0:R, 1024:1536], in0=numa[0:R, :], in1=sga[0:R, :], op=ALU.mult), reads=["numa", "sga"], writes=["cat_s"])

            dma(qbc, sk_scr.rearrange("(o b) c -> o (b c)", o=1).partition_broadcast(128), ["sk_scr", "prodA", "prodB"], ["qbc"], "l_q")
            for mt in range(2):
                dma(Kj.rearrange("p (b c) -> p b c", b=R), cmk[:, mt * 128:(mt + 1) * 128, :].rearrange("b m c -> m b c"), [], ["KjA", "KjB"], "l_kg")
                dma(Vj.rearrange("p (b c) -> p b c", b=R), cmv[:, mt * 128:(mt + 1) * 128, :].rearrange("b m c -> m b c"), [], ["Vj"], "l_vg")
                S.op("dve", lambda e: e.tensor_tensor(out=prod[:, 0:5632], in0=Kj[:, 0:5632], in1=qbc[:, 0:5632], op=ALU.mult), reads=["KjA", "qbc"], writes=["prodA"])
                S.op("pool", lambda e: e.tensor_tensor(out=prod[:, 5632:8192], in0=Kj[:, 5632:8192], in1=qbc[:, 5632:8192], op=ALU.mult), reads=["KjB", "qbc"], writes=["prodB"])
                S.op("dve", lambda e: e.reduce_sum(out=sj[:, 0:64], in_=prod.rearrange("p (g e) -> p g e", e=128), axis=AX.X), reads=["prodA", "prodB"], writes=["sj"])
                S.op("act", lambda e: e.activation(out=pj[:, 0:64], in_=sj[:, 0:64], func=A.Exp, scale=128 ** -0.5), reads=["sj"], writes=["pj"])
                S.op("act", lambda e: e.copy(out=pj_bf[:, 0:64], in_=pj[:, 0:64]), reads=["pj"], writes=["pj_bf"])
                S.op("dve", lambda e: e.tensor_tensor(out=prod2A.rearrange("p (g e) -> p g e", e=128), in0=Vj[:, 0:5632].rearrange("p (g e) -> p g e", e=128),
                                                      in1=pj[:, 0:44].unsqueeze(2).broadcast_to([128, 44, 128]), op=ALU.mult), reads=["Vj", "pj", "sj"], writes=["prodA"])
                S.op("pool", lambda e: e.tensor_tensor(out=prod2B.rearrange("p (g e) -> p g e", e=128), in0=Vj[:, 5632:8192].rearrange("p (g e) -> p g e", e=128),
                                                       in1=pj[:, 44:64].unsqueeze(2).broadcast_to([128, 20, 128]), op=ALU.mult), reads=["Vj", "pj", "sj"], writes=["prodB"])

                def mmnc(e):
                    for b_ in range(R):
                        ins = e.matmul(ps_in[0:R, 512:1024], lhsT=Esel[:, b_, :], rhs=p2(b_), start=(b_ == 0), stop=(b_ == R - 1))
                    return ins
                S.op("pe", mmnc, reads=["prodA", "prodB", "Esel"], writes=["ps_in1"])

                def mmdc(e):
                    for b_ in range(R):
                        ins = e.matmul(ps_cb[0:R, 8:12], lhsT=Esel[:, b_, :], rhs=pj_bf[:, b_ * 4:(b_ + 1) * 4], start=(b_ == 0), stop=(b_ == R - 1))
                    return ins
                S.op("pe", mmdc, reads=["pj_bf", "Esel"], writes=["ps_cb"])
                if mt == 0:
                    S.op("dve", lambda e: e.tensor_copy(out=numc[0:R, :], in_=ps_in[0:R, 512:1024]), reads=["ps_in1"], writes=["numc"])
                    S.op("dve", lambda e: e.tensor_copy(out=denc, in_=ps_cb[0:R, 8:12]), reads=["ps_cb"], writes=["denc"])
                else:
                    S.op("dve", lambda e: e.tensor_tensor(out=numc[0:R, :], in0=ps_in[0:R, 512:1024], in1=numc[0:R, :], op=ALU.add), reads=["ps_in1", "numc"], writes=["numc"])
                    S.op("dve", lambda e: e.tensor_tensor(out=denc, in0=ps_cb[0:R, 8:12], in1=denc, op=ALU.add), reads=["ps_cb", "denc"], writes=["denc"])
            S.op("dve", lambda e: e.reciprocal(out=denc, in_=denc), reads=["denc"], writes=["denc"])
            S.op("dve", lambda e: e.tensor_tensor(out=numc[0:R, :].rearrange("p (h e) -> p h e", h=4), in0=numc[0:R, :].rearrange("p (h e) -> p h e", h=4),
                                                  in1=denc.unsqueeze(2).broadcast_to([R, 4, 128]), op=ALU.mult), reads=["numc", "denc"], writes=["numc"])
            S.op("act", lambda e: e.activation(out=sga[0:R, :], in_=usb[0:R, 5648:6160], func=A.Silu), reads=["usb", "cat_s"], writes=["sga"])
            S.op("dve", lambda e: e.tensor_tensor(out=cat_s[0:R, 1536:2048], in0=numc[0:R, :], in1=sga[0:R, :], op=ALU.mult), reads=["numc", "sga"], writes=["cat_s"])
            dma(cat_s_scr, cat_s[0:R, :], ["cat_s"], ["cat_s_scr"], "s_s")

        if UPTO >= 'B':
            S.full_barrier()
            off[0] = 0
            QKT = carve_bf(8 * SEQ).rearrange("p (c t) -> p c t", c=8)
            P_bfs = [carve_bf(2048).rearrange("p (h j q) -> p h j q", h=8, j=2) for _ in range(2)]
            o_sbs = [carve(520).rearrange("p (h c) -> p h c", h=8) for _ in range(2)]
            for c_ in range(8):
                dma(QKT[:, c_, :], qkT_scr[:, c_, :], ["qkT_scr"], ["QKT"], "l_qk")

            def qsel(pb, c, n_, d_, r_):
                s0 = 128 * d_ * n_ + r_
                return QKT[pb:pb + 64, c, s0:s0 + 127 * d_ + 1:d_]
            for di, d in enumerate(DILS):
                nb = 32 // d
                for r in range(d):
                    src = v_scr[r:SEQ - d + r + 1:d, :].rearrange("(n p) c -> p n c", p=128)
                    dma(V3[:, di, r * nb:(r + 1) * nb, :], src, ["v_scr"], ["V3_%d" % di], "l_v", eng=("sp" if r % 2 == 0 else "act"))
            blk = 0
            for di, d in enumerate(DILS):
                nb = 32 // d
                for r in range(d):
                    for n in range(nb):
                        st = 128 * d * n + r
                        kts = [n - 1, n] if n > 0 else [n]
                        nk = len(kts)
                        P_bf = P_bfs[blk % 2]; pres = "P%d" % (blk % 2)
                        o_sb = o_sbs[blk % 2]; ores = "o%d" % (blk % 2)
                        blk += 1

                        b2 = (blk - 1) % 2
                        ps4v = ps_4.rearrange("p (h j q) -> p h j q", h=8, j=2)
                        for par in range(2):
                            def mms(e, kts=kts, d=d, r=r, n=n, par=par):
                                for c in range(4):
                                    hh = par * 4 + c
                                    for j, kn in enumerate(kts):
                                        ins = e.matmul(ps_4[:, (hh * 2 + j) * 128:(hh * 2 + j + 1) * 128],
                                                       lhsT=qsel(par * 64, 4 + c, kn, d, r),
                                                       rhs=qsel(par * 64, c, n, d, r), start=True, stop=True)
                                return ins
                            S.op("pe", mms, reads=["QKT"], writes=["ps4_%d" % par])
                        for par in range(2):
                            S.op("act", lambda e, P_bf=P_bf, nk=nk, par=par: e.activation(out=P_bf[:, par * 4:(par + 1) * 4, 0:nk, :], in_=ps4v[:, par * 4:(par + 1) * 4, 0:nk, :], func=A.Exp, scale=0.125),
                                 reads=["ps4_%d" % par], writes=["P%d_%d" % (b2, par)])
                        if nk == 2:
                            mk_ = mask2[:].unsqueeze(1).broadcast_to([128, 4, 2, 128])
                        else:
                            mk_ = mask2[:, 1:2, :].unsqueeze(1).broadcast_to([128, 4, 1, 128])
                        for par in range(2):
                            S.op("dve", lambda e, P_bf=P_bf, nk=nk, mk_=mk_, par=par: e.tensor_tensor(out=P_bf[:, par * 4:(par + 1) * 4, 0:nk, :], in0=P_bf[:, par * 4:(par + 1) * 4, 0:nk, :], in1=mk_, op=ALU.mult),
                                 reads=["P%d_%d" % (b2, par), "mask2a", "mask2b"], writes=["P%d_%d" % (b2, par)])
                        for par in range(2):
                            def mmo(e, P_bf=P_bf, kts=kts, di=di, r=r, nb=nb, nk=nk, par=par):
                                for c in range(4):
                                    h = 2 * c + par
                                    for j, kn in enumerate(kts):
                                        ins = e.matmul(ps_in[:, par * 512 + c * 128:par * 512 + c * 128 + 65], lhsT=P_bf[:, par * 4 + c, j, :],
                                                       rhs=V3[:, di, r * nb + kn, h * 66:h * 66 + 65], start=(j == 0), stop=(j == nk - 1))
                                return ins
                            S.op("pe", mmo, reads=["P%d_%d" % (b2, par), "V3_%d" % di], writes=["ps_in%d" % par])
                        o_v = o_sb[:].rearrange("p (c q) k -> p c q k", q=2)
                        for par in range(2):
                            S.op("dve", lambda e, o_v=o_v, par=par: e.tensor_copy(out=o_v[:, :, par, :], in_=ps_in[:, par * 512:(par + 1) * 512].rearrange("p (c k) -> p c k", c=4)[:, :, 0:65]),
                                 reads=["ps_in%d" % par], writes=[ores])
                        dma(o_scr[di, st:st + 127 * d + 1:d, :], o_sb[:].rearrange("p h c -> p (h c)"), [ores], ["o_scr"], "s_o%d" % b2)

        if UPTO >= 'C':
            S.full_barrier()
            off[0] = 0
            xt2 = [carve(1024), carve(1024)]
            o3s = [carve(3 * 520).rearrange("p (d c) -> p d c", d=3) for _ in range(2)]
            sg_ts = [carve_bf(512) for _ in range(2)]
            catTs = [carve_bf(16 * 128).rearrange("p (c t) -> p c t", c=16) for _ in range(2)]
            osums = [carve(520).rearrange("p (h c) -> p h c", h=8) for _ in range(2)]
            rdns = [carve(8) for _ in range(2)]
            oatts = [carve(512) for _ in range(2)]
            oattbfs = [carve_bf(512) for _ in range(2)]
            hps = [carve(1024) for _ in range(2)]
            youts = [carve(1024) for _ in range(2)]
            wst = carve(2048).rearrange("p (s n) -> p s n", s=2)
            cat_s_bf = carve_bf(2048)
            statC = carve(8)
            pso = [ps_in[:, :], ps_4[:, 0:1024]]
            fng_bc = ln_bc
            dma(fng_bc[:], final_norm_g.partition_broadcast(128), [], ["ln_bc"], "c0")
            for j in range(16):
                sidx = j % 2
                dma(wst[:, sidx, :], w_out[j * 128:(j + 1) * 128, :], [], ["wst%d" % sidx], "stg%d" % sidx)
                cast(conv_engs[j % 3], w_out_bf[:, j, :], wst[:, sidx, :], ["wst%d" % sidx], ["w_out"])

            def final_norm_out(b, rows, out_ap, okey):
                hp_ap = hps[b][0:rows, :]; yo = youts[b][0:rows, :]
                sq = statC[0:rows, 3 * b:3 * b + 1]; vr = statC[0:rows, 3 * b + 1:3 * b + 2]; rs_ = statC[0:rows, 3 * b + 2:3 * b + 3]
                S.op("act", lambda e: e.activation(out=yo, in_=hp_ap, func=A.Square, accum_out=sq), reads=["hp%d" % b], writes=["yout%d" % b, "sq%d" % b])
                S.op("dve", lambda e: e.tensor_scalar(out=vr, in0=sq, scalar1=1.0 / D, scalar2=EPS, op0=ALU.mult, op1=ALU.add), reads=["sq%d" % b], writes=["vr%d" % b])
                S.op("act", lambda e: e.activation(out=vr, in_=vr, func=A.Sqrt), reads=["vr%d" % b], writes=["vr%d" % b])
                S.op("dve", lambda e: e.reciprocal(out=rs_, in_=vr), reads=["vr%d" % b], writes=["rs%d" % b])
                S.op("dve", lambda e: e.scalar_tensor_tensor(out=yo, in0=hp_ap, scalar=rs_, in1=fng_bc[0:rows, :], op0=ALU.mult, op1=ALU.mult),
                     reads=["hp%d" % b, "rs%d" % b, "ln_bc", "yout%d" % b], writes=["yout%d" % b])
                dma(out_ap, yo, ["yout%d" % b], [okey], okey + str(b), eng="act_store")

            WMAP = list(range(8)) + [12, 13, 14, 15] + [8, 9, 10, 11]

            def out_proj(b, rows, wmap=tuple(WMAP)):
                def mm(e):
                    for hf in range(2):
                        for j in range(16):
                            ins = e.matmul(pso[b][0:rows, hf * 512:(hf + 1) * 512], lhsT=catTs[b][:, j, 0:rows], rhs=w_out_bf[:, wmap[j], hf * 512:(hf + 1) * 512], start=(j == 0), stop=(j == 15))
                    return ins
                S.op("pe", mm, reads=["catT%d" % b, "w_out"], writes=["po%d" % b])

            def stageP(t):
                b = t % 2
                tsl = slice(t * 128, (t + 1) * 128)
                o3 = o3s[b]; sg_t = sg_ts[b]; catT_t = catTs[b]; osum = osums[b]; rdn = rdns[b]; oatt = oatts[b]; oatt_bf = oattbfs[b]
                dma(xt2[b][:], x[tsl, :], [], ["xt2_%d" % b], "x%d" % b)
                dma(o3[:], o_scr[:, tsl, :].rearrange("d p c -> p d c"), [], ["o3_%d" % b], "l_o3%d" % b)
                dma(sg_t, sg_scr[tsl, :], [], ["sg%d" % b], "l_sg%d" % b, eng="act")
                dma(catT_t[:, 0:12, :], cat_scr[t], [], ["catT%d" % b], "l_cat%d" % b, eng="act")
                o3v = o3.rearrange("p d (h c) -> p d h c", h=8)
                S.op("dve", lambda e: e.tensor_tensor(out=osum[:], in0=o3v[:, 0], in1=o3v[:, 1], op=ALU.add), reads=["o3_%d" % b], writes=["osum%d" % b])
                S.op("dve", lambda e: e.tensor_tensor(out=osum[:], in0=osum[:], in1=o3v[:, 2], op=ALU.add), reads=["o3_%d" % b, "osum%d" % b], writes=["osum%d" % b])
                S.op("dve", lambda e: e.reciprocal(out=rdn, in_=osum[:, :, 64]), reads=["osum%d" % b], writes=["rdn%d" % b])
                S.op("dve", lambda e: e.tensor_tensor(out=oatt.rearrange("p (h c) -> p h c", h=8), in0=osum[:, :, 0:64], in1=b3(rdn, 8, 64), op=ALU.mult), reads=["osum%d" % b, "rdn%d" % b], writes=["oatt%d" % b])
                S.op("dve", lambda e: e.tensor_tensor(out=oatt_bf, in0=oatt, in1=sg_t, op=ALU.mult), reads=["oatt%d" % b, "sg%d" % b], writes=["oattbf%d" % b])

            def stageP2(t):
                b = t % 2
                catT_t = catTs[b]; oatt_bf = oattbfs[b]

                def tra(e):
                    for c in range(4):
                        ins = e.transpose(out=ps_tr[:, b * 512 + c * 128:b * 512 + (c + 1) * 128], in_=oatt_bf[:, c * 128:(c + 1) * 128], identity=identb[:])
                    return ins
                S.op("pe", tra, reads=["oattbf%d" % b, "identb"], writes=["ptr%d" % b])
                S.op("act", lambda e: e.copy(out=catT_t[:, 12:16, :], in_=ps_tr[:, b * 512:(b + 1) * 512].rearrange("p (c t) -> p c t", c=4)), reads=["ptr%d" % b], writes=["catT%d" % b])

            def stageQ1(t):
                out_proj(t % 2, 128)

            def stageQ2(t):
                b = t % 2
                S.op("dve", lambda e: e.tensor_tensor(out=hps[b][:], in0=pso[b], in1=xt2[b][:], op=ALU.add), reads=["po%d" % b, "xt2_%d" % b], writes=["hp%d" % b])
                final_norm_out(b, 128, yp[t * 128:(t + 1) * 128, :], "o_y")

            stageP(0)
            stageP2(0)
            for t in range(NT):
                if t + 1 < NT:
                    stageP(t + 1)
                stageQ1(t)
                if t + 1 < NT:
                    stageP2(t + 1)
                stageQ2(t)

            if UPTO >= 'S':
                for q4 in range(4):
                    dma(oatts[0][0:NB, 0:512], cat_s_scr[:, q4 * 512:(q4 + 1) * 512], ["cat_s_scr"], ["oatt0"], "l_sg0")
                    S.op("dve", lambda e, q4=q4: e.tensor_copy(out=cat_s_bf[0:NB, q4 * 512:(q4 + 1) * 512], in_=oatts[0][0:NB, 0:512]), reads=["oatt0"], writes=["cat_s_bf"])

                def trs(e):
                    for j in range(16):
                        ins = e.transpose(out=ps_tr[:, j * 16:(j + 1) * 16], in_=cat_s_bf[0:NB, j * 128:(j + 1) * 128], identity=identb[0:NB, 0:NB])
                    return ins
                S.op("pe", trs, reads=["cat_s_bf", "identb"], writes=["ptr0"])
                S.op("act", lambda e: e.copy(out=catTs[0][:, :, 0:NB], in_=ps_tr[:, 0:256].rearrange("p (c t) -> p c t", c=16)), reads=["ptr0"], writes=["catT0"])
                out_proj(0, NB, wmap=tuple(range(16)))
                dma(xt2[0][0:NB, :], xs, [], ["xt2_0"], "x0")
                S.op("dve", lambda e: e.tensor_tensor(out=hps[0][0:NB, :], in0=pso[0][0:NB, :], in1=xt2[0][0:NB, :], op=ALU.add), reads=["po0", "xt2_0"], writes=["hp0"])
                final_norm_out(0, NB, ys, "o_ys")
        S.barrier_all("sp")
        S.emit()
    return nc


_CACHE = {}


def _consts():
    k = np.arange(128)
    tri = (k[:, None] <= k[None, :]).astype(np.float32)
    return dict(c_ident=np.eye(128, dtype=np.float32), c_tri=tri, c_triT=np.ascontiguousarray(tri.T),
                c_posf=(128 * np.arange(32)[None, :] + k[:, None]).astype(np.float32),
                c_inv=(500000.0 ** (-np.arange(8, dtype=np.float32) / 8)).astype(np.float32)[None, :],
                c_esel=np.eye(16, dtype=np.float32).reshape(1, 256))


def kernel(x_prompt, x_sample, mem_prompt, cache_win_k, cache_win_v, cache_mem_k, cache_mem_v,
           state_conv, state_ssm, pos_sample, ln_g, w_in, conv_w, conv_b, dt_bias, a_log, d_skip,
           ssd_norm_g, mem_norm_g, w_mem_kv, w_out, final_norm_g):
    f = lambda a: np.ascontiguousarray(np.asarray(a, dtype=np.float32))
    if "nc" not in _CACHE:
        _CACHE["nc"] = build_program()
    nc = _CACHE["nc"]
    cs = _consts()
    in_maps = []
    for i in range(8):
        sl = slice(NB * i, NB * (i + 1))
        d = dict(x=f(x_prompt[i]), xs=f(x_sample[sl, 0]), mem=f(mem_prompt[i]),
                 cwk=f(cache_win_k[0, sl]).reshape(NB, 2048, 512), cwv=f(cache_win_v[0, sl]).reshape(NB, 2048, 512),
                 cmk=f(cache_mem_k[0, sl]).reshape(NB, 256, 512), cmv=f(cache_mem_v[0, sl]).reshape(NB, 256, 512),
                 sconv=f(state_conv[0, sl]), sssm=f(state_ssm[0, sl]).reshape(-1, 128),
                 pos=np.ascontiguousarray(np.asarray(pos_sample[sl], dtype=np.int32)),
                 ln_g=f(ln_g), w_in=f(w_in[0]), conv_w=f(conv_w[0]), conv_b=f(conv_b), dt_bias=f(dt_bias),
                 a_log=f(a_log), d_skip=f(d_skip), ssd_norm_g=f(ssd_norm_g), mem_norm_g=f(mem_norm_g),
                 w_mem_kv=f(w_mem_kv[0]), w_out=f(w_out[0]), final_norm_g=f(final_norm_g)[None, :])
        d.update(cs)
        in_maps.append(d)
    res = run_bass_kernel_spmd(nc, in_maps, core_ids=list(range(8)))
    R = res.results
    cat = lambda name: np.stack([np.asarray(r[name], dtype=np.float32) for r in R], axis=0)
    y_prompt = cat("yp")
    y_sample = cat("ys").reshape(128, 1, 1024)
    win_k_prompt = cat("wk").reshape(1, 8, 2048, 8, 64)
    win_v_prompt = cat("wv").reshape(1, 8, 2048, 8, 64)
    mem_k_prompt = cat("mko").reshape(1, 8, 256, 4, 128)
    mem_v_prompt = cat("mvo").reshape(1, 8, 256, 4, 128)
    conv_prompt = cat("convp").reshape(1, 8, 3, 2048)
    ssm_prompt = cat("ssmp").reshape(1, 8, 16, 64, 128)
    win_k_sample = cat("wks").reshape(1, 128, 1, 8, 64)
    win_v_sample = cat("wvs").reshape(1, 128, 1, 8, 64)
    conv_sample = cat("convs").reshape(1, 128, 3, 2048)
    ssm_sample = cat("ssms").reshape(1, 128, 16, 64, 128)
    return (y_prompt, y_sample, win_k_prompt, win_v_prompt, mem_k_prompt, mem_v_prompt, conv_prompt, ssm_prompt,
            win_k_sample, win_v_sample, conv_sample, ssm_sample)
```
